# Optimizing a Trainium2 kernel written in Bass

```python
import math
import jax, jax.numpy as jnp
from jax import lax
import numpy as np

D_MODEL = 1024
BATCH = 32
SEQ = 2048
DEPTH = 1

CHUNK = 64
EPS = 1e-6

D_MIX = D_MODEL
D_LRU = D_MIX // 2
D_SSM = D_MIX - D_LRU

LRU_HEADS = 8
LRU_HEAD_DIM = D_LRU // LRU_HEADS
CONV_WIDTH = 4
LRU_C = 8.0

SSM_GROUP = 16
SSM_GROUPS = D_SSM // SSM_GROUP
SSM_STATE = 64
DT_MIN = 0.001
DT_MAX = 0.1

PEER_HEADS = 8
PEER_NKEYS = 128
PEER_EXPERTS = PEER_NKEYS * PEER_NKEYS
PEER_TOPK = 16
PEER_QDIM = 256
PEER_HALF = PEER_QDIM // 2
TOK_BLOCK = 128

kernel_name = 'hymba_rglru_s5_peer_block'


def rms_norm(x, g):
    xf = x.astype(jnp.float32)
    y = xf * lax.rsqrt(jnp.mean(xf * xf, axis=-1, keepdims=True) + EPS) * g.astype(jnp.float32)
    return y.astype(x.dtype)


def rg_lru_group(xa, gate, conv_w, conv_b, wa, ba, wx, bx, lam):
    b, s, _ = xa.shape
    xc = lax.conv_general_dilated(
        xa, conv_w[:, None, :].astype(xa.dtype), window_strides=(1,),
        padding=[(CONV_WIDTH - 1, 0)], dimension_numbers=('NWC', 'WIO', 'NWC'),
        feature_group_count=D_LRU) + conv_b
    xf = xc.astype(jnp.float32)
    xh = xf.reshape(b, s, LRU_HEADS, LRU_HEAD_DIM)
    r = jax.nn.sigmoid(jnp.einsum('bshi,hij->bshj', xh, wa.astype(jnp.float32)).reshape(b, s, D_LRU)
                       + ba.astype(jnp.float32))
    i_g = jax.nn.sigmoid(jnp.einsum('bshi,hij->bshj', xh, wx.astype(jnp.float32)).reshape(b, s, D_LRU)
                         + bx.astype(jnp.float32))
    log_a = -LRU_C * r * jax.nn.softplus(-lam.astype(jnp.float32))
    a = jnp.exp(log_a)
    u = jnp.sqrt(-jnp.expm1(2.0 * log_a)) * (i_g * xf)

    def step(h, au):
        a_t, u_t = au
        h = a_t * h + u_t
        return h, h

    _, hs = lax.scan(step, jnp.zeros((b, D_LRU), jnp.float32),
                     (jnp.swapaxes(a, 0, 1), jnp.swapaxes(u, 0, 1)))
    h = jnp.swapaxes(hs, 0, 1)
    return (h * jax.nn.gelu(gate.astype(jnp.float32))).astype(xa.dtype)


def _complex_combine(e1, e2):
    a1r, a1i, b1r, b1i = e1
    a2r, a2i, b2r, b2i = e2
    return (a2r * a1r - a2i * a1i,
            a2r * a1i + a2i * a1r,
            a2r * b1r - a2i * b1i + b2r,
            a2r * b1i + a2i * b1r + b2i)


def s5_group(u, lam_re, lam_im, log_dt, b_re, b_im, c_re, c_im, d_skip, glu_w, glu_b):
    b, s, _ = u.shape
    uf = u.astype(jnp.float32)
    dt = jnp.exp(log_dt.astype(jnp.float32))[:, None]
    lr = lam_re.astype(jnp.float32)
    li = lam_im.astype(jnp.float32)
    mag = jnp.exp(lr * dt)
    ab_re = mag * jnp.cos(li * dt)
    ab_im = mag * jnp.sin(li * dt)
    den = lr * lr + li * li
    n_re = ab_re - 1.0
    n_im = ab_im
    zc_re = (n_re * lr + n_im * li) / den
    zc_im = (n_im * lr - n_re * li) / den
    br = b_re.astype(jnp.float32)
    bi = b_im.astype(jnp.float32)
    bb_re = zc_re[..., None] * br - zc_im[..., None] * bi
    bb_im = zc_re[..., None] * bi + zc_im[..., None] * br
    cr = c_re.astype(jnp.float32)
    ci = c_im.astype(jnp.float32)

    n_chunks = s // CHUNK
    uc = uf.reshape(b, n_chunks, CHUNK, SSM_GROUPS, SSM_GROUP).transpose(1, 0, 2, 3, 4)
    a_re = jnp.broadcast_to(ab_re, (b, CHUNK, SSM_GROUPS, SSM_STATE))
    a_im = jnp.broadcast_to(ab_im, (b, CHUNK, SSM_GROUPS, SSM_STATE))

    def chunk_step(carry, u_c):
        h_re, h_im = carry
        bu_re = jnp.einsum('bcgi,gpi->bcgp', u_c, bb_re)
        bu_im = jnp.einsum('bcgi,gpi->bcgp', u_c, bb_im)
        acum_re, acum_im, s_re, s_im = lax.associative_scan(
            _complex_combine, (a_re, a_im, bu_re, bu_im), axis=1)
        st_re = s_re + acum_re * h_re[:, None] - acum_im * h_im[:, None]
        st_im = s_im + acum_re * h_im[:, None] + acum_im * h_re[:, None]
        y = jnp.einsum('bcgp,gip->bcgi', st_re, cr) - jnp.einsum('bcgp,gip->bcgi', st_im, ci)
        return (st_re[:, -1], st_im[:, -1]), y

    h0 = jnp.zeros((b, SSM_GROUPS, SSM_STATE), jnp.float32)
    _, ys = lax.scan(chunk_step, (h0, h0), uc)
    y = ys.transpose(1, 0, 2, 3, 4).reshape(b, s, D_SSM) + d_skip.astype(jnp.float32) * uf
    g = jax.nn.gelu(y)
    out = g * jax.nn.sigmoid(g @ glu_w.astype(jnp.float32) + glu_b.astype(jnp.float32))
    return out.astype(u.dtype)


def peer_ffn(x, wq, k1, k2, u_tab, v_tab):
    b, s, d = x.shape
    xt = x.reshape(-1, TOK_BLOCK, d)

    def block(xb):
        q = (xb @ wq).astype(jnp.float32).reshape(TOK_BLOCK, PEER_HEADS, PEER_QDIM)
        s1 = jnp.einsum('thd,kd->thk', q[..., :PEER_HALF], k1.astype(jnp.float32))
        s2 = jnp.einsum('thd,kd->thk', q[..., PEER_HALF:], k2.astype(jnp.float32))
        v1, i1 = lax.top_k(s1, PEER_TOPK)
        v2, i2 = lax.top_k(s2, PEER_TOPK)
        cand = (v1[..., :, None] + v2[..., None, :]).reshape(TOK_BLOCK, PEER_HEADS, PEER_TOPK * PEER_TOPK)
        sc, cidx = lax.top_k(cand, PEER_TOPK)
        e = (jnp.take_along_axis(i1, cidx // PEER_TOPK, axis=-1) * PEER_NKEYS
             + jnp.take_along_axis(i2, cidx % PEER_TOPK, axis=-1))
        g = jax.nn.softmax(sc, axis=-1)
        u_sel = u_tab[e]
        act = jax.nn.gelu(jnp.einsum('thkd,td->thk', u_sel, xb).astype(jnp.float32))
        v_sel = v_tab[e]
        return jnp.einsum('thk,thkd->td', (g * act).astype(xb.dtype), v_sel)

    out = lax.map(block, xt)
    return out.reshape(b, s, d)


def setup_inputs(seed: int = 0) -> dict:
    key = jax.random.key(seed)
    ks = jax.random.split(key, 32)
    L = DEPTH

    def nrm(k, shape, scale):
        return jax.random.normal(k, shape, jnp.float32) * scale

    x = jax.random.normal(ks[0], (BATCH, SEQ, D_MODEL), jnp.float32)
    norm1_g = 1.0 + nrm(ks[1], (L, D_MODEL), 0.02)
    w_in = nrm(ks[2], (L, D_MODEL, D_LRU + D_LRU + D_SSM), D_MODEL ** -0.5)
    conv_w = nrm(ks[3], (L, CONV_WIDTH, D_LRU), CONV_WIDTH ** -0.5)
    conv_b = nrm(ks[4], (L, D_LRU), 0.01)
    lru_wa = nrm(ks[5], (L, LRU_HEADS, LRU_HEAD_DIM, LRU_HEAD_DIM), LRU_HEAD_DIM ** -0.5)
    lru_ba = nrm(ks[6], (L, D_LRU), 0.01)
    lru_wx = nrm(ks[7], (L, LRU_HEADS, LRU_HEAD_DIM, LRU_HEAD_DIM), LRU_HEAD_DIM ** -0.5)
    lru_bx = nrm(ks[8], (L, D_LRU), 0.01)
    a_c = jax.random.uniform(ks[9], (L, D_LRU), jnp.float32, 0.9, 0.999)
    sig_l = a_c ** (1.0 / LRU_C)
    lru_lambda = jnp.log(sig_l) - jnp.log1p(-sig_l)
    n_idx = jnp.arange(SSM_STATE, dtype=jnp.float32)
    s5_lambda_re = -0.5 + nrm(ks[10], (L, SSM_GROUPS, SSM_STATE), 0.01)
    s5_lambda_im = math.pi * n_idx + nrm(ks[11], (L, SSM_GROUPS, SSM_STATE), 0.01)
    s5_log_dt = jax.random.uniform(ks[12], (L, SSM_GROUPS), jnp.float32,
                                   math.log(DT_MIN), math.log(DT_MAX))
    s5_b_re = nrm(ks[13], (L, SSM_GROUPS, SSM_STATE, SSM_GROUP), (2.0 * SSM_GROUP) ** -0.5)
    s5_b_im = nrm(ks[14], (L, SSM_GROUPS, SSM_STATE, SSM_GROUP), (2.0 * SSM_GROUP) ** -0.5)
    s5_c_re = nrm(ks[15], (L, SSM_GROUPS, SSM_GROUP, SSM_STATE), (2.0 * SSM_STATE) ** -0.5)
    s5_c_im = nrm(ks[16], (L, SSM_GROUPS, SSM_GROUP, SSM_STATE), (2.0 * SSM_STATE) ** -0.5)
    s5_d = nrm(ks[17], (L, D_SSM), 1.0)
    s5_glu_w = nrm(ks[18], (L, D_SSM, D_SSM), D_SSM ** -0.5)
    s5_glu_b = nrm(ks[19], (L, D_SSM), 0.01)
    gnorm_lru = 1.0 + nrm(ks[20], (L, D_LRU), 0.02)
    gnorm_s5 = 1.0 + nrm(ks[21], (L, D_SSM), 0.02)
    w_out = nrm(ks[22], (L, D_MIX, D_MODEL), D_MIX ** -0.5)
    norm2_g = 1.0 + nrm(ks[23], (L, D_MODEL), 0.02)
    peer_wq = nrm(ks[24], (L, D_MODEL, PEER_HEADS * PEER_QDIM), D_MODEL ** -0.5)
    peer_k1 = nrm(ks[25], (L, PEER_NKEYS, PEER_HALF), PEER_HALF ** -0.5)
    peer_k2 = nrm(ks[26], (L, PEER_NKEYS, PEER_HALF), PEER_HALF ** -0.5)
    peer_u = nrm(ks[27], (L, PEER_EXPERTS, D_MODEL), D_MODEL ** -0.5)
    peer_v = nrm(ks[28], (L, PEER_EXPERTS, D_MODEL), (PEER_HEADS * PEER_TOPK) ** -0.5)
    normf_g = 1.0 + nrm(ks[29], (D_MODEL,), 0.02)
    return {'x': x, 'norm1_g': norm1_g, 'w_in': w_in, 'conv_w': conv_w, 'conv_b': conv_b,
            'lru_wa': lru_wa, 'lru_ba': lru_ba, 'lru_wx': lru_wx, 'lru_bx': lru_bx,
            'lru_lambda': lru_lambda, 's5_lambda_re': s5_lambda_re, 's5_lambda_im': s5_lambda_im,
            's5_log_dt': s5_log_dt, 's5_b_re': s5_b_re, 's5_b_im': s5_b_im,
            's5_c_re': s5_c_re, 's5_c_im': s5_c_im, 's5_d': s5_d, 's5_glu_w': s5_glu_w,
            's5_glu_b': s5_glu_b, 'gnorm_lru': gnorm_lru, 'gnorm_s5': gnorm_s5, 'w_out': w_out,
            'norm2_g': norm2_g, 'peer_wq': peer_wq, 'peer_k1': peer_k1, 'peer_k2': peer_k2,
            'peer_u': peer_u, 'peer_v': peer_v, 'normf_g': normf_g}


def reference(x, norm1_g, w_in, conv_w, conv_b, lru_wa, lru_ba, lru_wx, lru_bx, lru_lambda,
              s5_lambda_re, s5_lambda_im, s5_log_dt, s5_b_re, s5_b_im, s5_c_re, s5_c_im,
              s5_d, s5_glu_w, s5_glu_b, gnorm_lru, gnorm_s5, w_out, norm2_g,
              peer_wq, peer_k1, peer_k2, peer_u, peer_v, normf_g):
    h = x
    for l in range(DEPTH):
        xn = rms_norm(h, norm1_g[l])
        p = xn @ w_in[l]
        xa = p[..., :D_LRU]
        gate = p[..., D_LRU:2 * D_LRU]
        us = p[..., 2 * D_LRU:]
        ya = rg_lru_group(xa, gate, conv_w[l], conv_b[l], lru_wa[l], lru_ba[l],
                          lru_wx[l], lru_bx[l], lru_lambda[l])
        yb = s5_group(us, s5_lambda_re[l], s5_lambda_im[l], s5_log_dt[l], s5_b_re[l], s5_b_im[l],
                      s5_c_re[l], s5_c_im[l], s5_d[l], s5_glu_w[l], s5_glu_b[l])
        y = jnp.concatenate([rms_norm(ya, gnorm_lru[l]), rms_norm(yb, gnorm_s5[l])], axis=-1)
        h = h + y @ w_out[l]
        h = h + peer_ffn(rms_norm(h, norm2_g[l]), peer_wq[l], peer_k1[l], peer_k2[l],
                         peer_u[l], peer_v[l])
    return rms_norm(h, normf_g)
```

```python
import math
from contextlib import ExitStack

import numpy as np
import concourse.bass as bass
import concourse.mybir as mybir
from concourse.bass_utils import run_bass_kernel_spmd

F32 = mybir.dt.float32
BF16 = mybir.dt.bfloat16
I32 = mybir.dt.int32
U32 = mybir.dt.uint32
ALU = mybir.AluOpType
AF = mybir.ActivationFunctionType
AX = mybir.AxisListType

ENG = ('tensor', 'vector', 'scalar', 'gpsimd', 'sync')
CAP = 30000
NCORES = 8
TPS = 16
EPS = 1e-6
TWO_PI = 6.283185
NV = 112
RHO_BCAST = True


class Op:
    __slots__ = ('eng', 'fn', 'deps', 'sig', 'is_dma', 'chan', 'chan_val', 'signo')


class Sched:
    def __init__(self, nc):
        self.nc = nc
        self.ops = {e: [] for e in ENG}
        self.lastw = {}
        self.readers = {}
        self.chan_cnt = {}
        self.last_op = {}
        self.last_dma = {}

    def add(self, eng, fn, reads=(), writes=(), chan=None, extra=()):
        op = Op()
        op.eng = eng
        op.fn = fn
        op.is_dma = chan is not None
        op.chan = chan
        op.sig = False
        op.signo = 0
        op.chan_val = 0
        deps = {}
        for r in reads:
            w = self.lastw.get(r)
            if w is not None:
                deps[id(w)] = w
        for k in writes:
            w = self.lastw.get(k)
            if w is not None:
                deps[id(w)] = w
            for d in self.readers.get(k, ()):
                deps[id(d)] = d
        for d in extra:
            deps[id(d)] = d
        op.deps = []
        for d in deps.values():
            if (not d.is_dma) and d.eng == 'tensor' and eng == 'tensor':
                continue
            op.deps.append(d)
            if not d.is_dma:
                d.sig = True
        if op.is_dma:
            c = self.chan_cnt.get(chan, 0) + 1
            self.chan_cnt[chan] = c
            op.chan_val = 16 * c
            self.last_dma[chan] = op
        else:
            self.last_op[eng] = op
        for r in reads:
            self.readers.setdefault(r, []).append(op)
        for k in writes:
            self.lastw[k] = op
            self.readers[k] = []
        self.ops[eng].append(op)
        return op

    def barrier(self):
        prev = list(self.last_op.values()) + list(self.last_dma.values())
        for e in ENG:
            self.add(e, lambda eo: eo.nop(), extra=prev)
        self.lastw = {}
        self.readers = {}

    def emit(self, es):
        nc = self.nc
        esem = {}
        for e in ENG:
            n = 0
            for op in self.ops[e]:
                if op.sig and not op.is_dma:
                    n += 1
                    op.signo = n
            nep = max(1, (n + CAP - 1) // CAP)
            esem[e] = [es.enter_context(nc.semaphore(f"s_{e}_{i}")) for i in range(nep)]
        csem = {c: es.enter_context(nc.semaphore(f"c_{c}")) for c in self.chan_cnt}
        block = es.enter_context(nc.Block())
        for e in ENG:
            ops = self.ops[e]

            def body(eo, e=e, ops=ops):
                waited = {}
                maxep = {}
                for op in ops:
                    need = {}
                    for d in op.deps:
                        if d.is_dma:
                            key = ('c', d.chan)
                            val = d.chan_val
                        else:
                            ep = (d.signo - 1) // CAP
                            if maxep.get(d.eng, -1) > ep:
                                continue
                            key = ('e', d.eng, ep)
                            val = (d.signo - 1) % CAP + 1
                        if need.get(key, 0) < val:
                            need[key] = val
                    for key, val in need.items():
                        if waited.get(key, 0) >= val:
                            continue
                        if key[0] == 'c':
                            eo.wait_ge(csem[key[1]], val)
                        else:
                            eo.wait_ge(esem[key[1]][key[2]], val)
                            if maxep.get(key[1], -1) < key[2]:
                                maxep[key[1]] = key[2]
                        waited[key] = val
                    ins = op.fn(eo)
                    if op.is_dma:
                        ins.then_inc(csem[op.chan], 16)
                    elif op.sig:
                        ins.then_inc(esem[e][(op.signo - 1) // CAP], 1)

            getattr(block, e)(body)


class Arena:
    def __init__(self, nc, es, nbytes):
        self.t = es.enter_context(nc.sbuf_tensor("arena", [128, nbytes // 4], F32))
        self.off = 0
        self.cap = nbytes

    def alloc(self, shape, dtype=F32):
        esz = {F32: 4, I32: 4, U32: 4, BF16: 2}[dtype]
        n = int(np.prod(shape))
        nb = (n * esz + 31) // 32 * 32
        o = self.off
        self.off += nb
        assert self.off <= self.cap, (self.off, self.cap)
        ap = self.t[:, o // 4:(o + nb) // 4]
        if dtype != F32:
            ap = ap.bitcast(dtype)
        ap = ap[:, 0:n]
        if len(shape) == 2:
            ap = ap.rearrange("p (a b) -> p a b", a=shape[0])
        elif len(shape) == 3:
            ap = ap.rearrange("p (a b c) -> p a b c", a=shape[0], b=shape[1])
        return ap


def bcast(ap2d_src, off, dims):
    base = ap2d_src
    pstep = base.ap[0][0]
    return bass.AP(base.tensor, base.offset + off, [[pstep, 128]] + [list(d) for d in dims])


def build(NT, stage=2, PIPE=True):
    nc = bass.Bass("TRN2", target_bir_lowering=False)
    es = ExitStack()

    def D(n, s, dt=F32, kind="ExternalInput"):
        return nc.dram_tensor(n, s, dt, kind=kind).ap()

    x_d = D("x", [NT * 128, 1024])
    out_d = D("out", [NT * 128, 1024], kind="ExternalOutput")
    w_in_d = D("w_in", [1024, 1536])
    w_out_d = D("w_out", [1024, 1024])
    glu_d = D("glu_w", [512, 512])
    wq_d = D("wq", [1024, 2048])
    pu_d = D("peer_u", [16384, 1024])
    pv_d = D("peer_v", [16384, 1024])
    kT_d = D("kT", [128, 2, 128])
    vecs_d = D("vecs", [128, NV])
    wax_d = D("wax", [128, 8, 128])
    gbc_d = D("gbc", [128, 3, 1024])
    s5l_d = D("s5l", [128, 5, 2048])
    s5c_d = D("s5c", [128, 2, 2048])
    uvb_d = nc.dram_tensor("uvb_scr", [16384, 2048], BF16).ap()

    ar = Arena(nc, es, 207 * 1024)
    psum = es.enter_context(nc.psum_tensor("ps", [128, 8, 512], F32))
    sc = Sched(nc)

    cur = [None]

    def _emit(eng, fn, r, w, chan=None):
        if cur[0] is not None:
            cur[0].append(lambda: sc.add(eng, fn, r, w, chan=chan))
            return None
        return sc.add(eng, fn, r, w, chan=chan)

    def T(fn, r=(), w=()):
        return _emit('tensor', fn, r, w)

    def V(fn, r=(), w=()):
        return _emit('vector', fn, r, w)

    def A(fn, r=(), w=()):
        return _emit('scalar', fn, r, w)

    def G(fn, r=(), w=()):
        return _emit('gpsimd', fn, r, w)

    def GD(fn, r, w, chan):
        return _emit('gpsimd', fn, r, w, chan)

    clc = [0]

    def dma(eng, out, in_, r, w, chan):
        if chan == 'cl':
            chan = f'cl{clc[0]}'
            clc[0] += 1
        return _emit(eng, lambda e: e.dma_start(out=out, in_=in_), r, w, chan)

    def pk(b, q=None):
        return [f"ps{b}_{i}" for i in range(4)] if q is None else [f"ps{b}_{q}"]

    w_in = ar.alloc([8, 1536], BF16)
    w_out = ar.alloc([8, 1024], BF16)
    glu = ar.alloc([4, 512], BF16)
    bbR = ar.alloc([16, 128], BF16)
    bbI = ar.alloc([16, 128], BF16)
    cR = ar.alloc([16, 128], BF16)
    cIn = ar.alloc([16, 128], BF16)
    cosT = ar.alloc([16, 128])
    sinT = ar.alloc([16, 128])
    rhoT = None if RHO_BCAST else ar.alloc([16, 128])
    wax = ar.alloc([8, 128])
    kT = ar.alloc([2, 128])
    gbc = ar.alloc([3, 1024])
    vecs = ar.alloc([NV])
    cst = ar.alloc([128])
    ident = ar.alloc([128], BF16)
    identf = ar.alloc([128])
    ones = ar.alloc([8])
    hl = ar.alloc([4])
    ire = ar.alloc([16])
    iim = ar.alloc([16])
    xah = ar.alloc([4, 131])
    base = ar.off

    CW, CB, BA, BX, LAM, SD, GB, GNL, GNS, LRS, LIS, LDS, IOTA = 0, 16, 20, 24, 28, 32, 36, 40, 44, 48, 64, 80, 96
    SC8, SC16, MAGS, COSS, SINS, C128, S128 = 0, 4, 8, 24, 40, 56, 72

    wl_prev = [None]

    def wload(dst, src_):
        i = len(sc.ops['gpsimd'])
        op = sc.add('gpsimd', lambda e: e.dma_start(out=dst, in_=src_), [], [], chan=f'wl{i % 2}',
                    extra=[wl_prev[0]] if wl_prev[0] is not None else [])
        wl_prev[0] = op

    for kc in range(8):
        wload(w_in[:, kc, :], w_in_d[kc * 128:(kc + 1) * 128, :])
        wload(w_out[:, kc, :], w_out_d[kc * 128:(kc + 1) * 128, :])
    for kc in range(4):
        wload(glu[:, kc, :], glu_d[kc * 128:(kc + 1) * 128, :])
    for i in range(16):
        rs_ = slice(i * 1024, (i + 1) * 1024)
        sc.add('gpsimd', lambda e, rs_=rs_: e.dma_start(out=uvb_d[rs_, 0:1024], in_=pu_d[rs_, :]), [], [], chan='sbu')
        sc.add('gpsimd', lambda e, rs_=rs_: e.dma_start(out=uvb_d[rs_, 1024:2048], in_=pv_d[rs_, :]), [], [], chan='sbv')
    dma('sync', wax, wax_d, [], ['wax'], 'cl')
    dma('sync', kT, kT_d, [], ['kT'], 'cl')
    dma('sync', gbc, gbc_d, [], ['gbc'], 'cl')
    dma('sync', vecs, vecs_d, [], ['vecs'], 'cl')
    G(lambda e: e.memset(identf, 0.0), [], ['identf'])
    G(lambda e: e.affine_select(out=identf, in_=identf, pattern=[[-1, 128]], compare_op=ALU.not_equal,
                                fill=1.0, base=0, channel_multiplier=1), ['identf'], ['identf'])
    V(lambda e: e.tensor_copy(out=ident, in_=identf), ['identf'], ['ident'])
    G(lambda e: e.memset(ones, 1.0), [], ['ones'])
    G(lambda e: e.memset(cst, 0.0), [], ['cst'])

    A(lambda e: e.activation(out=cst[:, 100:104], in_=vecs[:, LAM:LAM + 4], func=AF.Exp, scale=-1.0), ['vecs', 'cst'], ['cst'])
    A(lambda e: e.activation(out=cst[:, 104:108], in_=cst[:, 100:104], func=AF.Ln, bias=1.0), ['cst'], ['cst'])
    V(lambda e: e.tensor_scalar(out=cst[:, SC8:SC8 + 4], in0=cst[:, 104:108], scalar1=-8.0, scalar2=None, op0=ALU.mult), ['cst'], ['cst'])
    V(lambda e: e.tensor_scalar(out=cst[:, SC16:SC16 + 4], in0=cst[:, 104:108], scalar1=-16.0, scalar2=None, op0=ALU.mult), ['cst'], ['cst'])

    def abar(lamR, lamI, logdt, mag, cosv, sinv, tmp, tmpi, kp):
        t0, t1, t2 = tmp
        k = [kp + str(i) for i in range(8)]
        A(lambda e: e.activation(out=t0, in_=logdt, func=AF.Exp), [k[0]], [k[1]])
        V(lambda e: e.tensor_tensor(out=t1, in0=lamI, in1=t0, op=ALU.mult), [k[0], k[1]], [k[2]])
        V(lambda e: e.tensor_tensor(out=t2, in0=lamR, in1=t0, op=ALU.mult), [k[0], k[1]], [k[3]])
        A(lambda e: e.activation(out=mag, in_=t2, func=AF.Exp), [k[3]], [k[4]])
        V(lambda e: e.tensor_scalar(out=t1, in0=t1, scalar1=1.0 / (2.0 * math.pi), scalar2=None, op0=ALU.mult), [k[2]], [k[2]])
        V(lambda e: e.tensor_copy(out=tmpi, in_=t1), [k[2]], [k[5]])
        V(lambda e: e.tensor_copy(out=t0, in_=tmpi), [k[5], k[1]], [k[1]])
        V(lambda e: e.tensor_tensor(out=t1, in0=t1, in1=t0, op=ALU.subtract), [k[2], k[1]], [k[2]])
        A(lambda e: e.activation(out=sinv, in_=t1, func=AF.Sin, scale=TWO_PI), [k[2]], [k[6]])
        V(lambda e: e.tensor_scalar(out=t0, in0=t1, scalar1=0.25, scalar2=None, op0=ALU.is_gt), [k[2], k[1]], [k[1]])
        V(lambda e: e.scalar_tensor_tensor(out=t2, in0=t1, scalar=0.25, in1=t0, op0=ALU.add, op1=ALU.subtract), [k[2], k[1], k[3], k[4]], [k[3]])
        A(lambda e: e.activation(out=cosv, in_=t2, func=AF.Sin, scale=TWO_PI), [k[3]], [k[7]])
        return k

    LT = [ar.alloc([1024]) for _ in range(15)]
    lti = ar.alloc([1024], I32)
    bbR2 = bbR.rearrange("p a b -> p (a b)")
    bbI2 = bbI.rearrange("p a b -> p (a b)")
    cR2 = cR.rearrange("p a b -> p (a b)")
    cI2 = cIn.rearrange("p a b -> p (a b)")
    for hf in range(2):
        cs = slice(hf * 1024, (hf + 1) * 1024)
        KL = ['L' + str(i) for i in range(8)]
        allL = KL + ['LT']
        for i in range(5):
            dma('sync', LT[i], s5l_d[:, i, cs], [], ['LT'], 'cl')
        dma('sync', LT[5], s5c_d[:, 0, cs], [], ['LT'], 'cl')
        dma('sync', LT[6], s5c_d[:, 1, cs], [], ['LT'], 'cl')
        bRp, bIp, lRl, lIl, ldl, cRp, cIp = LT[0:7]
        magL, cosL, sinL, tA, tB = LT[7:12]
        V(lambda e: e.tensor_copy(out=LT[12][:, 0:1], in_=LT[2][:, 0:1]), allL, allL)
        abar(lRl, lIl, ldl, magL, cosL, sinL, (LT[12], LT[13], LT[14]), lti, 'L')
        V(lambda e, cosL=cosL, magL=magL: e.tensor_tensor(out=cosL, in0=cosL, in1=magL, op=ALU.mult), allL, allL)
        V(lambda e, sinL=sinL, magL=magL: e.tensor_tensor(out=sinL, in0=sinL, in1=magL, op=ALU.mult), allL, allL)
        V(lambda e, cosL=cosL: e.tensor_scalar(out=cosL, in0=cosL, scalar1=-1.0, scalar2=None, op0=ALU.add), allL, allL)
        V(lambda e, magL=magL, lRl=lRl: e.tensor_tensor(out=magL, in0=lRl, in1=lRl, op=ALU.mult), allL, allL)
        V(lambda e, tA=tA, lIl=lIl: e.tensor_tensor(out=tA, in0=lIl, in1=lIl, op=ALU.mult), allL, allL)
        V(lambda e, magL=magL, tA=tA: e.tensor_tensor(out=magL, in0=magL, in1=tA, op=ALU.add), allL, allL)
        V(lambda e, magL=magL: e.reciprocal(out=magL, in_=magL), allL, allL)
        V(lambda e, tA=tA, cosL=cosL, lRl=lRl: e.tensor_tensor(out=tA, in0=cosL, in1=lRl, op=ALU.mult), allL, allL)
        V(lambda e, tB=tB, sinL=sinL, lIl=lIl: e.tensor_tensor(out=tB, in0=sinL, in1=lIl, op=ALU.mult), allL, allL)
        V(lambda e, tA=tA, tB=tB: e.tensor_tensor(out=tA, in0=tA, in1=tB, op=ALU.add), allL, allL)
        V(lambda e, tA=tA, magL=magL: e.tensor_tensor(out=tA, in0=tA, in1=magL, op=ALU.mult), allL, allL)
        V(lambda e, tB=tB, sinL=sinL, lRl=lRl: e.tensor_tensor(out=tB, in0=sinL, in1=lRl, op=ALU.mult), allL, allL)
        V(lambda e, cosL=cosL, lIl=lIl: e.tensor_tensor(out=cosL, in0=cosL, in1=lIl, op=ALU.mult), allL, allL)
        V(lambda e, tB=tB, cosL=cosL: e.tensor_tensor(out=tB, in0=tB, in1=cosL, op=ALU.subtract), allL, allL)
        V(lambda e, tB=tB, magL=magL: e.tensor_tensor(out=tB, in0=tB, in1=magL, op=ALU.mult), allL, allL)
        V(lambda e, cosL=cosL, tA=tA, bRp=bRp: e.tensor_tensor(out=cosL, in0=tA, in1=bRp, op=ALU.mult), allL, allL)
        V(lambda e, sinL=sinL, tB=tB, bIp=bIp: e.tensor_tensor(out=sinL, in0=tB, in1=bIp, op=ALU.mult), allL, allL)
        V(lambda e, cs=cs, cosL=cosL, sinL=sinL: e.tensor_tensor(out=bbR2[:, cs], in0=cosL, in1=sinL, op=ALU.subtract), allL, allL)
        V(lambda e, cosL=cosL, tA=tA, bIp=bIp: e.tensor_tensor(out=cosL, in0=tA, in1=bIp, op=ALU.mult), allL, allL)
        V(lambda e, sinL=sinL, tB=tB, bRp=bRp: e.tensor_tensor(out=sinL, in0=tB, in1=bRp, op=ALU.mult), allL, allL)
        V(lambda e, cs=cs, cosL=cosL, sinL=sinL: e.tensor_tensor(out=bbI2[:, cs], in0=cosL, in1=sinL, op=ALU.add), allL, allL)
        V(lambda e, cs=cs, cRp=cRp: e.tensor_copy(out=cR2[:, cs], in_=cRp), allL, allL)
        V(lambda e, cs=cs, cIp=cIp: e.tensor_scalar(out=cI2[:, cs], in0=cIp, scalar1=-1.0, scalar2=None, op0=ALU.mult), allL, allL)

    st = [ar.alloc([16]) for _ in range(3)]
    sti = ar.alloc([16], I32)
    V(lambda e: e.tensor_copy(out=st[0][:, 0:1], in_=vecs[:, 0:1]), ['vecs'], ['S0'])
    abar(vecs[:, LRS:LRS + 16], vecs[:, LIS:LIS + 16], vecs[:, LDS:LDS + 16],
         cst[:, MAGS:MAGS + 16], cst[:, COSS:COSS + 16], cst[:, SINS:SINS + 16], st, sti, 'S')
    KS = ['S' + str(i) for i in range(8)] + ['cst', 'tab']
    if not RHO_BCAST:
        V(lambda e: e.tensor_copy(out=rhoT, in_=bcast(cst, MAGS, [[1, 16], [0, 128]])), KS, KS)
    V(lambda e: e.memset(cosT[:, :, 0:1], 1.0), KS, KS)
    V(lambda e: e.memset(sinT[:, :, 0:1], 0.0), KS, KS)
    V(lambda e: e.tensor_copy(out=cosT[:, :, 1:2], in_=cst[:, COSS:COSS + 16].unsqueeze(2)), KS, KS)
    V(lambda e: e.tensor_copy(out=sinT[:, :, 1:2], in_=cst[:, SINS:SINS + 16].unsqueeze(2)), KS, KS)
    t4 = ar.alloc([16, 64])
    t5 = ar.alloc([16, 64])
    cps = cosT.ap[0][0]

    def cmul(oc, osn, ic, isn, mc, ms, a4, a5):
        V(lambda e: e.tensor_tensor(out=a4, in0=ic, in1=mc, op=ALU.mult), KS, KS)
        V(lambda e: e.tensor_tensor(out=a5, in0=isn, in1=ms, op=ALU.mult), KS, KS)
        V(lambda e: e.tensor_tensor(out=oc, in0=a4, in1=a5, op=ALU.subtract), KS, KS)
        V(lambda e: e.tensor_tensor(out=a4, in0=ic, in1=ms, op=ALU.mult), KS, KS)
        V(lambda e: e.tensor_tensor(out=a5, in0=isn, in1=mc, op=ALU.mult), KS, KS)
        V(lambda e: e.tensor_tensor(out=osn, in0=a4, in1=a5, op=ALU.add), KS, KS)

    for n in (2, 4, 8, 16, 32, 64, 128):
        if n < 128:
            oc, osn = cosT[:, :, n:n + 1], sinT[:, :, n:n + 1]
        else:
            oc, osn = cst[:, C128:C128 + 16].unsqueeze(2), cst[:, S128:S128 + 16].unsqueeze(2)
        cmul(oc, osn, cosT[:, :, n - 1:n], sinT[:, :, n - 1:n], cosT[:, :, 1:2], sinT[:, :, 1:2], t4[:, :, 0:1], t5[:, :, 0:1])
        if n < 128:
            m = n - 1
            mc = bass.AP(cosT.tensor, cosT.offset + n, [[cps, 128], [128, 16], [0, m]])
            ms = bass.AP(sinT.tensor, sinT.offset + n, [[cps, 128], [128, 16], [0, m]])
            cmul(cosT[:, :, n + 1:2 * n], sinT[:, :, n + 1:2 * n], cosT[:, :, 1:n], sinT[:, :, 1:n], mc, ms, t4[:, :, 0:m], t5[:, :, 0:m])

    sc.barrier()

    ar.off = base
    NXH = 3
    xh = [ar.alloc([1024]) for _ in range(NXH)]
    xn = ar.alloc([1024], BF16)
    jbf = ar.alloc([1024], BF16)
    xnT = ar.alloc([8, 128], BF16)
    pT = ar.alloc([8, 128])
    usb = ar.alloc([4, 128], BF16)
    lt = [ar.alloc([128]) for _ in range(10)]
    ya = ar.alloc([4, 128], BF16)
    yb = ar.alloc([4, 128], BF16)
    s5t = [ar.alloc([128]) for _ in range(6)]
    hreb = ar.alloc([128], BF16)
    himb = ar.alloc([128], BF16)
    gsf = ar.alloc([4, 128])
    gsb = ar.alloc([4, 128], BF16)
    sm = ar.alloc([16])
    obuf = ar.alloc([1024])
    xn2bs = [ar.alloc([1024], BF16) for _ in range(2)]
    xn2T = ar.alloc([8, 128], BF16)
    wqb = [ar.alloc([8, 128], BF16) for _ in range(3)]
    qTs = [ar.alloc([4, 128]) for _ in range(2)]
    ssb = [ar.alloc([4, 128]) for _ in range(2)]
    mr = ar.alloc([256])
    v_all = ar.alloc([16, 16])
    ix_all = ar.alloc([16, 16], U32)
    ixf = ar.alloc([16, 16])
    cand = [ar.alloc([256]) for _ in range(2)]
    sc_all = ar.alloc([8, 16])
    pos_all = ar.alloc([8, 16], U32)
    pa_i = ar.alloc([128], I32)
    pb_i = ar.alloc([128], I32)
    paf = ar.alloc([128])
    pbf = ar.alloc([128])
    oh = ar.alloc([16, 16])
    i1s = ar.alloc([128])
    i2s = ar.alloc([128])
    ef = ar.alloc([128])
    eis = [ar.alloc([128], I32) for _ in range(2)]
    ex = ar.alloc([8, 16])
    zz = ar.alloc([16])
    gws = [ar.alloc([128]) for _ in range(2)]
    dd = ar.alloc([128])
    coef = ar.alloc([128])
    ND = 4
    diag = [ar.alloc([128], BF16) for _ in range(ND)]
    GS = 4
    gtmp = [ar.alloc([GS]) for _ in range(2)]
    NR = min(12, (ar.cap - ar.off) // 4096)
    assert NR >= 2 * GS, NR
    ring = [ar.alloc([2048], BF16) for _ in range(NR)]

    tpb = psum[:, 0, :].bitcast(BF16)

    def rstd_from(ss_ap, n, out_ap, kr, kw):
        A(lambda e: e.activation(out=out_ap, in_=ss_ap, func=AF.Sqrt, scale=1.0 / n, bias=EPS), kr, kw)
        V(lambda e: e.reciprocal(out=out_ap, in_=out_ap), kw, kw)

    def transpose8(src_bf, dst, ksrc, kdst):
        def f(e):
            ins = None
            for kc in range(8):
                ins = e.transpose(out=tpb[:, kc * 128:(kc + 1) * 128], in_=src_bf[:, kc * 128:(kc + 1) * 128], identity=ident)
            return ins
        T(f, [ksrc, 'ident'], pk(0))
        A(lambda e: e.copy(out=dst.rearrange("p a b -> p (a b)"), in_=tpb), pk(0), [kdst])

    rcount = [0]
    _bc = {}

    def bchk(e):
        if 'r' not in _bc:
            _bc['r'] = e.to_reg(16383)
        return _bc['r']

    RHO_BC = RHO_BCAST

    def front(t):
        first = (t % TPS == 0)
        xb = xh[t % NXH]
        kx = f'xh{t % NXH}'
        p2 = t % 2
        xn2 = xn2bs[p2]
        kxn2 = f'xn2_{p2}'
        ei = eis[p2]
        kei = f'ei{p2}'
        gw = gws[p2]
        kgw = f'gw{p2}'
        if t == 0:
            dma('sync', xb, x_d[0:128, :], [], [kx], 'xl0')
        if t + 1 < NT:
            n1 = (t + 1) % NXH
            dma('sync', xh[n1], x_d[(t + 1) * 128:(t + 2) * 128, :], [], [f'xh{n1}'], f'xl{n1}')
        A(lambda e: e.activation(out=jbf, in_=xb, func=AF.Square, accum_out=sm[:, 0:1]), [kx], ['jbf', 'sm0'])
        rstd_from(sm[:, 0:1], 1024.0, sm[:, 0:1], ['sm0'], ['sm0'])
        V(lambda e: e.scalar_tensor_tensor(out=xn, in0=xb, scalar=sm[:, 0:1], in1=gbc[:, 0, :], op0=ALU.mult, op1=ALU.mult),
          [kx, 'sm0', 'gbc'], ['xn'])
        transpose8(xn, xnT, 'xn', 'xnT')
        if first:
            V(lambda e: e.memset(xah[:, :, 0:3], 0.0), [], ['xah'])
        else:
            A(lambda e: e.copy(out=xah[:, :, 0:3], in_=xah[:, :, 128:131]), ['xah'], ['xah'])
        for gi in range(3):
            b = 1 + (gi % 2)

            def f(e, gi=gi, b=b):
                ins = None
                for j in range(4):
                    m = gi * 4 + j
                    for kc in range(8):
                        ins = e.matmul(psum[:, b, j * 128:(j + 1) * 128], lhsT=w_in[:, kc, m * 128:(m + 1) * 128],
                                       rhs=xnT[:, kc, :], start=(kc == 0), stop=(kc == 7))
                return ins
            T(f, ['w_in', 'xnT'], pk(b))
            if gi == 0:
                A(lambda e, b=b: e.copy(out=xah[:, :, 3:131], in_=psum[:, b, :].rearrange("p (a b) -> p a b", a=4)), pk(b), ['xah'])
            elif gi == 1:
                A(lambda e, b=b: e.copy(out=pT[:, 0:4, :], in_=psum[:, b, :].rearrange("p (a b) -> p a b", a=4)), pk(b), ['gate'])
            else:
                A(lambda e, b=b: e.copy(out=pT[:, 4:8, :], in_=psum[:, b, :].rearrange("p (a b) -> p a b", a=4)), pk(b), ['usf'])
                A(lambda e: e.copy(out=usb, in_=pT[:, 4:8, :]), ['usf'], ['usb'])
        xc, rr, ig, aa, a2, uu, hh_, gg, yar, sq = lt
        for c in range(4):
            cw = lambda k, c=c: vecs[:, CW + c * 4 + k:CW + c * 4 + k + 1]
            V(lambda e, c=c, cw=cw: e.tensor_scalar(out=xc, in0=xah[:, c, 0:128], scalar1=cw(0), scalar2=vecs[:, CB + c:CB + c + 1],
                                                    op0=ALU.mult, op1=ALU.add), ['xah', 'vecs'], ['xc'])
            for k in range(1, 4):
                V(lambda e, c=c, k=k, cw=cw: e.scalar_tensor_tensor(out=xc, in0=xah[:, c, k:k + 128], scalar=cw(k), in1=xc,
                                                                    op0=ALU.mult, op1=ALU.add), ['xah', 'vecs', 'xc'], ['xc'])

            def f(e, c=c):
                e.matmul(psum[:, 2, 0:128], lhsT=wax[:, c, :], rhs=xc, start=True, stop=True)
                return e.matmul(psum[:, 2, 128:256], lhsT=wax[:, 4 + c, :], rhs=xc, start=True, stop=True)
            T(f, ['wax', 'xc'], pk(2, 0) + pk(2, 1))
            A(lambda e, c=c: e.activation(out=rr, in_=psum[:, 2, 0:128], func=AF.Sigmoid, bias=vecs[:, BA + c:BA + c + 1]), pk(2, 0) + ['vecs'], ['rr'])
            A(lambda e, c=c: e.activation(out=ig, in_=psum[:, 2, 128:256], func=AF.Sigmoid, bias=vecs[:, BX + c:BX + c + 1]), pk(2, 1) + ['vecs'], ['ig'])
            A(lambda e, c=c: e.activation(out=aa, in_=rr, func=AF.Exp, scale=cst[:, SC8 + c:SC8 + c + 1]), ['rr'], ['aa'])
            A(lambda e, c=c: e.activation(out=a2, in_=rr, func=AF.Exp, scale=cst[:, SC16 + c:SC16 + c + 1]), ['rr'], ['a2'])
            A(lambda e: e.activation(out=a2, in_=a2, func=AF.Sqrt, scale=-1.0, bias=1.0), ['a2'], ['a2'])
            V(lambda e: e.tensor_tensor(out=uu, in0=ig, in1=xc, op=ALU.mult), ['ig', 'xc'], ['uu'])
            V(lambda e: e.tensor_tensor(out=uu, in0=uu, in1=a2, op=ALU.mult), ['uu', 'a2'], ['uu'])
            init = 0.0 if first else hl[:, c:c + 1]
            V(lambda e, init=init: e.tensor_tensor_scan(out=hh_, data0=aa, data1=uu, initial=init, op0=ALU.mult, op1=ALU.add),
              ['aa', 'uu', f'hl{c}'], ['hh'])
            A(lambda e, c=c: e.copy(out=hl[:, c:c + 1], in_=hh_[:, 127:128]), ['hh'], [f'hl{c}'])
            A(lambda e, c=c: e.activation(out=gg, in_=pT[:, c, :], func=AF.Gelu_apprx_tanh), ['gate'], ['gg'])
            V(lambda e: e.tensor_tensor(out=yar, in0=hh_, in1=gg, op=ALU.mult), ['hh', 'gg'], ['yar'])
            A(lambda e: e.activation(out=sq, in_=yar, func=AF.Square), ['yar'], ['sq'])
            T(lambda e, c=c: e.matmul(psum[:, 3, 256:257], lhsT=sq, rhs=ones[:, 0:1], start=(c == 0), stop=(c == 3)), ['sq', 'ones'], pk(3, 2))
            A(lambda e, c=c: e.activation(out=ya[:, c, :], in_=yar, func=AF.Copy, scale=vecs[:, GNL + c:GNL + c + 1]), ['yar', 'vecs'], ['ya'])
        t1, t2, wre, wim, gre, gim = s5t
        for cc in range(4):
            for pq in range(4):
                q = cc * 4 + pq
                bq = 4 + (q % 2)

                def f(e, q=q, cc=cc, bq=bq):
                    e.matmul(psum[:, bq, 0:128], lhsT=bbR[:, q, :], rhs=usb[:, cc, :], start=True, stop=True)
                    return e.matmul(psum[:, bq, 128:256], lhsT=bbI[:, q, :], rhs=usb[:, cc, :], start=True, stop=True)
                T(f, ['usb'], pk(bq, 0) + pk(bq, 1))
                buR = psum[:, bq, 0:128]
                buI = psum[:, bq, 128:256]
                kb = pk(bq, 0) + pk(bq, 1)
                cq = cosT[:, q, :]
                sq_ = sinT[:, q, :]
                rho = bcast(cst, MAGS + q, [[0, 128]]) if RHO_BC else rhoT[:, q, :]
                V(lambda e, cq=cq, buR=buR: e.tensor_tensor(out=t1, in0=buR, in1=cq, op=ALU.mult), kb, ['t1'])
                V(lambda e, sq_=sq_, buI=buI: e.tensor_tensor(out=t2, in0=buI, in1=sq_, op=ALU.mult), kb, ['t2'])
                V(lambda e: e.tensor_tensor(out=wre, in0=t1, in1=t2, op=ALU.add), ['t1', 't2'], ['wre'])
                V(lambda e, cq=cq, buI=buI: e.tensor_tensor(out=t1, in0=buI, in1=cq, op=ALU.mult), kb + ['t1'], ['t1'])
                V(lambda e, sq_=sq_, buR=buR: e.tensor_tensor(out=t2, in0=buR, in1=sq_, op=ALU.mult), kb + ['t2'], ['t2'])
                V(lambda e: e.tensor_tensor(out=wim, in0=t1, in1=t2, op=ALU.subtract), ['t1', 't2'], ['wim'])
                i_r = 0.0 if first else ire[:, q:q + 1]
                i_i = 0.0 if first else iim[:, q:q + 1]
                V(lambda e, rho=rho, i_r=i_r: e.tensor_tensor_scan(out=gre, data0=rho, data1=wre, initial=i_r, op0=ALU.mult, op1=ALU.add),
                  ['wre', f'ic{q}'], ['gre'])
                V(lambda e, rho=rho, i_i=i_i: e.tensor_tensor_scan(out=gim, data0=rho, data1=wim, initial=i_i, op0=ALU.mult, op1=ALU.add),
                  ['wim', f'ic{q}'], ['gim'])
                c128 = cst[:, C128 + q:C128 + q + 1]
                s128 = cst[:, S128 + q:S128 + q + 1]
                V(lambda e, s128=s128: e.tensor_scalar(out=sm[:, 8:9], in0=gim[:, 127:128], scalar1=s128, scalar2=None, op0=ALU.mult), ['gim'], ['sm8'])
                V(lambda e, q=q, c128=c128: e.scalar_tensor_tensor(out=ire[:, q:q + 1], in0=gre[:, 127:128], scalar=c128, in1=sm[:, 8:9],
                                                                   op0=ALU.mult, op1=ALU.subtract), ['gre', 'sm8', f'ic{q}'], [f'ic{q}'])
                V(lambda e, s128=s128: e.tensor_scalar(out=sm[:, 9:10], in0=gre[:, 127:128], scalar1=s128, scalar2=None, op0=ALU.mult), ['gre'], ['sm9'])
                V(lambda e, q=q, c128=c128: e.scalar_tensor_tensor(out=iim[:, q:q + 1], in0=gim[:, 127:128], scalar=c128, in1=sm[:, 9:10],
                                                                   op0=ALU.mult, op1=ALU.add), ['gim', 'sm9', f'ic{q}'], [f'ic{q}'])
                V(lambda e, cq=cq: e.tensor_tensor(out=t1, in0=gre, in1=cq, op=ALU.mult), ['gre', 't1'], ['t1'])
                V(lambda e, sq_=sq_: e.tensor_tensor(out=t2, in0=gim, in1=sq_, op=ALU.mult), ['gim', 't2'], ['t2'])
                V(lambda e: e.tensor_tensor(out=hreb, in0=t1, in1=t2, op=ALU.subtract), ['t1', 't2'], ['hreb'])
                V(lambda e, cq=cq: e.tensor_tensor(out=t1, in0=gim, in1=cq, op=ALU.mult), ['gim', 't1'], ['t1'])
                V(lambda e, sq_=sq_: e.tensor_tensor(out=t2, in0=gre, in1=sq_, op=ALU.mult), ['gre', 't2'], ['t2'])
                V(lambda e: e.tensor_tensor(out=himb, in0=t1, in1=t2, op=ALU.add), ['t1', 't2'], ['himb'])

                def f2(e, q=q, pq=pq):
                    e.matmul(psum[:, 1, 0:128], lhsT=cR[:, q, :], rhs=hreb, start=(pq == 0), stop=False)
                    return e.matmul(psum[:, 1, 0:128], lhsT=cIn[:, q, :], rhs=himb, start=False, stop=(pq == 3))
                T(f2, ['hreb', 'himb'], pk(1, 0))
            V(lambda e, cc=cc: e.scalar_tensor_tensor(out=t1, in0=pT[:, 4 + cc, :], scalar=vecs[:, SD + cc:SD + cc + 1], in1=psum[:, 1, 0:128],
                                                      op0=ALU.mult, op1=ALU.add), ['usf', 'vecs', 't1'] + pk(1, 0), ['t1'])
            A(lambda e, cc=cc: e.activation(out=gsf[:, cc, :], in_=t1, func=AF.Gelu_apprx_tanh), ['t1'], ['gsf'])
            A(lambda e, cc=cc: e.copy(out=gsb[:, cc, :], in_=gsf[:, cc, :]), ['gsf'], ['gsb'])
        for m in range(4):
            def f(e, m=m):
                ins = None
                for k in range(4):
                    ins = e.matmul(psum[:, 2, 0:128], lhsT=glu[:, k, m * 128:(m + 1) * 128], rhs=gsb[:, k, :], start=(k == 0), stop=(k == 3))
                return ins
            T(f, ['gsb'], pk(2, 0))
            A(lambda e, m=m: e.activation(out=rr, in_=psum[:, 2, 0:128], func=AF.Sigmoid, bias=vecs[:, GB + m:GB + m + 1]), pk(2, 0) + ['vecs'], ['rr'])
            V(lambda e, m=m: e.tensor_tensor(out=yar, in0=gsf[:, m, :], in1=rr, op=ALU.mult), ['gsf', 'rr'], ['yar'])
            A(lambda e: e.activation(out=sq, in_=yar, func=AF.Square), ['yar'], ['sq'])
            T(lambda e, m=m: e.matmul(psum[:, 3, 257:258], lhsT=sq, rhs=ones[:, 0:1], start=(m == 0), stop=(m == 3)), ['sq', 'ones'], pk(3, 2))
            A(lambda e, m=m: e.activation(out=yb[:, m, :], in_=yar, func=AF.Copy, scale=vecs[:, GNS + m:GNS + m + 1]), ['yar', 'vecs'], ['yb'])
        rstd_from(psum[:, 3, 256:258], 512.0, sm[:, 1:3], pk(3, 2), ['sm12'])
        for half in range(2):
            def f(e, half=half):
                ins = None
                for c in range(4):
                    ins = e.matmul(psum[:, 4 + half, :], lhsT=ya[:, c, :], rhs=w_out[:, c, half * 512:(half + 1) * 512], start=(c == 0), stop=(c == 3))
                return ins
            T(f, ['ya', 'w_out'], pk(4 + half))

            def f(e, half=half):
                ins = None
                for c in range(4):
                    ins = e.matmul(psum[:, 1 + half, :], lhsT=yb[:, c, :], rhs=w_out[:, 4 + c, half * 512:(half + 1) * 512], start=(c == 0), stop=(c == 3))
                return ins
            T(f, ['yb', 'w_out'], pk(1 + half))
            xs = xb[:, half * 512:(half + 1) * 512]
            V(lambda e, half=half, xs=xs: e.scalar_tensor_tensor(out=xs, in0=psum[:, 4 + half, :], scalar=sm[:, 1:2], in1=xs, op0=ALU.mult, op1=ALU.add),
              pk(4 + half) + ['sm12', kx], [kx])
            V(lambda e, half=half, xs=xs: e.scalar_tensor_tensor(out=xs, in0=psum[:, 1 + half, :], scalar=sm[:, 2:3], in1=xs, op0=ALU.mult, op1=ALU.add),
              pk(1 + half) + ['sm12', kx], [kx])
        A(lambda e: e.activation(out=jbf, in_=xb, func=AF.Square, accum_out=sm[:, 3:4]), [kx], ['jbf', 'sm3'])
        rstd_from(sm[:, 3:4], 1024.0, sm[:, 3:4], ['sm3'], ['sm3'])
        V(lambda e: e.scalar_tensor_tensor(out=xn2, in0=xb, scalar=sm[:, 3:4], in1=gbc[:, 1, :], op0=ALU.mult, op1=ALU.mult),
          [kx, 'sm3', 'gbc'], [kxn2])
        transpose8(xn2, xn2T, kxn2, 'xn2T')
        for grp in range(4):
            bqT = 4 + (grp % 2)
            bsc = 1 + (grp % 2)
            for jj in range(4):
                j = grp * 4 + jj
                wslot = j % 3
                wb = wqb[wslot]
                GD(lambda e, wb=wb, j=j: e.dma_start(out=wb, in_=wq_d.rearrange("(kc p) n -> p kc n", p=128)[:, :, j * 128:(j + 1) * 128]),
                   [], [f'wq{wslot}'], f'wq{wslot}')

                def f(e, wb=wb, jj=jj, bqT=bqT):
                    ins = None
                    for kc in range(8):
                        ins = e.matmul(psum[:, bqT, jj * 128:(jj + 1) * 128], lhsT=wb[:, kc, :], rhs=xn2T[:, kc, :], start=(kc == 0), stop=(kc == 7))
                    return ins
                T(f, [f'wq{wslot}', 'xn2T'], pk(bqT, jj))
            qs = qTs[grp % 2]
            kq = f'qTs{grp % 2}'
            A(lambda e, qs=qs, bqT=bqT: e.copy(out=qs.rearrange("p a b -> p (a b)"), in_=psum[:, bqT, :]), pk(bqT), [kq])

            def f(e, qs=qs, bsc=bsc):
                ins = None
                for jj in range(4):
                    ins = e.matmul(psum[:, bsc, jj * 128:(jj + 1) * 128], lhsT=qs[:, jj, :], rhs=kT[:, jj % 2, :], start=True, stop=True)
                return ins
            T(f, [kq, 'kT'], pk(bsc))
            sb = ssb[grp % 2]
            ks = f'ssb{grp % 2}'
            A(lambda e, sb=sb, bsc=bsc: e.copy(out=sb.rearrange("p a b -> p (a b)"), in_=psum[:, bsc, :]), pk(bsc), [ks])
            for jj in range(4):
                j = grp * 4 + jj
                s_j = sb[:, jj, :]
                V(lambda e, j=j, s_j=s_j: e.max(out=v_all[:, j, 0:8], in_=s_j), [ks], ['v_all'])
                V(lambda e, j=j, s_j=s_j: e.match_replace(out=mr[:, 0:128], in_to_replace=v_all[:, j, 0:8], in_values=s_j, imm_value=-1e30), [ks, 'v_all'], ['mr'])
                V(lambda e, j=j: e.max(out=v_all[:, j, 8:16], in_=mr[:, 0:128]), ['mr'], ['v_all'])
                V(lambda e, j=j, s_j=s_j: e.max_index(out=ix_all[:, j, 0:8], in_max=v_all[:, j, 0:8], in_values=s_j), [ks, 'v_all'], ['ix_all'])
                V(lambda e, j=j, s_j=s_j: e.max_index(out=ix_all[:, j, 8:16], in_max=v_all[:, j, 8:16], in_values=s_j), [ks, 'v_all'], ['ix_all'])
        V(lambda e: e.tensor_copy(out=ixf, in_=ix_all), ['ix_all'], ['ixf'])
        vps = v_all.ap[0][0]
        for h in range(8):
            cd = cand[h % 2]
            kc_ = f'cand{h % 2}'
            in0 = bass.AP(v_all.tensor, v_all.offset + (2 * h) * 16, [[vps, 128], [1, 16], [0, 16]])
            in1 = bass.AP(v_all.tensor, v_all.offset + (2 * h + 1) * 16, [[vps, 128], [0, 16], [1, 16]])
            V(lambda e, cd=cd, in0=in0, in1=in1: e.tensor_tensor(out=cd.rearrange("p (a b) -> p a b", a=16), in0=in0, in1=in1, op=ALU.add), ['v_all'], [kc_])
            V(lambda e, cd=cd, h=h: e.max(out=sc_all[:, h, 0:8], in_=cd), [kc_], ['sc_all'])
            V(lambda e, cd=cd, h=h: e.match_replace(out=mr, in_to_replace=sc_all[:, h, 0:8], in_values=cd, imm_value=-1e30), [kc_, 'sc_all'], ['mr'])
            V(lambda e, h=h: e.max(out=sc_all[:, h, 8:16], in_=mr), ['mr'], ['sc_all'])
            V(lambda e, cd=cd, h=h: e.max_index(out=pos_all[:, h, 0:8], in_max=sc_all[:, h, 0:8], in_values=cd), [kc_, 'sc_all'], ['pos_all'])
            V(lambda e, cd=cd, h=h: e.max_index(out=pos_all[:, h, 8:16], in_max=sc_all[:, h, 8:16], in_values=cd), [kc_, 'sc_all'], ['pos_all'])
        pos2 = pos_all.rearrange("p a b -> p (a b)").bitcast(I32)
        V(lambda e: e.tensor_single_scalar(out=pa_i, in_=pos2, scalar=4, op=ALU.arith_shift_right), ['pos_all'], ['pa_i'])
        V(lambda e: e.tensor_single_scalar(out=pb_i, in_=pos2, scalar=15, op=ALU.bitwise_and), ['pos_all'], ['pb_i'])
        V(lambda e: e.tensor_copy(out=paf, in_=pa_i), ['pa_i'], ['paf'])
        V(lambda e: e.tensor_copy(out=pbf, in_=pb_i), ['pb_i'], ['pbf'])
        iot = bcast(vecs, IOTA, [[0, 16], [1, 16]])
        ixf2 = ixf.rearrange("p a b -> p (a b)")
        for h in range(8):
            for (pf, kpf, off, dst, kd) in ((paf, 'paf', 0, i1s, 'i1s'), (pbf, 'pbf', 16, i2s, 'i2s')):
                pfb = bcast(pf, h * 16, [[1, 16], [0, 16]])
                ixb = bcast(ixf2, h * 32 + off, [[0, 16], [1, 16]])
                V(lambda e, pfb=pfb: e.tensor_tensor(out=oh, in0=pfb, in1=iot, op=ALU.is_equal), [kpf, 'vecs', 'oh'], ['oh'])
                V(lambda e, ixb=ixb: e.tensor_tensor(out=oh, in0=oh, in1=ixb, op=ALU.mult), ['oh', 'ixf'], ['oh'])
                V(lambda e, dst=dst, h=h: e.tensor_reduce(out=dst[:, h * 16:(h + 1) * 16], in_=oh, axis=AX.X, op=ALU.add), ['oh', kd], [kd])
        V(lambda e: e.scalar_tensor_tensor(out=ef, in0=i1s, scalar=128.0, in1=i2s, op0=ALU.mult, op1=ALU.add), ['i1s', 'i2s'], ['ef'])
        V(lambda e: e.tensor_copy(out=ei, in_=ef), ['ef'], [kei])
        sps = sc_all.ap[0][0]
        mxb = bass.AP(sc_all.tensor, sc_all.offset, [[sps, 128], [16, 8], [0, 16]])
        V(lambda e: e.tensor_tensor(out=ex, in0=sc_all, in1=mxb, op=ALU.subtract), ['sc_all'], ['ex'])
        A(lambda e: e.activation(out=ex, in_=ex, func=AF.Exp), ['ex'], ['ex'])
        V(lambda e: e.tensor_reduce(out=zz[:, 0:8], in_=ex, axis=AX.X, op=ALU.add), ['ex'], ['zz'])
        V(lambda e: e.reciprocal(out=zz[:, 0:8], in_=zz[:, 0:8]), ['zz'], ['zz'])
        zb = bcast(zz, 0, [[1, 8], [0, 16]])
        V(lambda e: e.tensor_tensor(out=gw.rearrange("p (a b) -> p a b", a=8), in0=ex, in1=zb, op=ALU.mult), ['ex', 'zz'], [kgw])

    def back(t):
        xb = xh[t % NXH]
        kx = f'xh{t % NXH}'
        p2 = t % 2
        xn2 = xn2bs[p2]
        kxn2 = f'xn2_{p2}'
        ei = eis[p2]
        kei = f'ei{p2}'
        gw = gws[p2]
        kgw = f'gw{p2}'
        ddk = [f'dd{n}' for n in range(128)]
        V(lambda e: e.memset(dd, 0.0), [], ddk)
        NG = 128 // GS
        slots = {}

        def gathers(g):
            for n in range(g * GS, (g + 1) * GS):
                rs = rcount[0] % NR
                rcount[0] += 1
                slots[n] = rs
                rb = ring[rs]
                GD(lambda e, rb=rb, n=n: e.indirect_dma_start(out=rb, out_offset=None, in_=uvb_d,
                                                              in_offset=bass.IndirectOffsetOnAxis(ap=ei[:, n:n + 1], axis=0), bounds_check=bchk(e), oob_is_err=False),
                   [kei], [f'ring{rs}'], f'rg{rs}')

        gathers(0)
        for g in range(NG):
            if g + 1 < NG:
                gathers(g + 1)
            cs = slice(g * GS, (g + 1) * GS)
            dk = [f'dd{n}' for n in range(g * GS, (g + 1) * GS)]
            for n in range(g * GS, (g + 1) * GS):
                rs = slots[n]
                rb = ring[rs]
                V(lambda e, rb=rb, n=n: e.scalar_tensor_tensor(out=rb[:, 0:1024], in0=rb[:, 0:1024], scalar=1.0, in1=xn2, op0=ALU.mult, op1=ALU.mult,
                                                               accum_out=dd[:, n:n + 1]),
                  [f'ring{rs}', kxn2, f'dd{n}'], [f'dd{n}', f'ring{rs}'])
            kt = f'gt{g % 2}'
            tg = gtmp[g % 2]
            d_ = dd[:, cs]
            G(lambda e, tg=tg, d_=d_: e.tensor_tensor(out=tg, in0=d_, in1=d_, op=ALU.mult), dk, [kt])
            G(lambda e, tg=tg: e.tensor_scalar(out=tg, in0=tg, scalar1=0.044715, scalar2=1.0, op0=ALU.mult, op1=ALU.add), [kt], [kt])
            G(lambda e, tg=tg, d_=d_: e.tensor_tensor(out=tg, in0=tg, in1=d_, op=ALU.mult), [kt] + dk, [kt])
            A(lambda e, tg=tg: e.activation(out=tg, in_=tg, func=AF.Tanh, scale=0.7978845608028654), [kt], [kt])
            G(lambda e, tg=tg: e.tensor_scalar(out=tg, in0=tg, scalar1=1.0, scalar2=0.5, op0=ALU.add, op1=ALU.mult), [kt], [kt])
            G(lambda e, tg=tg, d_=d_: e.tensor_tensor(out=tg, in0=tg, in1=d_, op=ALU.mult), [kt] + dk, [kt])
            G(lambda e, tg=tg, cs=cs: e.tensor_tensor(out=coef[:, cs], in0=tg, in1=gw[:, cs], op=ALU.mult), [kt, kgw], [f'coef{g % 2}'])
            for n in range(g * GS, (g + 1) * GS):
                rs = slots[n]
                rb = ring[rs]
                dg = diag[n % ND]
                kdg = f'diag{n % ND}'
                A(lambda e, dg=dg, n=n: e.activation(out=dg, in_=identf, func=AF.Copy, scale=coef[:, n:n + 1]), [f'coef{g % 2}'], [kdg])

                def f(e, dg=dg, rb=rb, n=n):
                    e.matmul(psum[:, 6, :], lhsT=dg, rhs=rb[:, 1024:1536], start=(n == 0), stop=(n == 127))
                    return e.matmul(psum[:, 7, :], lhsT=dg, rhs=rb[:, 1536:2048], start=(n == 0), stop=(n == 127))
                T(f, [kdg, f'ring{rs}'], pk(6) + pk(7))
        V(lambda e: e.tensor_tensor(out=xb[:, 0:512], in0=psum[:, 6, :], in1=xb[:, 0:512], op=ALU.add), pk(6) + [kx], [kx])
        V(lambda e: e.tensor_tensor(out=xb[:, 512:1024], in0=psum[:, 7, :], in1=xb[:, 512:1024], op=ALU.add), pk(7) + [kx], [kx])
        A(lambda e: e.activation(out=jbf, in_=xb, func=AF.Square, accum_out=sm[:, 4:5]), [kx], ['jbf', 'sm4'])
        rstd_from(sm[:, 4:5], 1024.0, sm[:, 4:5], ['sm4'], ['sm4'])
        V(lambda e: e.scalar_tensor_tensor(out=obuf, in0=xb, scalar=sm[:, 4:5], in1=gbc[:, 2, :], op0=ALU.mult, op1=ALU.mult),
          [kx, 'sm4', 'gbc'], ['obuf'])
        dma('sync', out_d[t * 128:(t + 1) * 128, :], obuf, ['obuf'], [], 'ost')

    def record(fn, t):
        lst = []
        cur[0] = lst
        fn(t)
        cur[0] = None
        return lst

    def run_merged(F, B):
        nf, nb = len(F), len(B)
        i = j = 0
        while i < nf or j < nb:
            if j >= nb or (i < nf and i * nb <= j * nf):
                F[i]()
                i += 1
            else:
                B[j]()
                j += 1

    if PIPE:
        run_merged(record(front, 0), [])
        for t in range(1, NT):
            Fl = record(front, t)
            Bl = record(back, t - 1)
            run_merged(Fl, Bl)
        run_merged([], record(back, NT - 1))
    else:
        for t in range(NT):
            run_merged(record(front, t), [])
            run_merged(record(back, t), [])
    sc.add('sync', lambda e: e.nop(), [], [], extra=list(sc.last_dma.values()))
    sc.emit(es)
    es.close()
    return nc


def prep_shared(inp):
    f = lambda a: np.ascontiguousarray(np.asarray(a, dtype=np.float32))
    col = lambda v, n: f(v).reshape(n, 128).T
    vecs = np.zeros((128, NV), np.float32)
    cw = f(inp['conv_w'][0])
    for c in range(4):
        for k in range(4):
            vecs[:, c * 4 + k] = cw[k, c * 128:(c + 1) * 128]
    vecs[:, 16:20] = col(inp['conv_b'][0], 4)
    vecs[:, 20:24] = col(inp['lru_ba'][0], 4)
    vecs[:, 24:28] = col(inp['lru_bx'][0], 4)
    vecs[:, 28:32] = col(inp['lru_lambda'][0], 4)
    vecs[:, 32:36] = col(inp['s5_d'][0], 4)
    vecs[:, 36:40] = col(inp['s5_glu_b'][0], 4)
    vecs[:, 40:44] = col(inp['gnorm_lru'][0], 4)
    vecs[:, 44:48] = col(inp['gnorm_s5'][0], 4)
    lr = f(inp['s5_lambda_re'][0])
    li = f(inp['s5_lambda_im'][0])
    ld = f(inp['s5_log_dt'][0])
    lrs = lr.reshape(16, 128).T
    lis = li.reshape(16, 128).T
    lds = np.repeat(ld.reshape(16, 2, 1), 64, axis=2).reshape(16, 128).T
    vecs[:, 48:64] = lrs
    vecs[:, 64:80] = lis
    vecs[:, 80:96] = lds
    vecs[:, 96:112] = np.arange(16, dtype=np.float32)[None, :]
    wax = np.zeros((128, 8, 128), np.float32)
    wa = f(inp['lru_wa'][0])
    wx = f(inp['lru_wx'][0])
    for h in range(8):
        c, o = h // 2, (h % 2) * 64
        wax[o:o + 64, c, o:o + 64] = wa[h]
        wax[o:o + 64, 4 + c, o:o + 64] = wx[h]
    kT = np.stack([f(inp['peer_k1'][0]).T, f(inp['peer_k2'][0]).T], axis=1)
    gbc = np.stack([np.broadcast_to(f(inp['norm1_g'][0]), (128, 1024)),
                    np.broadcast_to(f(inp['norm2_g'][0]), (128, 1024)),
                    np.broadcast_to(f(inp['normf_g']), (128, 1024))], axis=1)
    bR = f(inp['s5_b_re'][0])
    bI = f(inp['s5_b_im'][0])
    cRe = f(inp['s5_c_re'][0])
    cIm = f(inp['s5_c_im'][0])
    s5l = np.zeros((128, 5, 16, 128), np.float32)
    s5c = np.zeros((128, 2, 16, 128), np.float32)
    for g in range(32):
        q, r0, c0 = g // 2, (g % 8) * 16, (g % 2) * 64
        s5l[r0:r0 + 16, 0, q, c0:c0 + 64] = bR[g].T
        s5l[r0:r0 + 16, 1, q, c0:c0 + 64] = bI[g].T
        s5c[c0:c0 + 64, 0, q, r0:r0 + 16] = cRe[g].T
        s5c[c0:c0 + 64, 1, q, r0:r0 + 16] = cIm[g].T
    s5l[:, 2] = lrs.T[None, :, :]
    s5l[:, 3] = lis.T[None, :, :]
    s5l[:, 4] = lds.T[None, :, :]
    return dict(
        w_in=f(inp['w_in'][0]), w_out=f(inp['w_out'][0]), glu_w=f(inp['s5_glu_w'][0]), wq=f(inp['peer_wq'][0]),
        peer_u=f(inp['peer_u'][0]), peer_v=f(inp['peer_v'][0]), kT=f(kT), vecs=vecs, wax=wax, gbc=f(gbc),
        s5l=f(s5l.reshape(128, 5, 2048)), s5c=f(s5c.reshape(128, 2, 2048)))


_NC_CACHE = {}


def run(inputs, NT, ncores=NCORES, stage=2):
    x = np.asarray(inputs['x'], dtype=np.float32)
    shared = prep_shared(inputs)
    if (NT, stage) not in _NC_CACHE:
        _NC_CACHE[(NT, stage)] = build(NT, stage)
    nc = _NC_CACHE[(NT, stage)]
    in_maps = []
    for c in range(ncores):
        xc = x[4 * c:4 * c + 4].reshape(-1, 1024)[:NT * 128]
        m = dict(shared)
        m['x'] = np.ascontiguousarray(xc)
        in_maps.append(m)
    res = run_bass_kernel_spmd(nc, in_maps, core_ids=list(range(ncores)))
    return [r['out'] for r in res.results]


def kernel(**inputs):
    outs = run(inputs, 4 * TPS)
    out = np.stack([o.reshape(4, 2048, 1024) for o in outs], axis=0).reshape(32, 2048, 1024)
    return out.astype(np.float32)
```

```python
import math
from contextlib import ExitStack

import numpy as np
import concourse.bass as bass
import concourse.mybir as mybir
from concourse.bass_utils import run_bass_kernel_spmd

F32 = mybir.dt.float32
BF16 = mybir.dt.bfloat16
I32 = mybir.dt.int32
U32 = mybir.dt.uint32
ALU = mybir.AluOpType
AF = mybir.ActivationFunctionType
AX = mybir.AxisListType

ENG = ('tensor', 'vector', 'scalar', 'gpsimd', 'sync')
CAP = 30000
NCORES = 8
TPS = 16
EPS = 1e-6
TWO_PI = 6.283185
NV = 112
RHO_BCAST = True


class Op:
    __slots__ = ('eng', 'fn', 'deps', 'sig', 'is_dma', 'chan', 'chan_val', 'signo')


class Sched:
    def __init__(self, nc):
        self.nc = nc
        self.ops = {e: [] for e in ENG}
        self.lastw = {}
        self.readers = {}
        self.chan_cnt = {}
        self.last_op = {}
        self.last_dma = {}

    def add(self, eng, fn, reads=(), writes=(), chan=None, extra=()):
        op = Op()
        op.eng = eng
        op.fn = fn
        op.is_dma = chan is not None
        op.chan = chan
        op.sig = False
        op.signo = 0
        op.chan_val = 0
        deps = {}
        for r in reads:
            w = self.lastw.get(r)
            if w is not None:
                deps[id(w)] = w
        for k in writes:
            w = self.lastw.get(k)
            if w is not None:
                deps[id(w)] = w
            for d in self.readers.get(k, ()):
                deps[id(d)] = d
        for d in extra:
            deps[id(d)] = d
        op.deps = []
        for d in deps.values():
            if (not d.is_dma) and d.eng == 'tensor' and eng == 'tensor':
                continue
            op.deps.append(d)
            if not d.is_dma:
                d.sig = True
        if op.is_dma:
            c = self.chan_cnt.get(chan, 0) + 1
            self.chan_cnt[chan] = c
            op.chan_val = 16 * c
            self.last_dma[chan] = op
        else:
            self.last_op[eng] = op
        for r in reads:
            self.readers.setdefault(r, []).append(op)
        for k in writes:
            self.lastw[k] = op
            self.readers[k] = []
        self.ops[eng].append(op)
        return op

    def barrier(self):
        prev = list(self.last_op.values()) + list(self.last_dma.values())
        for e in ENG:
            self.add(e, lambda eo: eo.nop(), extra=prev)
        self.lastw = {}
        self.readers = {}

    def emit(self, es):
        nc = self.nc
        esem = {}
        for e in ENG:
            n = 0
            for op in self.ops[e]:
                if op.sig and not op.is_dma:
                    n += 1
                    op.signo = n
            nep = max(1, (n + CAP - 1) // CAP)
            esem[e] = [es.enter_context(nc.semaphore(f"s_{e}_{i}")) for i in range(nep)]
        csem = {c: es.enter_context(nc.semaphore(f"c_{c}")) for c in self.chan_cnt}
        block = es.enter_context(nc.Block())
        for e in ENG:
            ops = self.ops[e]

            def body(eo, e=e, ops=ops):
                waited = {}
                maxep = {}
                for op in ops:
                    need = {}
                    for d in op.deps:
                        if d.is_dma:
                            key = ('c', d.chan)
                            val = d.chan_val
                        else:
                            ep = (d.signo - 1) // CAP
                            if maxep.get(d.eng, -1) > ep:
                                continue
                            key = ('e', d.eng, ep)
                            val = (d.signo - 1) % CAP + 1
                        if need.get(key, 0) < val:
                            need[key] = val
                    for key, val in need.items():
                        if waited.get(key, 0) >= val:
                            continue
                        if key[0] == 'c':
                            eo.wait_ge(csem[key[1]], val)
                        else:
                            eo.wait_ge(esem[key[1]][key[2]], val)
                            if maxep.get(key[1], -1) < key[2]:
                                maxep[key[1]] = key[2]
                        waited[key] = val
                    ins = op.fn(eo)
                    if op.is_dma:
                        ins.then_inc(csem[op.chan], 16)
                    elif op.sig:
                        ins.then_inc(esem[e][(op.signo - 1) // CAP], 1)

            getattr(block, e)(body)


class Arena:
    def __init__(self, nc, es, nbytes):
        self.t = es.enter_context(nc.sbuf_tensor("arena", [128, nbytes // 4], F32))
        self.off = 0
        self.cap = nbytes

    def alloc(self, shape, dtype=F32):
        esz = {F32: 4, I32: 4, U32: 4, BF16: 2}[dtype]
        n = int(np.prod(shape))
        nb = (n * esz + 31) // 32 * 32
        o = self.off
        self.off += nb
        assert self.off <= self.cap, (self.off, self.cap)
        ap = self.t[:, o // 4:(o + nb) // 4]
        if dtype != F32:
            ap = ap.bitcast(dtype)
        ap = ap[:, 0:n]
        if len(shape) == 2:
            ap = ap.rearrange("p (a b) -> p a b", a=shape[0])
        elif len(shape) == 3:
            ap = ap.rearrange("p (a b c) -> p a b c", a=shape[0], b=shape[1])
        return ap


def bcast(ap2d_src, off, dims):
    base = ap2d_src
    pstep = base.ap[0][0]
    return bass.AP(base.tensor, base.offset + off, [[pstep, 128]] + [list(d) for d in dims])


def build(NT, stage=2, PIPE=True):
    nc = bass.Bass("TRN2", target_bir_lowering=False)
    es = ExitStack()

    def D(n, s, dt=F32, kind="ExternalInput"):
        return nc.dram_tensor(n, s, dt, kind=kind).ap()

    x_d = D("x", [NT * 128, 1024])
    out_d = D("out", [NT * 128, 1024], kind="ExternalOutput")
    w_in_d = D("w_in", [1024, 1536])
    w_out_d = D("w_out", [1024, 1024])
    glu_d = D("glu_w", [512, 512])
    wq_d = D("wq", [1024, 2048])
    pu_d = D("peer_u", [16384, 1024])
    pv_d = D("peer_v", [16384, 1024])
    kT_d = D("kT", [128, 2, 128])
    vecs_d = D("vecs", [128, NV])
    wax_d = D("wax", [128, 8, 128])
    gbc_d = D("gbc", [128, 3, 1024])
    s5l_d = D("s5l", [128, 5, 2048])
    s5c_d = D("s5c", [128, 2, 2048])
    uvb_d = nc.dram_tensor("uvb_scr", [16384, 2048], BF16).ap()

    ar = Arena(nc, es, 207 * 1024)
    psum = es.enter_context(nc.psum_tensor("ps", [128, 8, 512], F32))
    sc = Sched(nc)

    cur = [None]

    def _emit(eng, fn, r, w, chan=None):
        if cur[0] is not None:
            cur[0].append(lambda: sc.add(eng, fn, r, w, chan=chan))
            return None
        return sc.add(eng, fn, r, w, chan=chan)

    def T(fn, r=(), w=()):
        return _emit('tensor', fn, r, w)

    def V(fn, r=(), w=()):
        return _emit('vector', fn, r, w)

    def A(fn, r=(), w=()):
        return _emit('scalar', fn, r, w)

    def G(fn, r=(), w=()):
        return _emit('gpsimd', fn, r, w)

    def GD(fn, r, w, chan):
        return _emit('gpsimd', fn, r, w, chan)

    clc = [0]

    def dma(eng, out, in_, r, w, chan):
        if chan == 'cl':
            chan = f'cl{clc[0]}'
            clc[0] += 1
        return _emit(eng, lambda e: e.dma_start(out=out, in_=in_), r, w, chan)

    def pk(b, q=None):
        return [f"ps{b}_{i}" for i in range(4)] if q is None else [f"ps{b}_{q}"]

    w_in = ar.alloc([8, 1536], BF16)
    w_out = ar.alloc([8, 1024], BF16)
    glu = ar.alloc([4, 512], BF16)
    bbR = ar.alloc([16, 128], BF16)
    bbI = ar.alloc([16, 128], BF16)
    cR = ar.alloc([16, 128], BF16)
    cIn = ar.alloc([16, 128], BF16)
    cosT = ar.alloc([16, 128])
    sinT = ar.alloc([16, 128])
    rhoT = None if RHO_BCAST else ar.alloc([16, 128])
    wax = ar.alloc([8, 128])
    kT = ar.alloc([2, 128])
    gbc = ar.alloc([3, 1024])
    vecs = ar.alloc([NV])
    cst = ar.alloc([128])
    ident = ar.alloc([128], BF16)
    identf = ar.alloc([128])
    ones = ar.alloc([8])
    hl = ar.alloc([4])
    ire = ar.alloc([16])
    iim = ar.alloc([16])
    xah = ar.alloc([4, 131])
    base = ar.off

    CW, CB, BA, BX, LAM, SD, GB, GNL, GNS, LRS, LIS, LDS, IOTA = 0, 16, 20, 24, 28, 32, 36, 40, 44, 48, 64, 80, 96
    SC8, SC16, MAGS, COSS, SINS, C128, S128 = 0, 4, 8, 24, 40, 56, 72

    wl_prev = [None]

    def wload(dst, src_):
        i = len(sc.ops['gpsimd'])
        op = sc.add('gpsimd', lambda e: e.dma_start(out=dst, in_=src_), [], [], chan=f'wl{i % 2}',
                    extra=[wl_prev[0]] if wl_prev[0] is not None else [])
        wl_prev[0] = op

    for kc in range(8):
        wload(w_in[:, kc, :], w_in_d[kc * 128:(kc + 1) * 128, :])
        wload(w_out[:, kc, :], w_out_d[kc * 128:(kc + 1) * 128, :])
    for kc in range(4):
        wload(glu[:, kc, :], glu_d[kc * 128:(kc + 1) * 128, :])
    for i in range(16):
        rs_ = slice(i * 1024, (i + 1) * 1024)
        sc.add('gpsimd', lambda e, rs_=rs_: e.dma_start(out=uvb_d[rs_, 0:1024], in_=pu_d[rs_, :]), [], [], chan='sbu')
        sc.add('gpsimd', lambda e, rs_=rs_: e.dma_start(out=uvb_d[rs_, 1024:2048], in_=pv_d[rs_, :]), [], [], chan='sbv')
    dma('sync', wax, wax_d, [], ['wax'], 'cl')
    dma('sync', kT, kT_d, [], ['kT'], 'cl')
    dma('sync', gbc, gbc_d, [], ['gbc'], 'cl')
    dma('sync', vecs, vecs_d, [], ['vecs'], 'cl')
    G(lambda e: e.memset(identf, 0.0), [], ['identf'])
    G(lambda e: e.affine_select(out=identf, in_=identf, pattern=[[-1, 128]], compare_op=ALU.not_equal,
                                fill=1.0, base=0, channel_multiplier=1), ['identf'], ['identf'])
    V(lambda e: e.tensor_copy(out=ident, in_=identf), ['identf'], ['ident'])
    G(lambda e: e.memset(ones, 1.0), [], ['ones'])
    G(lambda e: e.memset(cst, 0.0), [], ['cst'])

    A(lambda e: e.activation(out=cst[:, 100:104], in_=vecs[:, LAM:LAM + 4], func=AF.Exp, scale=-1.0), ['vecs', 'cst'], ['cst'])
    A(lambda e: e.activation(out=cst[:, 104:108], in_=cst[:, 100:104], func=AF.Ln, bias=1.0), ['cst'], ['cst'])
    V(lambda e: e.tensor_scalar(out=cst[:, SC8:SC8 + 4], in0=cst[:, 104:108], scalar1=-8.0, scalar2=None, op0=ALU.mult), ['cst'], ['cst'])
    V(lambda e: e.tensor_scalar(out=cst[:, SC16:SC16 + 4], in0=cst[:, 104:108], scalar1=-16.0, scalar2=None, op0=ALU.mult), ['cst'], ['cst'])

    def abar(lamR, lamI, logdt, mag, cosv, sinv, tmp, tmpi, kp):
        t0, t1, t2 = tmp
        k = [kp + str(i) for i in range(8)]
        A(lambda e: e.activation(out=t0, in_=logdt, func=AF.Exp), [k[0]], [k[1]])
        V(lambda e: e.tensor_tensor(out=t1, in0=lamI, in1=t0, op=ALU.mult), [k[0], k[1]], [k[2]])
        V(lambda e: e.tensor_tensor(out=t2, in0=lamR, in1=t0, op=ALU.mult), [k[0], k[1]], [k[3]])
        A(lambda e: e.activation(out=mag, in_=t2, func=AF.Exp), [k[3]], [k[4]])
        V(lambda e: e.tensor_scalar(out=t1, in0=t1, scalar1=1.0 / (2.0 * math.pi), scalar2=None, op0=ALU.mult), [k[2]], [k[2]])
        V(lambda e: e.tensor_copy(out=tmpi, in_=t1), [k[2]], [k[5]])
        V(lambda e: e.tensor_copy(out=t0, in_=tmpi), [k[5], k[1]], [k[1]])
        V(lambda e: e.tensor_tensor(out=t1, in0=t1, in1=t0, op=ALU.subtract), [k[2], k[1]], [k[2]])
        A(lambda e: e.activation(out=sinv, in_=t1, func=AF.Sin, scale=TWO_PI), [k[2]], [k[6]])
        V(lambda e: e.tensor_scalar(out=t0, in0=t1, scalar1=0.25, scalar2=None, op0=ALU.is_gt), [k[2], k[1]], [k[1]])
        V(lambda e: e.scalar_tensor_tensor(out=t2, in0=t1, scalar=0.25, in1=t0, op0=ALU.add, op1=ALU.subtract), [k[2], k[1], k[3], k[4]], [k[3]])
        A(lambda e: e.activation(out=cosv, in_=t2, func=AF.Sin, scale=TWO_PI), [k[3]], [k[7]])
        return k

    LT = [ar.alloc([1024]) for _ in range(15)]
    lti = ar.alloc([1024], I32)
    bbR2 = bbR.rearrange("p a b -> p (a b)")
    bbI2 = bbI.rearrange("p a b -> p (a b)")
    cR2 = cR.rearrange("p a b -> p (a b)")
    cI2 = cIn.rearrange("p a b -> p (a b)")
    for hf in range(2):
        cs = slice(hf * 1024, (hf + 1) * 1024)
        KL = ['L' + str(i) for i in range(8)]
        allL = KL + ['LT']
        for i in range(5):
            dma('sync', LT[i], s5l_d[:, i, cs], [], ['LT'], 'cl')
        dma('sync', LT[5], s5c_d[:, 0, cs], [], ['LT'], 'cl')
        dma('sync', LT[6], s5c_d[:, 1, cs], [], ['LT'], 'cl')
        bRp, bIp, lRl, lIl, ldl, cRp, cIp = LT[0:7]
        magL, cosL, sinL, tA, tB = LT[7:12]
        V(lambda e: e.tensor_copy(out=LT[12][:, 0:1], in_=LT[2][:, 0:1]), allL, allL)
        abar(lRl, lIl, ldl, magL, cosL, sinL, (LT[12], LT[13], LT[14]), lti, 'L')
        V(lambda e, cosL=cosL, magL=magL: e.tensor_tensor(out=cosL, in0=cosL, in1=magL, op=ALU.mult), allL, allL)
        V(lambda e, sinL=sinL, magL=magL: e.tensor_tensor(out=sinL, in0=sinL, in1=magL, op=ALU.mult), allL, allL)
        V(lambda e, cosL=cosL: e.tensor_scalar(out=cosL, in0=cosL, scalar1=-1.0, scalar2=None, op0=ALU.add), allL, allL)
        V(lambda e, magL=magL, lRl=lRl: e.tensor_tensor(out=magL, in0=lRl, in1=lRl, op=ALU.mult), allL, allL)
        V(lambda e, tA=tA, lIl=lIl: e.tensor_tensor(out=tA, in0=lIl, in1=lIl, op=ALU.mult), allL, allL)
        V(lambda e, magL=magL, tA=tA: e.tensor_tensor(out=magL, in0=magL, in1=tA, op=ALU.add), allL, allL)
        V(lambda e, magL=magL: e.reciprocal(out=magL, in_=magL), allL, allL)
        V(lambda e, tA=tA, cosL=cosL, lRl=lRl: e.tensor_tensor(out=tA, in0=cosL, in1=lRl, op=ALU.mult), allL, allL)
        V(lambda e, tB=tB, sinL=sinL, lIl=lIl: e.tensor_tensor(out=tB, in0=sinL, in1=lIl, op=ALU.mult), allL, allL)
        V(lambda e, tA=tA, tB=tB: e.tensor_tensor(out=tA, in0=tA, in1=tB, op=ALU.add), allL, allL)
        V(lambda e, tA=tA, magL=magL: e.tensor_tensor(out=tA, in0=tA, in1=magL, op=ALU.mult), allL, allL)
        V(lambda e, tB=tB, sinL=sinL, lRl=lRl: e.tensor_tensor(out=tB, in0=sinL, in1=lRl, op=ALU.mult), allL, allL)
        V(lambda e, cosL=cosL, lIl=lIl: e.tensor_tensor(out=cosL, in0=cosL, in1=lIl, op=ALU.mult), allL, allL)
        V(lambda e, tB=tB, cosL=cosL: e.tensor_tensor(out=tB, in0=tB, in1=cosL, op=ALU.subtract), allL, allL)
        V(lambda e, tB=tB, magL=magL: e.tensor_tensor(out=tB, in0=tB, in1=magL, op=ALU.mult), allL, allL)
        V(lambda e, cosL=cosL, tA=tA, bRp=bRp: e.tensor_tensor(out=cosL, in0=tA, in1=bRp, op=ALU.mult), allL, allL)
        V(lambda e, sinL=sinL, tB=tB, bIp=bIp: e.tensor_tensor(out=sinL, in0=tB, in1=bIp, op=ALU.mult), allL, allL)
        V(lambda e, cs=cs, cosL=cosL, sinL=sinL: e.tensor_tensor(out=bbR2[:, cs], in0=cosL, in1=sinL, op=ALU.subtract), allL, allL)
        V(lambda e, cosL=cosL, tA=tA, bIp=bIp: e.tensor_tensor(out=cosL, in0=tA, in1=bIp, op=ALU.mult), allL, allL)
        V(lambda e, sinL=sinL, tB=tB, bRp=bRp: e.tensor_tensor(out=sinL, in0=tB, in1=bRp, op=ALU.mult), allL, allL)
        V(lambda e, cs=cs, cosL=cosL, sinL=sinL: e.tensor_tensor(out=bbI2[:, cs], in0=cosL, in1=sinL, op=ALU.add), allL, allL)
        V(lambda e, cs=cs, cRp=cRp: e.tensor_copy(out=cR2[:, cs], in_=cRp), allL, allL)
        V(lambda e, cs=cs, cIp=cIp: e.tensor_scalar(out=cI2[:, cs], in0=cIp, scalar1=-1.0, scalar2=None, op0=ALU.mult), allL, allL)

    st = [ar.alloc([16]) for _ in range(3)]
    sti = ar.alloc([16], I32)
    V(lambda e: e.tensor_copy(out=st[0][:, 0:1], in_=vecs[:, 0:1]), ['vecs'], ['S0'])
    abar(vecs[:, LRS:LRS + 16], vecs[:, LIS:LIS + 16], vecs[:, LDS:LDS + 16],
         cst[:, MAGS:MAGS + 16], cst[:, COSS:COSS + 16], cst[:, SINS:SINS + 16], st, sti, 'S')
    KS = ['S' + str(i) for i in range(8)] + ['cst', 'tab']
    if not RHO_BCAST:
        V(lambda e: e.tensor_copy(out=rhoT, in_=bcast(cst, MAGS, [[1, 16], [0, 128]])), KS, KS)
    V(lambda e: e.memset(cosT[:, :, 0:1], 1.0), KS, KS)
    V(lambda e: e.memset(sinT[:, :, 0:1], 0.0), KS, KS)
    V(lambda e: e.tensor_copy(out=cosT[:, :, 1:2], in_=cst[:, COSS:COSS + 16].unsqueeze(2)), KS, KS)
    V(lambda e: e.tensor_copy(out=sinT[:, :, 1:2], in_=cst[:, SINS:SINS + 16].unsqueeze(2)), KS, KS)
    t4 = ar.alloc([16, 64])
    t5 = ar.alloc([16, 64])
    cps = cosT.ap[0][0]

    def cmul(oc, osn, ic, isn, mc, ms, a4, a5):
        V(lambda e: e.tensor_tensor(out=a4, in0=ic, in1=mc, op=ALU.mult), KS, KS)
        V(lambda e: e.tensor_tensor(out=a5, in0=isn, in1=ms, op=ALU.mult), KS, KS)
        V(lambda e: e.tensor_tensor(out=oc, in0=a4, in1=a5, op=ALU.subtract), KS, KS)
        V(lambda e: e.tensor_tensor(out=a4, in0=ic, in1=ms, op=ALU.mult), KS, KS)
        V(lambda e: e.tensor_tensor(out=a5, in0=isn, in1=mc, op=ALU.mult), KS, KS)
        V(lambda e: e.tensor_tensor(out=osn, in0=a4, in1=a5, op=ALU.add), KS, KS)

    for n in (2, 4, 8, 16, 32, 64, 128):
        if n < 128:
            oc, osn = cosT[:, :, n:n + 1], sinT[:, :, n:n + 1]
        else:
            oc, osn = cst[:, C128:C128 + 16].unsqueeze(2), cst[:, S128:S128 + 16].unsqueeze(2)
        cmul(oc, osn, cosT[:, :, n - 1:n], sinT[:, :, n - 1:n], cosT[:, :, 1:2], sinT[:, :, 1:2], t4[:, :, 0:1], t5[:, :, 0:1])
        if n < 128:
            m = n - 1
            mc = bass.AP(cosT.tensor, cosT.offset + n, [[cps, 128], [128, 16], [0, m]])
            ms = bass.AP(sinT.tensor, sinT.offset + n, [[cps, 128], [128, 16], [0, m]])
            cmul(cosT[:, :, n + 1:2 * n], sinT[:, :, n + 1:2 * n], cosT[:, :, 1:n], sinT[:, :, 1:n], mc, ms, t4[:, :, 0:m], t5[:, :, 0:m])

    sc.barrier()

    ar.off = base
    NXH = 3
    xh = [ar.alloc([1024]) for _ in range(NXH)]
    jbf = ar.alloc([1024], BF16)
    xnT = ar.alloc([8, 128], BF16)
    pT = ar.alloc([8, 128])
    usb = ar.alloc([4, 128], BF16)
    lt = [ar.alloc([128]) for _ in range(10)]
    ya = ar.alloc([4, 128], BF16)
    yb = ar.alloc([4, 128], BF16)
    s5t = [ar.alloc([128]) for _ in range(6)]
    hreb = ar.alloc([128], BF16)
    himb = ar.alloc([128], BF16)
    gsf = ar.alloc([4, 128])
    gsb = ar.alloc([4, 128], BF16)
    sm = ar.alloc([16])
    xn2bs = [ar.alloc([1024], BF16) for _ in range(2)]
    xn2T = ar.alloc([8, 128], BF16)
    wqb = [ar.alloc([8, 128], BF16) for _ in range(2)]
    qTs = [ar.alloc([4, 128])] * 2
    ssb = [ar.alloc([4, 128])] * 2
    mr = ar.alloc([256])
    v_all = ar.alloc([16, 16])
    ix_all = ar.alloc([16, 16], U32)
    ixf = ar.alloc([16, 16])
    cand = [ar.alloc([256])] * 2
    sc_all = ar.alloc([8, 16])
    pos_all = ar.alloc([8, 16], U32)
    _ixi = ix_all.rearrange("p a b -> p (a b)").bitcast(I32)
    pa_i = _ixi[:, 0:128]
    pb_i = _ixi[:, 128:256]
    paf = ar.alloc([128])
    pbf = ar.alloc([128])
    oh = mr.rearrange("p (a b) -> p a b", a=16)
    i1s = ar.alloc([128])
    i2s = ar.alloc([128])
    ef = ar.alloc([128])
    eis = [ar.alloc([128], I32) for _ in range(2)]
    ex = ar.alloc([8, 16])
    zz = ar.alloc([16])
    gws = [ar.alloc([128]) for _ in range(2)]
    dd = ar.alloc([128])
    coef = ar.alloc([128])
    ND = 4
    diag = [ar.alloc([128], BF16) for _ in range(ND)]
    GS = 4
    gtmp = [ar.alloc([GS]) for _ in range(2)]
    NR = min(16, (ar.cap - ar.off) // 4096) // GS * GS
    assert NR >= 2 * GS, NR
    ring = [ar.alloc([2048], BF16) for _ in range(NR)]

    tpb = psum[:, 0, :].bitcast(BF16)

    def rstd_from(ss_ap, n, out_ap, kr, kw):
        A(lambda e: e.activation(out=out_ap, in_=ss_ap, func=AF.Sqrt, scale=1.0 / n, bias=EPS), kr, kw)
        V(lambda e: e.reciprocal(out=out_ap, in_=out_ap), kw, kw)

    def transpose8(src_bf, dst, ksrc, kdst):
        def f(e):
            ins = None
            for kc in range(8):
                ins = e.transpose(out=tpb[:, kc * 128:(kc + 1) * 128], in_=src_bf[:, kc * 128:(kc + 1) * 128], identity=ident)
            return ins
        T(f, [ksrc, 'ident'], pk(0))
        A(lambda e: e.copy(out=dst.rearrange("p a b -> p (a b)"), in_=tpb), pk(0), [kdst])

    rcount = [0]
    _bc = {}

    def bchk(e):
        if 'r' not in _bc:
            _bc['r'] = e.to_reg(16383)
        return _bc['r']

    RHO_BC = RHO_BCAST

    def front(t):
        first = (t % TPS == 0)
        xb = xh[t % NXH]
        kx = f'xh{t % NXH}'
        p2 = t % 2
        xn2 = xn2bs[p2]
        kxn2 = f'xn2_{p2}'
        ei = eis[p2]
        kei = f'ei{p2}'
        gw = gws[p2]
        kgw = f'gw{p2}'
        if t == 0:
            dma('sync', xb, x_d[0:128, :], [], [kx], 'xl0')
        if t + 1 < NT:
            n1 = (t + 1) % NXH
            dma('sync', xh[n1], x_d[(t + 1) * 128:(t + 2) * 128, :], [], [f'xh{n1}'], f'xl{n1}')
        A(lambda e: e.activation(out=jbf, in_=xb, func=AF.Square, accum_out=sm[:, 0:1]), [kx], ['jbf', 'sm0'])
        rstd_from(sm[:, 0:1], 1024.0, sm[:, 0:1], ['sm0'], ['sm0'])
        V(lambda e: e.scalar_tensor_tensor(out=xn2, in0=xb, scalar=sm[:, 0:1], in1=gbc[:, 0, :], op0=ALU.mult, op1=ALU.mult),
          [kx, 'sm0', 'gbc'], [kxn2])
        transpose8(xn2, xnT, kxn2, 'xnT')
        if first:
            V(lambda e: e.memset(xah[:, :, 0:3], 0.0), [], ['xah'])
        else:
            A(lambda e: e.copy(out=xah[:, :, 0:3], in_=xah[:, :, 128:131]), ['xah'], ['xah'])
        for gi in range(3):
            b = 1 + (gi % 2)

            def f(e, gi=gi, b=b):
                ins = None
                for j in range(4):
                    m = gi * 4 + j
                    for kc in range(8):
                        ins = e.matmul(psum[:, b, j * 128:(j + 1) * 128], lhsT=w_in[:, kc, m * 128:(m + 1) * 128],
                                       rhs=xnT[:, kc, :], start=(kc == 0), stop=(kc == 7))
                return ins
            T(f, ['w_in', 'xnT'], pk(b))
            if gi == 0:
                A(lambda e, b=b: e.copy(out=xah[:, :, 3:131], in_=psum[:, b, :].rearrange("p (a b) -> p a b", a=4)), pk(b), ['xah'])
            elif gi == 1:
                A(lambda e, b=b: e.copy(out=pT[:, 0:4, :], in_=psum[:, b, :].rearrange("p (a b) -> p a b", a=4)), pk(b), ['gate'])
            else:
                A(lambda e, b=b: e.copy(out=pT[:, 4:8, :], in_=psum[:, b, :].rearrange("p (a b) -> p a b", a=4)), pk(b), ['usf'])
                A(lambda e: e.copy(out=usb, in_=pT[:, 4:8, :]), ['usf'], ['usb'])
        xc, rr, ig, aa, a2, uu, hh_, gg, yar, sq = lt
        for c in range(4):
            cw = lambda k, c=c: vecs[:, CW + c * 4 + k:CW + c * 4 + k + 1]
            V(lambda e, c=c, cw=cw: e.tensor_scalar(out=xc, in0=xah[:, c, 0:128], scalar1=cw(0), scalar2=vecs[:, CB + c:CB + c + 1],
                                                    op0=ALU.mult, op1=ALU.add), ['xah', 'vecs'], ['xc'])
            for k in range(1, 4):
                V(lambda e, c=c, k=k, cw=cw: e.scalar_tensor_tensor(out=xc, in0=xah[:, c, k:k + 128], scalar=cw(k), in1=xc,
                                                                    op0=ALU.mult, op1=ALU.add), ['xah', 'vecs', 'xc'], ['xc'])

            def f(e, c=c):
                e.matmul(psum[:, 2, 0:128], lhsT=wax[:, c, :], rhs=xc, start=True, stop=True)
                return e.matmul(psum[:, 2, 128:256], lhsT=wax[:, 4 + c, :], rhs=xc, start=True, stop=True)
            T(f, ['wax', 'xc'], pk(2, 0) + pk(2, 1))
            A(lambda e, c=c: e.activation(out=rr, in_=psum[:, 2, 0:128], func=AF.Sigmoid, bias=vecs[:, BA + c:BA + c + 1]), pk(2, 0) + ['vecs'], ['rr'])
            A(lambda e, c=c: e.activation(out=ig, in_=psum[:, 2, 128:256], func=AF.Sigmoid, bias=vecs[:, BX + c:BX + c + 1]), pk(2, 1) + ['vecs'], ['ig'])
            A(lambda e, c=c: e.activation(out=aa, in_=rr, func=AF.Exp, scale=cst[:, SC8 + c:SC8 + c + 1]), ['rr'], ['aa'])
            A(lambda e, c=c: e.activation(out=a2, in_=rr, func=AF.Exp, scale=cst[:, SC16 + c:SC16 + c + 1]), ['rr'], ['a2'])
            A(lambda e: e.activation(out=a2, in_=a2, func=AF.Sqrt, scale=-1.0, bias=1.0), ['a2'], ['a2'])
            V(lambda e: e.tensor_tensor(out=uu, in0=ig, in1=xc, op=ALU.mult), ['ig', 'xc'], ['uu'])
            V(lambda e: e.tensor_tensor(out=uu, in0=uu, in1=a2, op=ALU.mult), ['uu', 'a2'], ['uu'])
            init = 0.0 if first else hl[:, c:c + 1]
            V(lambda e, init=init: e.tensor_tensor_scan(out=hh_, data0=aa, data1=uu, initial=init, op0=ALU.mult, op1=ALU.add),
              ['aa', 'uu', f'hl{c}'], ['hh'])
            A(lambda e, c=c: e.copy(out=hl[:, c:c + 1], in_=hh_[:, 127:128]), ['hh'], [f'hl{c}'])
            A(lambda e, c=c: e.activation(out=gg, in_=pT[:, c, :], func=AF.Gelu_apprx_tanh), ['gate'], ['gg'])
            V(lambda e: e.tensor_tensor(out=yar, in0=hh_, in1=gg, op=ALU.mult), ['hh', 'gg'], ['yar'])
            A(lambda e: e.activation(out=sq, in_=yar, func=AF.Square), ['yar'], ['sq'])
            T(lambda e, c=c: e.matmul(psum[:, 3, 256:257], lhsT=sq, rhs=ones[:, 0:1], start=(c == 0), stop=(c == 3)), ['sq', 'ones'], pk(3, 2))
            A(lambda e, c=c: e.activation(out=ya[:, c, :], in_=yar, func=AF.Copy, scale=vecs[:, GNL + c:GNL + c + 1]), ['yar', 'vecs'], ['ya'])
        t1, t2, wre, wim, gre, gim = s5t
        for cc in range(4):
            for pq in range(4):
                q = cc * 4 + pq
                bq = 4 + (q % 2)

                def f(e, q=q, cc=cc, bq=bq):
                    e.matmul(psum[:, bq, 0:128], lhsT=bbR[:, q, :], rhs=usb[:, cc, :], start=True, stop=True)
                    return e.matmul(psum[:, bq, 128:256], lhsT=bbI[:, q, :], rhs=usb[:, cc, :], start=True, stop=True)
                T(f, ['usb'], pk(bq, 0) + pk(bq, 1))
                buR = psum[:, bq, 0:128]
                buI = psum[:, bq, 128:256]
                kb = pk(bq, 0) + pk(bq, 1)
                cq = cosT[:, q, :]
                sq_ = sinT[:, q, :]
                rho = bcast(cst, MAGS + q, [[0, 128]]) if RHO_BC else rhoT[:, q, :]
                V(lambda e, cq=cq, buR=buR: e.tensor_tensor(out=t1, in0=buR, in1=cq, op=ALU.mult), kb, ['t1'])
                V(lambda e, sq_=sq_, buI=buI: e.tensor_tensor(out=t2, in0=buI, in1=sq_, op=ALU.mult), kb, ['t2'])
                V(lambda e: e.tensor_tensor(out=wre, in0=t1, in1=t2, op=ALU.add), ['t1', 't2'], ['wre'])
                V(lambda e, cq=cq, buI=buI: e.tensor_tensor(out=t1, in0=buI, in1=cq, op=ALU.mult), kb + ['t1'], ['t1'])
                V(lambda e, sq_=sq_, buR=buR: e.tensor_tensor(out=t2, in0=buR, in1=sq_, op=ALU.mult), kb + ['t2'], ['t2'])
                V(lambda e: e.tensor_tensor(out=wim, in0=t1, in1=t2, op=ALU.subtract), ['t1', 't2'], ['wim'])
                i_r = 0.0 if first else ire[:, q:q + 1]
                i_i = 0.0 if first else iim[:, q:q + 1]
                V(lambda e, rho=rho, i_r=i_r: e.tensor_tensor_scan(out=gre, data0=rho, data1=wre, initial=i_r, op0=ALU.mult, op1=ALU.add),
                  ['wre', f'ic{q}'], ['gre'])
                V(lambda e, rho=rho, i_i=i_i: e.tensor_tensor_scan(out=gim, data0=rho, data1=wim, initial=i_i, op0=ALU.mult, op1=ALU.add),
                  ['wim', f'ic{q}'], ['gim'])
                c128 = cst[:, C128 + q:C128 + q + 1]
                s128 = cst[:, S128 + q:S128 + q + 1]
                V(lambda e, s128=s128: e.tensor_scalar(out=sm[:, 8:9], in0=gim[:, 127:128], scalar1=s128, scalar2=None, op0=ALU.mult), ['gim'], ['sm8'])
                V(lambda e, q=q, c128=c128: e.scalar_tensor_tensor(out=ire[:, q:q + 1], in0=gre[:, 127:128], scalar=c128, in1=sm[:, 8:9],
                                                                   op0=ALU.mult, op1=ALU.subtract), ['gre', 'sm8', f'ic{q}'], [f'ic{q}'])
                V(lambda e, s128=s128: e.tensor_scalar(out=sm[:, 9:10], in0=gre[:, 127:128], scalar1=s128, scalar2=None, op0=ALU.mult), ['gre'], ['sm9'])
                V(lambda e, q=q, c128=c128: e.scalar_tensor_tensor(out=iim[:, q:q + 1], in0=gim[:, 127:128], scalar=c128, in1=sm[:, 9:10],
                                                                   op0=ALU.mult, op1=ALU.add), ['gim', 'sm9', f'ic{q}'], [f'ic{q}'])
                V(lambda e, cq=cq: e.tensor_tensor(out=t1, in0=gre, in1=cq, op=ALU.mult), ['gre', 't1'], ['t1'])
                V(lambda e, sq_=sq_: e.tensor_tensor(out=t2, in0=gim, in1=sq_, op=ALU.mult), ['gim', 't2'], ['t2'])
                V(lambda e: e.tensor_tensor(out=hreb, in0=t1, in1=t2, op=ALU.subtract), ['t1', 't2'], ['hreb'])
                V(lambda e, cq=cq: e.tensor_tensor(out=t1, in0=gim, in1=cq, op=ALU.mult), ['gim', 't1'], ['t1'])
                V(lambda e, sq_=sq_: e.tensor_tensor(out=t2, in0=gre, in1=sq_, op=ALU.mult), ['gre', 't2'], ['t2'])
                V(lambda e: e.tensor_tensor(out=himb, in0=t1, in1=t2, op=ALU.add), ['t1', 't2'], ['himb'])

                def f2(e, q=q, pq=pq):
                    e.matmul(psum[:, 1, 0:128], lhsT=cR[:, q, :], rhs=hreb, start=(pq == 0), stop=False)
                    return e.matmul(psum[:, 1, 0:128], lhsT=cIn[:, q, :], rhs=himb, start=False, stop=(pq == 3))
                T(f2, ['hreb', 'himb'], pk(1, 0))
            V(lambda e, cc=cc: e.scalar_tensor_tensor(out=t1, in0=pT[:, 4 + cc, :], scalar=vecs[:, SD + cc:SD + cc + 1], in1=psum[:, 1, 0:128],
                                                      op0=ALU.mult, op1=ALU.add), ['usf', 'vecs', 't1'] + pk(1, 0), ['t1'])
            A(lambda e, cc=cc: e.activation(out=gsf[:, cc, :], in_=t1, func=AF.Gelu_apprx_tanh), ['t1'], ['gsf'])
            A(lambda e, cc=cc: e.copy(out=gsb[:, cc, :], in_=gsf[:, cc, :]), ['gsf'], ['gsb'])
        for m in range(4):
            def f(e, m=m):
                ins = None
                for k in range(4):
                    ins = e.matmul(psum[:, 2, 0:128], lhsT=glu[:, k, m * 128:(m + 1) * 128], rhs=gsb[:, k, :], start=(k == 0), stop=(k == 3))
                return ins
            T(f, ['gsb'], pk(2, 0))
            A(lambda e, m=m: e.activation(out=rr, in_=psum[:, 2, 0:128], func=AF.Sigmoid, bias=vecs[:, GB + m:GB + m + 1]), pk(2, 0) + ['vecs'], ['rr'])
            V(lambda e, m=m: e.tensor_tensor(out=yar, in0=gsf[:, m, :], in1=rr, op=ALU.mult), ['gsf', 'rr'], ['yar'])
            A(lambda e: e.activation(out=sq, in_=yar, func=AF.Square), ['yar'], ['sq'])
            T(lambda e, m=m: e.matmul(psum[:, 3, 257:258], lhsT=sq, rhs=ones[:, 0:1], start=(m == 0), stop=(m == 3)), ['sq', 'ones'], pk(3, 2))
            A(lambda e, m=m: e.activation(out=yb[:, m, :], in_=yar, func=AF.Copy, scale=vecs[:, GNS + m:GNS + m + 1]), ['yar', 'vecs'], ['yb'])
        rstd_from(psum[:, 3, 256:258], 512.0, sm[:, 1:3], pk(3, 2), ['sm12'])
        for half in range(2):
            def f(e, half=half):
                ins = None
                for c in range(4):
                    ins = e.matmul(psum[:, 4 + half, :], lhsT=ya[:, c, :], rhs=w_out[:, c, half * 512:(half + 1) * 512], start=(c == 0), stop=(c == 3))
                return ins
            T(f, ['ya', 'w_out'], pk(4 + half))

            def f(e, half=half):
                ins = None
                for c in range(4):
                    ins = e.matmul(psum[:, 1 + half, :], lhsT=yb[:, c, :], rhs=w_out[:, 4 + c, half * 512:(half + 1) * 512], start=(c == 0), stop=(c == 3))
                return ins
            T(f, ['yb', 'w_out'], pk(1 + half))
            xs = xb[:, half * 512:(half + 1) * 512]
            V(lambda e, half=half, xs=xs: e.scalar_tensor_tensor(out=xs, in0=psum[:, 4 + half, :], scalar=sm[:, 1:2], in1=xs, op0=ALU.mult, op1=ALU.add),
              pk(4 + half) + ['sm12', kx], [kx])
            V(lambda e, half=half, xs=xs: e.scalar_tensor_tensor(out=xs, in0=psum[:, 1 + half, :], scalar=sm[:, 2:3], in1=xs, op0=ALU.mult, op1=ALU.add),
              pk(1 + half) + ['sm12', kx], [kx])
        A(lambda e: e.activation(out=jbf, in_=xb, func=AF.Square, accum_out=sm[:, 3:4]), [kx], ['jbf', 'sm3'])
        rstd_from(sm[:, 3:4], 1024.0, sm[:, 3:4], ['sm3'], ['sm3'])
        V(lambda e: e.scalar_tensor_tensor(out=xn2, in0=xb, scalar=sm[:, 3:4], in1=gbc[:, 1, :], op0=ALU.mult, op1=ALU.mult),
          [kx, 'sm3', 'gbc'], [kxn2])
        transpose8(xn2, xn2T, kxn2, 'xn2T')
        for grp in range(4):
            bqT = 4 + (grp % 2)
            bsc = 1 + (grp % 2)
            for jj in range(4):
                j = grp * 4 + jj
                wslot = j % 2
                wb = wqb[wslot]
                GD(lambda e, wb=wb, j=j: e.dma_start(out=wb, in_=wq_d.rearrange("(kc p) n -> p kc n", p=128)[:, :, j * 128:(j + 1) * 128]),
                   [], [f'wq{wslot}'], f'wq{wslot}')

                def f(e, wb=wb, jj=jj, bqT=bqT):
                    ins = None
                    for kc in range(8):
                        ins = e.matmul(psum[:, bqT, jj * 128:(jj + 1) * 128], lhsT=wb[:, kc, :], rhs=xn2T[:, kc, :], start=(kc == 0), stop=(kc == 7))
                    return ins
                T(f, [f'wq{wslot}', 'xn2T'], pk(bqT, jj))
            qs = qTs[grp % 2]
            kq = 'qTs0'
            A(lambda e, qs=qs, bqT=bqT: e.copy(out=qs.rearrange("p a b -> p (a b)"), in_=psum[:, bqT, :]), pk(bqT), [kq])

            def f(e, qs=qs, bsc=bsc):
                ins = None
                for jj in range(4):
                    ins = e.matmul(psum[:, bsc, jj * 128:(jj + 1) * 128], lhsT=qs[:, jj, :], rhs=kT[:, jj % 2, :], start=True, stop=True)
                return ins
            T(f, [kq, 'kT'], pk(bsc))
            sb = ssb[grp % 2]
            ks = 'ssb0'
            A(lambda e, sb=sb, bsc=bsc: e.copy(out=sb.rearrange("p a b -> p (a b)"), in_=psum[:, bsc, :]), pk(bsc), [ks])
            for jj in range(4):
                j = grp * 4 + jj
                s_j = sb[:, jj, :]
                V(lambda e, j=j, s_j=s_j: e.max(out=v_all[:, j, 0:8], in_=s_j), [ks], ['v_all'])
                V(lambda e, j=j, s_j=s_j: e.match_replace(out=mr[:, 0:128], in_to_replace=v_all[:, j, 0:8], in_values=s_j, imm_value=-1e30), [ks, 'v_all'], ['mr'])
                V(lambda e, j=j: e.max(out=v_all[:, j, 8:16], in_=mr[:, 0:128]), ['mr'], ['v_all'])
                V(lambda e, j=j, s_j=s_j: e.max_index(out=ix_all[:, j, 0:8], in_max=v_all[:, j, 0:8], in_values=s_j), [ks, 'v_all'], ['ix_all'])
                V(lambda e, j=j, s_j=s_j: e.max_index(out=ix_all[:, j, 8:16], in_max=v_all[:, j, 8:16], in_values=s_j), [ks, 'v_all'], ['ix_all'])
        V(lambda e: e.tensor_copy(out=ixf, in_=ix_all), ['ix_all'], ['ixf'])
        vps = v_all.ap[0][0]
        for h in range(8):
            cd = cand[h % 2]
            kc_ = 'cand0'
            in0 = bass.AP(v_all.tensor, v_all.offset + (2 * h) * 16, [[vps, 128], [1, 16], [0, 16]])
            in1 = bass.AP(v_all.tensor, v_all.offset + (2 * h + 1) * 16, [[vps, 128], [0, 16], [1, 16]])
            V(lambda e, cd=cd, in0=in0, in1=in1: e.tensor_tensor(out=cd.rearrange("p (a b) -> p a b", a=16), in0=in0, in1=in1, op=ALU.add), ['v_all'], [kc_])
            V(lambda e, cd=cd, h=h: e.max(out=sc_all[:, h, 0:8], in_=cd), [kc_], ['sc_all'])
            V(lambda e, cd=cd, h=h: e.match_replace(out=mr, in_to_replace=sc_all[:, h, 0:8], in_values=cd, imm_value=-1e30), [kc_, 'sc_all'], ['mr'])
            V(lambda e, h=h: e.max(out=sc_all[:, h, 8:16], in_=mr), ['mr'], ['sc_all'])
            V(lambda e, cd=cd, h=h: e.max_index(out=pos_all[:, h, 0:8], in_max=sc_all[:, h, 0:8], in_values=cd), [kc_, 'sc_all'], ['pos_all'])
            V(lambda e, cd=cd, h=h: e.max_index(out=pos_all[:, h, 8:16], in_max=sc_all[:, h, 8:16], in_values=cd), [kc_, 'sc_all'], ['pos_all'])
        pos2 = pos_all.rearrange("p a b -> p (a b)").bitcast(I32)
        V(lambda e: e.tensor_single_scalar(out=pa_i, in_=pos2, scalar=4, op=ALU.arith_shift_right), ['pos_all', 'ixf'], ['ix_all'])
        V(lambda e: e.tensor_single_scalar(out=pb_i, in_=pos2, scalar=15, op=ALU.bitwise_and), ['pos_all', 'ixf'], ['ix_all'])
        V(lambda e: e.tensor_copy(out=paf, in_=pa_i), ['ix_all'], ['paf'])
        V(lambda e: e.tensor_copy(out=pbf, in_=pb_i), ['ix_all'], ['pbf'])
        iot = bcast(vecs, IOTA, [[0, 16], [1, 16]])
        ixf2 = ixf.rearrange("p a b -> p (a b)")
        for h in range(8):
            for (pf, kpf, off, dst, kd) in ((paf, 'paf', 0, i1s, 'i1s'), (pbf, 'pbf', 16, i2s, 'i2s')):
                pfb = bcast(pf, h * 16, [[1, 16], [0, 16]])
                ixb = bcast(ixf2, h * 32 + off, [[0, 16], [1, 16]])
                V(lambda e, pfb=pfb: e.tensor_tensor(out=oh, in0=pfb, in1=iot, op=ALU.is_equal), [kpf, 'vecs', 'mr'], ['mr'])
                V(lambda e, ixb=ixb: e.tensor_tensor(out=oh, in0=oh, in1=ixb, op=ALU.mult), ['mr', 'ixf'], ['mr'])
                V(lambda e, dst=dst, h=h: e.tensor_reduce(out=dst[:, h * 16:(h + 1) * 16], in_=oh, axis=AX.X, op=ALU.add), ['mr', kd], [kd])
        V(lambda e: e.scalar_tensor_tensor(out=ef, in0=i1s, scalar=128.0, in1=i2s, op0=ALU.mult, op1=ALU.add), ['i1s', 'i2s'], ['ef'])
        V(lambda e: e.tensor_copy(out=ei, in_=ef), ['ef'], [kei])
        sps = sc_all.ap[0][0]
        mxb = bass.AP(sc_all.tensor, sc_all.offset, [[sps, 128], [16, 8], [0, 16]])
        V(lambda e: e.tensor_tensor(out=ex, in0=sc_all, in1=mxb, op=ALU.subtract), ['sc_all'], ['ex'])
        A(lambda e: e.activation(out=ex, in_=ex, func=AF.Exp), ['ex'], ['ex'])
        V(lambda e: e.tensor_reduce(out=zz[:, 0:8], in_=ex, axis=AX.X, op=ALU.add), ['ex'], ['zz'])
        V(lambda e: e.reciprocal(out=zz[:, 0:8], in_=zz[:, 0:8]), ['zz'], ['zz'])
        zb = bcast(zz, 0, [[1, 8], [0, 16]])
        V(lambda e: e.tensor_tensor(out=gw.rearrange("p (a b) -> p a b", a=8), in0=ex, in1=zb, op=ALU.mult), ['ex', 'zz'], [kgw])

    def back(t):
        xb = xh[t % NXH]
        kx = f'xh{t % NXH}'
        p2 = t % 2
        xn2 = xn2bs[p2]
        kxn2 = f'xn2_{p2}'
        ei = eis[p2]
        kei = f'ei{p2}'
        gw = gws[p2]
        kgw = f'gw{p2}'
        ddk = [f'dd{n}' for n in range(128)]
        V(lambda e: e.memset(dd, 0.0), [], ddk)
        NG = 128 // GS
        slots = {}

        def gathers(g):
            for n in range(g * GS, (g + 1) * GS):
                rs = rcount[0] % NR
                rcount[0] += 1
                slots[n] = rs
                rb = ring[rs]
                GD(lambda e, rb=rb, n=n: e.indirect_dma_start(out=rb, out_offset=None, in_=uvb_d,
                                                              in_offset=bass.IndirectOffsetOnAxis(ap=ei[:, n:n + 1], axis=0), bounds_check=bchk(e), oob_is_err=False),
                   [kei], [f'ring{rs}'], f'rg{rs}')

        LOOK = NR // GS - 1
        for g0 in range(min(LOOK + 1, NG)):
            gathers(g0)

        def tail2_and_mm(g):
            cs = slice(g * GS, (g + 1) * GS)
            dk = [f'dd{n}' for n in range(g * GS, (g + 1) * GS)]
            kt = f'gt{g % 2}'
            tg = gtmp[g % 2]
            d_ = dd[:, cs]
            V(lambda e, tg=tg, d_=d_: e.scalar_tensor_tensor(out=tg, in0=tg, scalar=1.0, in1=d_, op0=ALU.add, op1=ALU.mult), [kt] + dk, [kt])
            V(lambda e, tg=tg, cs=cs: e.scalar_tensor_tensor(out=coef[:, cs], in0=tg, scalar=0.5, in1=gw[:, cs], op0=ALU.mult, op1=ALU.mult),
              [kt, kgw], [f'coef{g % 2}'])
            for n in range(g * GS, (g + 1) * GS):
                rs = slots[n]
                rb = ring[rs]
                dg = diag[n % ND]
                kdg = f'diag{n % ND}'
                A(lambda e, dg=dg, n=n: e.activation(out=dg, in_=identf, func=AF.Copy, scale=coef[:, n:n + 1]), [f'coef{g % 2}'], [kdg])

                def f(e, dg=dg, rb=rb, n=n):
                    e.matmul(psum[:, 6, :], lhsT=dg, rhs=rb[:, 1024:1536], start=(n == 0), stop=(n == 127))
                    return e.matmul(psum[:, 7, :], lhsT=dg, rhs=rb[:, 1536:2048], start=(n == 0), stop=(n == 127))
                T(f, [kdg, f'ring{rs}'], pk(6) + pk(7))

        for g in range(NG):
            cs = slice(g * GS, (g + 1) * GS)
            dk = [f'dd{n}' for n in range(g * GS, (g + 1) * GS)]
            for n in range(g * GS, (g + 1) * GS):
                rs = slots[n]
                rb = ring[rs]
                V(lambda e, rb=rb, n=n: e.scalar_tensor_tensor(out=rb[:, 0:1024], in0=rb[:, 0:1024], scalar=1.0, in1=xn2, op0=ALU.mult, op1=ALU.mult,
                                                               accum_out=dd[:, n:n + 1]),
                  [f'ring{rs}', kxn2, f'dd{n}'], [f'dd{n}', f'ring{rs}'])
            kt = f'gt{g % 2}'
            tg = gtmp[g % 2]
            d_ = dd[:, cs]
            V(lambda e, tg=tg, d_=d_: e.scalar_tensor_tensor(out=tg, in0=d_, scalar=0.044715, in1=d_, op0=ALU.mult, op1=ALU.mult), dk, [kt])
            V(lambda e, tg=tg, d_=d_: e.scalar_tensor_tensor(out=tg, in0=tg, scalar=1.0, in1=d_, op0=ALU.add, op1=ALU.mult), [kt] + dk, [kt])
            A(lambda e, tg=tg: e.activation(out=tg, in_=tg, func=AF.Tanh, scale=0.7978845608028654), [kt], [kt])
            if g >= 1:
                tail2_and_mm(g - 1)
                if g - 1 + LOOK + 1 < NG:
                    gathers(g - 1 + LOOK + 1)
        tail2_and_mm(NG - 1)
        V(lambda e: e.tensor_tensor(out=xb[:, 0:512], in0=psum[:, 6, :], in1=xb[:, 0:512], op=ALU.add), pk(6) + [kx], [kx])
        V(lambda e: e.tensor_tensor(out=xb[:, 512:1024], in0=psum[:, 7, :], in1=xb[:, 512:1024], op=ALU.add), pk(7) + [kx], [kx])
        A(lambda e: e.activation(out=jbf, in_=xb, func=AF.Square, accum_out=sm[:, 4:5]), [kx], ['jbf', 'sm4'])
        rstd_from(sm[:, 4:5], 1024.0, sm[:, 4:5], ['sm4'], ['sm4'])
        V(lambda e: e.scalar_tensor_tensor(out=xb, in0=xb, scalar=sm[:, 4:5], in1=gbc[:, 2, :], op0=ALU.mult, op1=ALU.mult),
          [kx, 'sm4', 'gbc'], [kx])
        dma('sync', out_d[t * 128:(t + 1) * 128, :], xb, [kx], [], 'ost')

    def record(fn, t):
        lst = []
        cur[0] = lst
        fn(t)
        cur[0] = None
        return lst

    def run_merged(F, B):
        nf, nb = len(F), len(B)
        i = j = 0
        while i < nf or j < nb:
            if j >= nb or (i < nf and i * nb <= j * nf):
                F[i]()
                i += 1
            else:
                B[j]()
                j += 1

    if PIPE:
        run_merged(record(front, 0), [])
        for t in range(1, NT):
            Fl = record(front, t)
            Bl = record(back, t - 1)
            run_merged(Fl, Bl)
        run_merged([], record(back, NT - 1))
    else:
        for t in range(NT):
            run_merged(record(front, t), [])
            run_merged(record(back, t), [])
    sc.add('sync', lambda e: e.nop(), [], [], extra=list(sc.last_dma.values()))
    sc.emit(es)
    es.close()
    return nc


def prep_shared(inp):
    f = lambda a: np.ascontiguousarray(np.asarray(a, dtype=np.float32))
    col = lambda v, n: f(v).reshape(n, 128).T
    vecs = np.zeros((128, NV), np.float32)
    cw = f(inp['conv_w'][0])
    for c in range(4):
        for k in range(4):
            vecs[:, c * 4 + k] = cw[k, c * 128:(c + 1) * 128]
    vecs[:, 16:20] = col(inp['conv_b'][0], 4)
    vecs[:, 20:24] = col(inp['lru_ba'][0], 4)
    vecs[:, 24:28] = col(inp['lru_bx'][0], 4)
    vecs[:, 28:32] = col(inp['lru_lambda'][0], 4)
    vecs[:, 32:36] = col(inp['s5_d'][0], 4)
    vecs[:, 36:40] = col(inp['s5_glu_b'][0], 4)
    vecs[:, 40:44] = col(inp['gnorm_lru'][0], 4)
    vecs[:, 44:48] = col(inp['gnorm_s5'][0], 4)
    lr = f(inp['s5_lambda_re'][0])
    li = f(inp['s5_lambda_im'][0])
    ld = f(inp['s5_log_dt'][0])
    lrs = lr.reshape(16, 128).T
    lis = li.reshape(16, 128).T
    lds = np.repeat(ld.reshape(16, 2, 1), 64, axis=2).reshape(16, 128).T
    vecs[:, 48:64] = lrs
    vecs[:, 64:80] = lis
    vecs[:, 80:96] = lds
    vecs[:, 96:112] = np.arange(16, dtype=np.float32)[None, :]
    wax = np.zeros((128, 8, 128), np.float32)
    wa = f(inp['lru_wa'][0])
    wx = f(inp['lru_wx'][0])
    for h in range(8):
        c, o = h // 2, (h % 2) * 64
        wax[o:o + 64, c, o:o + 64] = wa[h]
        wax[o:o + 64, 4 + c, o:o + 64] = wx[h]
    kT = np.stack([f(inp['peer_k1'][0]).T, f(inp['peer_k2'][0]).T], axis=1)
    gbc = np.stack([np.broadcast_to(f(inp['norm1_g'][0]), (128, 1024)),
                    np.broadcast_to(f(inp['norm2_g'][0]), (128, 1024)),
                    np.broadcast_to(f(inp['normf_g']), (128, 1024))], axis=1)
    bR = f(inp['s5_b_re'][0])
    bI = f(inp['s5_b_im'][0])
    cRe = f(inp['s5_c_re'][0])
    cIm = f(inp['s5_c_im'][0])
    s5l = np.zeros((128, 5, 16, 128), np.float32)
    s5c = np.zeros((128, 2, 16, 128), np.float32)
    for g in range(32):
        q, r0, c0 = g // 2, (g % 8) * 16, (g % 2) * 64
        s5l[r0:r0 + 16, 0, q, c0:c0 + 64] = bR[g].T
        s5l[r0:r0 + 16, 1, q, c0:c0 + 64] = bI[g].T
        s5c[c0:c0 + 64, 0, q, r0:r0 + 16] = cRe[g].T
        s5c[c0:c0 + 64, 1, q, r0:r0 + 16] = cIm[g].T
    s5l[:, 2] = lrs.T[None, :, :]
    s5l[:, 3] = lis.T[None, :, :]
    s5l[:, 4] = lds.T[None, :, :]
    return dict(
        w_in=f(inp['w_in'][0]), w_out=f(inp['w_out'][0]), glu_w=f(inp['s5_glu_w'][0]), wq=f(inp['peer_wq'][0]),
        peer_u=f(inp['peer_u'][0]), peer_v=f(inp['peer_v'][0]), kT=f(kT), vecs=vecs, wax=wax, gbc=f(gbc),
        s5l=f(s5l.reshape(128, 5, 2048)), s5c=f(s5c.reshape(128, 2, 2048)))


_NC_CACHE = {}


def run(inputs, NT, ncores=NCORES, stage=2):
    x = np.asarray(inputs['x'], dtype=np.float32)
    shared = prep_shared(inputs)
    if (NT, stage) not in _NC_CACHE:
        _NC_CACHE[(NT, stage)] = build(NT, stage)
    nc = _NC_CACHE[(NT, stage)]
    in_maps = []
    for c in range(ncores):
        xc = x[4 * c:4 * c + 4].reshape(-1, 1024)[:NT * 128]
        m = dict(shared)
        m['x'] = np.ascontiguousarray(xc)
        in_maps.append(m)
    res = run_bass_kernel_spmd(nc, in_maps, core_ids=list(range(ncores)))
    return [r['out'] for r in res.results]


def kernel(**inputs):
    outs = run(inputs, 4 * TPS)
    out = np.stack([o.reshape(4, 2048, 1024) for o in outs], axis=0).reshape(32, 2048, 1024)
    return out.astype(np.float32)
```

```python
import math
from contextlib import ExitStack

import numpy as np
import concourse.bass as bass
import concourse.mybir as mybir
from concourse.bass_utils import run_bass_kernel_spmd

F32 = mybir.dt.float32
BF16 = mybir.dt.bfloat16
I32 = mybir.dt.int32
U32 = mybir.dt.uint32
ALU = mybir.AluOpType
AF = mybir.ActivationFunctionType
AX = mybir.AxisListType

ENG = ('tensor', 'vector', 'scalar', 'gpsimd', 'sync')
CAP = 30000
NCORES = 8
TPS = 16
EPS = 1e-6
TWO_PI = 6.283185
NV = 112
RHO_BCAST = True


class Op:
    __slots__ = ('eng', 'fn', 'deps', 'sig', 'is_dma', 'chan', 'chan_val', 'signo')


class Sched:
    def __init__(self, nc):
        self.nc = nc
        self.ops = {e: [] for e in ENG}
        self.lastw = {}
        self.readers = {}
        self.chan_cnt = {}
        self.last_op = {}
        self.last_dma = {}

    def add(self, eng, fn, reads=(), writes=(), chan=None, extra=()):
        op = Op()
        op.eng = eng
        op.fn = fn
        op.is_dma = chan is not None
        op.chan = chan
        op.sig = False
        op.signo = 0
        op.chan_val = 0
        deps = {}
        for r in reads:
            w = self.lastw.get(r)
            if w is not None:
                deps[id(w)] = w
        for k in writes:
            w = self.lastw.get(k)
            if w is not None:
                deps[id(w)] = w
            for d in self.readers.get(k, ()):
                deps[id(d)] = d
        for d in extra:
            deps[id(d)] = d
        op.deps = []
        for d in deps.values():
            if (not d.is_dma) and d.eng == 'tensor' and eng == 'tensor':
                continue
            op.deps.append(d)
            if not d.is_dma:
                d.sig = True
        if op.is_dma:
            c = self.chan_cnt.get(chan, 0) + 1
            self.chan_cnt[chan] = c
            op.chan_val = 16 * c
            self.last_dma[chan] = op
        else:
            self.last_op[eng] = op
        for r in reads:
            self.readers.setdefault(r, []).append(op)
        for k in writes:
            self.lastw[k] = op
            self.readers[k] = []
        self.ops[eng].append(op)
        return op

    def barrier(self):
        prev = list(self.last_op.values()) + list(self.last_dma.values())
        for e in ENG:
            self.add(e, lambda eo: eo.nop(), extra=prev)
        self.lastw = {}
        self.readers = {}

    def emit(self, es):
        nc = self.nc
        esem = {}
        for e in ENG:
            n = 0
            for op in self.ops[e]:
                if op.sig and not op.is_dma:
                    n += 1
                    op.signo = n
            nep = max(1, (n + CAP - 1) // CAP)
            esem[e] = [es.enter_context(nc.semaphore(f"s_{e}_{i}")) for i in range(nep)]
        csem = {c: es.enter_context(nc.semaphore(f"c_{c}")) for c in self.chan_cnt}
        block = es.enter_context(nc.Block())
        for e in ENG:
            ops = self.ops[e]

            def body(eo, e=e, ops=ops):
                waited = {}
                maxep = {}
                for op in ops:
                    need = {}
                    for d in op.deps:
                        if d.is_dma:
                            key = ('c', d.chan)
                            val = d.chan_val
                        else:
                            ep = (d.signo - 1) // CAP
                            if maxep.get(d.eng, -1) > ep:
                                continue
                            key = ('e', d.eng, ep)
                            val = (d.signo - 1) % CAP + 1
                        if need.get(key, 0) < val:
                            need[key] = val
                    for key, val in need.items():
                        if waited.get(key, 0) >= val:
                            continue
                        if key[0] == 'c':
                            eo.wait_ge(csem[key[1]], val)
                        else:
                            eo.wait_ge(esem[key[1]][key[2]], val)
                            if maxep.get(key[1], -1) < key[2]:
                                maxep[key[1]] = key[2]
                        waited[key] = val
                    ins = op.fn(eo)
                    if op.is_dma:
                        ins.then_inc(csem[op.chan], 16)
                    elif op.sig:
                        ins.then_inc(esem[e][(op.signo - 1) // CAP], 1)

            getattr(block, e)(body)


class Arena:
    def __init__(self, nc, es, nbytes):
        self.t = es.enter_context(nc.sbuf_tensor("arena", [128, nbytes // 4], F32))
        self.off = 0
        self.cap = nbytes

    def alloc(self, shape, dtype=F32):
        esz = {F32: 4, I32: 4, U32: 4, BF16: 2}[dtype]
        n = int(np.prod(shape))
        nb = (n * esz + 31) // 32 * 32
        o = self.off
        self.off += nb
        assert self.off <= self.cap, (self.off, self.cap)
        ap = self.t[:, o // 4:(o + nb) // 4]
        if dtype != F32:
            ap = ap.bitcast(dtype)
        ap = ap[:, 0:n]
        if len(shape) == 2:
            ap = ap.rearrange("p (a b) -> p a b", a=shape[0])
        elif len(shape) == 3:
            ap = ap.rearrange("p (a b c) -> p a b c", a=shape[0], b=shape[1])
        return ap


def bcast(ap2d_src, off, dims):
    base = ap2d_src
    pstep = base.ap[0][0]
    return bass.AP(base.tensor, base.offset + off, [[pstep, 128]] + [list(d) for d in dims])


def build(NT, stage=2, PIPE=True):
    nc = bass.Bass("TRN2", target_bir_lowering=False)
    es = ExitStack()

    def D(n, s, dt=F32, kind="ExternalInput"):
        return nc.dram_tensor(n, s, dt, kind=kind).ap()

    x_d = D("x", [NT * 128, 1024])
    out_d = D("out", [NT * 128, 1024], kind="ExternalOutput")
    w_in_d = D("w_in", [1024, 1536])
    w_out_d = D("w_out", [1024, 1024])
    glu_d = D("glu_w", [512, 512])
    wq_d = D("wq", [1024, 2048])
    pu_d = D("peer_u", [16384, 1024])
    pv_d = D("peer_v", [16384, 1024])
    kT_d = D("kT", [128, 2, 128])
    vecs_d = D("vecs", [128, NV])
    wax_d = D("wax", [128, 8, 128])
    gbc_d = D("gbc", [128, 3, 1024])
    s5l_d = D("s5l", [128, 5, 2048])
    s5c_d = D("s5c", [128, 2, 2048])
    uvb_d = nc.dram_tensor("uvb_scr", [16384, 2048], BF16).ap()
    wqs_d = nc.dram_tensor("wq_scr", [16, 128, 8, 128], BF16).ap()

    ar = Arena(nc, es, 207 * 1024)
    psum = es.enter_context(nc.psum_tensor("ps", [128, 8, 512], F32))
    sc = Sched(nc)

    cur = [None]

    def _emit(eng, fn, r, w, chan=None):
        if cur[0] is not None:
            cur[0].append(lambda: sc.add(eng, fn, r, w, chan=chan))
            return None
        return sc.add(eng, fn, r, w, chan=chan)

    def T(fn, r=(), w=()):
        return _emit('tensor', fn, r, w)

    def V(fn, r=(), w=()):
        return _emit('vector', fn, r, w)

    def A(fn, r=(), w=()):
        return _emit('scalar', fn, r, w)

    def G(fn, r=(), w=()):
        return _emit('gpsimd', fn, r, w)

    def GD(fn, r, w, chan):
        return _emit('gpsimd', fn, r, w, chan)

    clc = [0]

    def dma(eng, out, in_, r, w, chan):
        if chan == 'cl':
            chan = f'cl{clc[0]}'
            clc[0] += 1
        return _emit(eng, lambda e: e.dma_start(out=out, in_=in_), r, w, chan)

    def pk(b, q=None):
        return [f"ps{b}_{i}" for i in range(4)] if q is None else [f"ps{b}_{q}"]

    w_in = ar.alloc([8, 1536], BF16)
    w_out = ar.alloc([8, 1024], BF16)
    glu = ar.alloc([4, 512], BF16)
    bbR = ar.alloc([16, 128], BF16)
    bbI = ar.alloc([16, 128], BF16)
    cR = ar.alloc([16, 128], BF16)
    cIn = ar.alloc([16, 128], BF16)
    cosT = ar.alloc([16, 128])
    sinT = ar.alloc([16, 128])
    rhoT = None if RHO_BCAST else ar.alloc([16, 128])
    wax = ar.alloc([8, 128])
    kT = ar.alloc([2, 128])
    gbc = ar.alloc([3, 1024])
    vecs = ar.alloc([NV])
    cst = ar.alloc([128])
    ident = ar.alloc([128], BF16)
    identf = ar.alloc([128])
    ones = ar.alloc([8])
    hl = ar.alloc([4])
    ire = ar.alloc([16])
    iim = ar.alloc([16])
    xah = ar.alloc([4, 131])
    base = ar.off

    CW, CB, BA, BX, LAM, SD, GB, GNL, GNS, LRS, LIS, LDS, IOTA = 0, 16, 20, 24, 28, 32, 36, 40, 44, 48, 64, 80, 96
    SC8, SC16, MAGS, COSS, SINS, C128, S128 = 0, 4, 8, 24, 40, 56, 72

    wl_prev = [None]

    def wload(dst, src_):
        i = len(sc.ops['gpsimd'])
        op = sc.add('gpsimd', lambda e: e.dma_start(out=dst, in_=src_), [], [], chan=f'wl{i % 2}',
                    extra=[wl_prev[0]] if wl_prev[0] is not None else [])
        wl_prev[0] = op

    for kc in range(8):
        wload(w_in[:, kc, :], w_in_d[kc * 128:(kc + 1) * 128, :])
        wload(w_out[:, kc, :], w_out_d[kc * 128:(kc + 1) * 128, :])
    for kc in range(4):
        wload(glu[:, kc, :], glu_d[kc * 128:(kc + 1) * 128, :])
    for i in range(16):
        rs_ = slice(i * 1024, (i + 1) * 1024)
        sc.add('gpsimd', lambda e, rs_=rs_: e.dma_start(out=uvb_d[rs_, 0:1024], in_=pu_d[rs_, :]), [], [], chan='sbu')
        sc.add('gpsimd', lambda e, rs_=rs_: e.dma_start(out=uvb_d[rs_, 1024:2048], in_=pv_d[rs_, :]), [], [], chan='sbv')
    for kc in range(8):
        sc.add('gpsimd', lambda e, kc=kc: e.dma_start(out=wqs_d[:, :, kc, :], in_=wq_d[kc * 128:(kc + 1) * 128, :].rearrange("p (j c) -> j p c", c=128)),
               [], [], chan='sbw')
    dma('sync', wax, wax_d, [], ['wax'], 'cl')
    dma('sync', kT, kT_d, [], ['kT'], 'cl')
    dma('sync', gbc, gbc_d, [], ['gbc'], 'cl')
    dma('sync', vecs, vecs_d, [], ['vecs'], 'cl')
    G(lambda e: e.memset(identf, 0.0), [], ['identf'])
    G(lambda e: e.affine_select(out=identf, in_=identf, pattern=[[-1, 128]], compare_op=ALU.not_equal,
                                fill=1.0, base=0, channel_multiplier=1), ['identf'], ['identf'])
    V(lambda e: e.tensor_copy(out=ident, in_=identf), ['identf'], ['ident'])
    G(lambda e: e.memset(ones, 1.0), [], ['ones'])
    G(lambda e: e.memset(cst, 0.0), [], ['cst'])

    A(lambda e: e.activation(out=cst[:, 100:104], in_=vecs[:, LAM:LAM + 4], func=AF.Exp, scale=-1.0), ['vecs', 'cst'], ['cst'])
    A(lambda e: e.activation(out=cst[:, 104:108], in_=cst[:, 100:104], func=AF.Ln, bias=1.0), ['cst'], ['cst'])
    V(lambda e: e.tensor_scalar(out=cst[:, SC8:SC8 + 4], in0=cst[:, 104:108], scalar1=-8.0, scalar2=None, op0=ALU.mult), ['cst'], ['cst'])
    V(lambda e: e.tensor_scalar(out=cst[:, SC16:SC16 + 4], in0=cst[:, 104:108], scalar1=-16.0, scalar2=None, op0=ALU.mult), ['cst'], ['cst'])

    def abar(lamR, lamI, logdt, mag, cosv, sinv, tmp, tmpi, kp):
        t0, t1, t2 = tmp
        k = [kp + str(i) for i in range(8)]
        A(lambda e: e.activation(out=t0, in_=logdt, func=AF.Exp), [k[0]], [k[1]])
        V(lambda e: e.tensor_tensor(out=t1, in0=lamI, in1=t0, op=ALU.mult), [k[0], k[1]], [k[2]])
        V(lambda e: e.tensor_tensor(out=t2, in0=lamR, in1=t0, op=ALU.mult), [k[0], k[1]], [k[3]])
        A(lambda e: e.activation(out=mag, in_=t2, func=AF.Exp), [k[3]], [k[4]])
        V(lambda e: e.tensor_scalar(out=t1, in0=t1, scalar1=1.0 / (2.0 * math.pi), scalar2=None, op0=ALU.mult), [k[2]], [k[2]])
        V(lambda e: e.tensor_copy(out=tmpi, in_=t1), [k[2]], [k[5]])
        V(lambda e: e.tensor_copy(out=t0, in_=tmpi), [k[5], k[1]], [k[1]])
        V(lambda e: e.tensor_tensor(out=t1, in0=t1, in1=t0, op=ALU.subtract), [k[2], k[1]], [k[2]])
        A(lambda e: e.activation(out=sinv, in_=t1, func=AF.Sin, scale=TWO_PI), [k[2]], [k[6]])
        V(lambda e: e.tensor_scalar(out=t0, in0=t1, scalar1=0.25, scalar2=None, op0=ALU.is_gt), [k[2], k[1]], [k[1]])
        V(lambda e: e.scalar_tensor_tensor(out=t2, in0=t1, scalar=0.25, in1=t0, op0=ALU.add, op1=ALU.subtract), [k[2], k[1], k[3], k[4]], [k[3]])
        A(lambda e: e.activation(out=cosv, in_=t2, func=AF.Sin, scale=TWO_PI), [k[3]], [k[7]])
        return k

    LT = [ar.alloc([1024]) for _ in range(15)]
    lti = ar.alloc([1024], I32)
    bbR2 = bbR.rearrange("p a b -> p (a b)")
    bbI2 = bbI.rearrange("p a b -> p (a b)")
    cR2 = cR.rearrange("p a b -> p (a b)")
    cI2 = cIn.rearrange("p a b -> p (a b)")
    for hf in range(2):
        cs = slice(hf * 1024, (hf + 1) * 1024)
        KL = ['L' + str(i) for i in range(8)]
        allL = KL + ['LT']
        for i in range(5):
            dma('sync', LT[i], s5l_d[:, i, cs], [], ['LT'], 'cl')
        dma('sync', LT[5], s5c_d[:, 0, cs], [], ['LT'], 'cl')
        dma('sync', LT[6], s5c_d[:, 1, cs], [], ['LT'], 'cl')
        bRp, bIp, lRl, lIl, ldl, cRp, cIp = LT[0:7]
        magL, cosL, sinL, tA, tB = LT[7:12]
        V(lambda e: e.tensor_copy(out=LT[12][:, 0:1], in_=LT[2][:, 0:1]), allL, allL)
        abar(lRl, lIl, ldl, magL, cosL, sinL, (LT[12], LT[13], LT[14]), lti, 'L')
        V(lambda e, cosL=cosL, magL=magL: e.tensor_tensor(out=cosL, in0=cosL, in1=magL, op=ALU.mult), allL, allL)
        V(lambda e, sinL=sinL, magL=magL: e.tensor_tensor(out=sinL, in0=sinL, in1=magL, op=ALU.mult), allL, allL)
        V(lambda e, cosL=cosL: e.tensor_scalar(out=cosL, in0=cosL, scalar1=-1.0, scalar2=None, op0=ALU.add), allL, allL)
        V(lambda e, magL=magL, lRl=lRl: e.tensor_tensor(out=magL, in0=lRl, in1=lRl, op=ALU.mult), allL, allL)
        V(lambda e, tA=tA, lIl=lIl: e.tensor_tensor(out=tA, in0=lIl, in1=lIl, op=ALU.mult), allL, allL)
        V(lambda e, magL=magL, tA=tA: e.tensor_tensor(out=magL, in0=magL, in1=tA, op=ALU.add), allL, allL)
        V(lambda e, magL=magL: e.reciprocal(out=magL, in_=magL), allL, allL)
        V(lambda e, tA=tA, cosL=cosL, lRl=lRl: e.tensor_tensor(out=tA, in0=cosL, in1=lRl, op=ALU.mult), allL, allL)
        V(lambda e, tB=tB, sinL=sinL, lIl=lIl: e.tensor_tensor(out=tB, in0=sinL, in1=lIl, op=ALU.mult), allL, allL)
        V(lambda e, tA=tA, tB=tB: e.tensor_tensor(out=tA, in0=tA, in1=tB, op=ALU.add), allL, allL)
        V(lambda e, tA=tA, magL=magL: e.tensor_tensor(out=tA, in0=tA, in1=magL, op=ALU.mult), allL, allL)
        V(lambda e, tB=tB, sinL=sinL, lRl=lRl: e.tensor_tensor(out=tB, in0=sinL, in1=lRl, op=ALU.mult), allL, allL)
        V(lambda e, cosL=cosL, lIl=lIl: e.tensor_tensor(out=cosL, in0=cosL, in1=lIl, op=ALU.mult), allL, allL)
        V(lambda e, tB=tB, cosL=cosL: e.tensor_tensor(out=tB, in0=tB, in1=cosL, op=ALU.subtract), allL, allL)
        V(lambda e, tB=tB, magL=magL: e.tensor_tensor(out=tB, in0=tB, in1=magL, op=ALU.mult), allL, allL)
        V(lambda e, cosL=cosL, tA=tA, bRp=bRp: e.tensor_tensor(out=cosL, in0=tA, in1=bRp, op=ALU.mult), allL, allL)
        V(lambda e, sinL=sinL, tB=tB, bIp=bIp: e.tensor_tensor(out=sinL, in0=tB, in1=bIp, op=ALU.mult), allL, allL)
        V(lambda e, cs=cs, cosL=cosL, sinL=sinL: e.tensor_tensor(out=bbR2[:, cs], in0=cosL, in1=sinL, op=ALU.subtract), allL, allL)
        V(lambda e, cosL=cosL, tA=tA, bIp=bIp: e.tensor_tensor(out=cosL, in0=tA, in1=bIp, op=ALU.mult), allL, allL)
        V(lambda e, sinL=sinL, tB=tB, bRp=bRp: e.tensor_tensor(out=sinL, in0=tB, in1=bRp, op=ALU.mult), allL, allL)
        V(lambda e, cs=cs, cosL=cosL, sinL=sinL: e.tensor_tensor(out=bbI2[:, cs], in0=cosL, in1=sinL, op=ALU.add), allL, allL)
        V(lambda e, cs=cs, cRp=cRp: e.tensor_copy(out=cR2[:, cs], in_=cRp), allL, allL)
        V(lambda e, cs=cs, cIp=cIp: e.tensor_scalar(out=cI2[:, cs], in0=cIp, scalar1=-1.0, scalar2=None, op0=ALU.mult), allL, allL)

    st = [ar.alloc([16]) for _ in range(3)]
    sti = ar.alloc([16], I32)
    V(lambda e: e.tensor_copy(out=st[0][:, 0:1], in_=vecs[:, 0:1]), ['vecs'], ['S0'])
    abar(vecs[:, LRS:LRS + 16], vecs[:, LIS:LIS + 16], vecs[:, LDS:LDS + 16],
         cst[:, MAGS:MAGS + 16], cst[:, COSS:COSS + 16], cst[:, SINS:SINS + 16], st, sti, 'S')
    KS = ['S' + str(i) for i in range(8)] + ['cst', 'tab']
    if not RHO_BCAST:
        V(lambda e: e.tensor_copy(out=rhoT, in_=bcast(cst, MAGS, [[1, 16], [0, 128]])), KS, KS)
    V(lambda e: e.memset(cosT[:, :, 0:1], 1.0), KS, KS)
    V(lambda e: e.memset(sinT[:, :, 0:1], 0.0), KS, KS)
    V(lambda e: e.tensor_copy(out=cosT[:, :, 1:2], in_=cst[:, COSS:COSS + 16].unsqueeze(2)), KS, KS)
    V(lambda e: e.tensor_copy(out=sinT[:, :, 1:2], in_=cst[:, SINS:SINS + 16].unsqueeze(2)), KS, KS)
    t4 = ar.alloc([16, 64])
    t5 = ar.alloc([16, 64])
    cps = cosT.ap[0][0]

    def cmul(oc, osn, ic, isn, mc, ms, a4, a5):
        V(lambda e: e.tensor_tensor(out=a4, in0=ic, in1=mc, op=ALU.mult), KS, KS)
        V(lambda e: e.tensor_tensor(out=a5, in0=isn, in1=ms, op=ALU.mult), KS, KS)
        V(lambda e: e.tensor_tensor(out=oc, in0=a4, in1=a5, op=ALU.subtract), KS, KS)
        V(lambda e: e.tensor_tensor(out=a4, in0=ic, in1=ms, op=ALU.mult), KS, KS)
        V(lambda e: e.tensor_tensor(out=a5, in0=isn, in1=mc, op=ALU.mult), KS, KS)
        V(lambda e: e.tensor_tensor(out=osn, in0=a4, in1=a5, op=ALU.add), KS, KS)

    for n in (2, 4, 8, 16, 32, 64, 128):
        if n < 128:
            oc, osn = cosT[:, :, n:n + 1], sinT[:, :, n:n + 1]
        else:
            oc, osn = cst[:, C128:C128 + 16].unsqueeze(2), cst[:, S128:S128 + 16].unsqueeze(2)
        cmul(oc, osn, cosT[:, :, n - 1:n], sinT[:, :, n - 1:n], cosT[:, :, 1:2], sinT[:, :, 1:2], t4[:, :, 0:1], t5[:, :, 0:1])
        if n < 128:
            m = n - 1
            mc = bass.AP(cosT.tensor, cosT.offset + n, [[cps, 128], [128, 16], [0, m]])
            ms = bass.AP(sinT.tensor, sinT.offset + n, [[cps, 128], [128, 16], [0, m]])
            cmul(cosT[:, :, n + 1:2 * n], sinT[:, :, n + 1:2 * n], cosT[:, :, 1:n], sinT[:, :, 1:n], mc, ms, t4[:, :, 0:m], t5[:, :, 0:m])

    sc.barrier()

    ar.off = base
    NXH = 3
    xh = [ar.alloc([1024]) for _ in range(NXH)]
    jbf = ar.alloc([1024], BF16)
    xnT = ar.alloc([8, 128], BF16)
    pT = ar.alloc([8, 128])
    usb = ar.alloc([4, 128], BF16)
    lt = [ar.alloc([128]) for _ in range(10)]
    ya = ar.alloc([4, 128], BF16)
    yb = ar.alloc([4, 128], BF16)
    s5t = [ar.alloc([128]) for _ in range(6)]
    hreb = ar.alloc([128], BF16)
    himb = ar.alloc([128], BF16)
    gsf = ar.alloc([4, 128])
    gsb = ar.alloc([4, 128], BF16)
    sm = ar.alloc([16])
    xn2bs = [ar.alloc([1024], BF16) for _ in range(2)]
    xn2T = ar.alloc([8, 128], BF16)
    wqb = [ar.alloc([8, 128], BF16) for _ in range(2)]
    qTs = [ar.alloc([4, 128])] * 2
    ssb = [ar.alloc([4, 128])] * 2
    mr = ar.alloc([256])
    v_all = ar.alloc([16, 16])
    ix_all = ar.alloc([16, 16], U32)
    ixf = ar.alloc([16, 16])
    cand = [ar.alloc([256])] * 2
    sc_all = ar.alloc([8, 16])
    pos_all = ar.alloc([8, 16], U32)
    _ixi = ix_all.rearrange("p a b -> p (a b)").bitcast(I32)
    pa_i = _ixi[:, 0:128]
    pb_i = _ixi[:, 128:256]
    paf = ar.alloc([128])
    pbf = ar.alloc([128])
    oh = mr.rearrange("p (a b) -> p a b", a=16)
    i1s = ar.alloc([128])
    i2s = ar.alloc([128])
    ef = ar.alloc([128])
    eis = [ar.alloc([128], I32) for _ in range(2)]
    ex = ar.alloc([8, 16])
    zz = ar.alloc([16])
    gws = [ar.alloc([128]) for _ in range(2)]
    dd = ar.alloc([128])
    coef = ar.alloc([128])
    ND = 4
    diag = [ar.alloc([128], BF16) for _ in range(ND)]
    GS = 4
    gtmp = [ar.alloc([GS]) for _ in range(2)]
    NR = min(16, (ar.cap - ar.off) // 4096) // GS * GS
    assert NR >= 2 * GS, NR
    ring = [ar.alloc([2048], BF16) for _ in range(NR)]

    tpb = psum[:, 0, :].bitcast(BF16)

    def rstd_from(ss_ap, n, out_ap, kr, kw):
        A(lambda e: e.activation(out=out_ap, in_=ss_ap, func=AF.Sqrt, scale=1.0 / n, bias=EPS), kr, kw)
        V(lambda e: e.reciprocal(out=out_ap, in_=out_ap), kw, kw)

    def transpose8(src_bf, dst, ksrc, kdst):
        def f(e):
            ins = None
            for kc in range(8):
                ins = e.transpose(out=tpb[:, kc * 128:(kc + 1) * 128], in_=src_bf[:, kc * 128:(kc + 1) * 128], identity=ident)
            return ins
        T(f, [ksrc, 'ident'], pk(0))
        A(lambda e: e.copy(out=dst.rearrange("p a b -> p (a b)"), in_=tpb), pk(0), [kdst])

    rcount = [0]
    _bc = {}

    def bchk(e):
        if 'r' not in _bc:
            _bc['r'] = e.to_reg(16383)
        return _bc['r']

    RHO_BC = RHO_BCAST

    def front(t):
        first = (t % TPS == 0)
        xb = xh[t % NXH]
        kx = f'xh{t % NXH}'
        p2 = t % 2
        xn2 = xn2bs[p2]
        kxn2 = f'xn2_{p2}'
        ei = eis[p2]
        kei = f'ei{p2}'
        gw = gws[p2]
        kgw = f'gw{p2}'
        if t == 0:
            dma('sync', xb, x_d[0:128, :], [], [kx], 'xl0')
        if t + 1 < NT:
            n1 = (t + 1) % NXH
            dma('sync', xh[n1], x_d[(t + 1) * 128:(t + 2) * 128, :], [], [f'xh{n1}'], f'xl{n1}')
        A(lambda e: e.activation(out=jbf, in_=xb, func=AF.Square, accum_out=sm[:, 0:1]), [kx], ['jbf', 'sm0'])
        rstd_from(sm[:, 0:1], 1024.0, sm[:, 0:1], ['sm0'], ['sm0'])
        V(lambda e: e.scalar_tensor_tensor(out=xn2, in0=xb, scalar=sm[:, 0:1], in1=gbc[:, 0, :], op0=ALU.mult, op1=ALU.mult),
          [kx, 'sm0', 'gbc'], [kxn2])
        transpose8(xn2, xnT, kxn2, 'xnT')
        if first:
            V(lambda e: e.memset(xah[:, :, 0:3], 0.0), [], ['xah'])
        else:
            A(lambda e: e.copy(out=xah[:, :, 0:3], in_=xah[:, :, 128:131]), ['xah'], ['xah'])
        for gi in range(3):
            b = 1 + (gi % 2)

            def f(e, gi=gi, b=b):
                ins = None
                for j in range(4):
                    m = gi * 4 + j
                    for kc in range(8):
                        ins = e.matmul(psum[:, b, j * 128:(j + 1) * 128], lhsT=w_in[:, kc, m * 128:(m + 1) * 128],
                                       rhs=xnT[:, kc, :], start=(kc == 0), stop=(kc == 7))
                return ins
            T(f, ['w_in', 'xnT'], pk(b))
            if gi == 0:
                A(lambda e, b=b: e.copy(out=xah[:, :, 3:131], in_=psum[:, b, :].rearrange("p (a b) -> p a b", a=4)), pk(b), ['xah'])
            elif gi == 1:
                A(lambda e, b=b: e.copy(out=pT[:, 0:4, :], in_=psum[:, b, :].rearrange("p (a b) -> p a b", a=4)), pk(b), ['gate'])
            else:
                A(lambda e, b=b: e.copy(out=pT[:, 4:8, :], in_=psum[:, b, :].rearrange("p (a b) -> p a b", a=4)), pk(b), ['usf'])
                A(lambda e: e.copy(out=usb, in_=pT[:, 4:8, :]), ['usf'], ['usb'])
        xc, rr, ig, aa, a2, uu, hh_, gg, yar, sq = lt
        for c in range(4):
            cw = lambda k, c=c: vecs[:, CW + c * 4 + k:CW + c * 4 + k + 1]
            V(lambda e, c=c, cw=cw: e.tensor_scalar(out=xc, in0=xah[:, c, 0:128], scalar1=cw(0), scalar2=vecs[:, CB + c:CB + c + 1],
                                                    op0=ALU.mult, op1=ALU.add), ['xah', 'vecs'], ['xc'])
            for k in range(1, 4):
                V(lambda e, c=c, k=k, cw=cw: e.scalar_tensor_tensor(out=xc, in0=xah[:, c, k:k + 128], scalar=cw(k), in1=xc,
                                                                    op0=ALU.mult, op1=ALU.add), ['xah', 'vecs', 'xc'], ['xc'])

            def f(e, c=c):
                e.matmul(psum[:, 2, 0:128], lhsT=wax[:, c, :], rhs=xc, start=True, stop=True)
                return e.matmul(psum[:, 2, 128:256], lhsT=wax[:, 4 + c, :], rhs=xc, start=True, stop=True)
            T(f, ['wax', 'xc'], pk(2, 0) + pk(2, 1))
            A(lambda e, c=c: e.activation(out=rr, in_=psum[:, 2, 0:128], func=AF.Sigmoid, bias=vecs[:, BA + c:BA + c + 1]), pk(2, 0) + ['vecs'], ['rr'])
            A(lambda e, c=c: e.activation(out=ig, in_=psum[:, 2, 128:256], func=AF.Sigmoid, bias=vecs[:, BX + c:BX + c + 1]), pk(2, 1) + ['vecs'], ['ig'])
            A(lambda e, c=c: e.activation(out=aa, in_=rr, func=AF.Exp, scale=cst[:, SC8 + c:SC8 + c + 1]), ['rr'], ['aa'])
            A(lambda e, c=c: e.activation(out=a2, in_=rr, func=AF.Exp, scale=cst[:, SC16 + c:SC16 + c + 1]), ['rr'], ['a2'])
            A(lambda e: e.activation(out=a2, in_=a2, func=AF.Sqrt, scale=-1.0, bias=1.0), ['a2'], ['a2'])
            V(lambda e: e.tensor_tensor(out=uu, in0=ig, in1=xc, op=ALU.mult), ['ig', 'xc'], ['uu'])
            V(lambda e: e.tensor_tensor(out=uu, in0=uu, in1=a2, op=ALU.mult), ['uu', 'a2'], ['uu'])
            init = 0.0 if first else hl[:, c:c + 1]
            V(lambda e, init=init: e.tensor_tensor_scan(out=hh_, data0=aa, data1=uu, initial=init, op0=ALU.mult, op1=ALU.add),
              ['aa', 'uu', f'hl{c}'], ['hh'])
            A(lambda e, c=c: e.copy(out=hl[:, c:c + 1], in_=hh_[:, 127:128]), ['hh'], [f'hl{c}'])
            A(lambda e, c=c: e.activation(out=gg, in_=pT[:, c, :], func=AF.Gelu_apprx_tanh), ['gate'], ['gg'])
            V(lambda e: e.tensor_tensor(out=yar, in0=hh_, in1=gg, op=ALU.mult), ['hh', 'gg'], ['yar'])
            A(lambda e: e.activation(out=sq, in_=yar, func=AF.Square), ['yar'], ['sq'])
            T(lambda e, c=c: e.matmul(psum[:, 3, 256:257], lhsT=sq, rhs=ones[:, 0:1], start=(c == 0), stop=(c == 3)), ['sq', 'ones'], pk(3, 2))
            A(lambda e, c=c: e.activation(out=ya[:, c, :], in_=yar, func=AF.Copy, scale=vecs[:, GNL + c:GNL + c + 1]), ['yar', 'vecs'], ['ya'])
        t1, t2, wre, wim, gre, gim = s5t
        for cc in range(4):
            for pq in range(4):
                q = cc * 4 + pq
                bq = 4 + (q % 2)

                def f(e, q=q, cc=cc, bq=bq):
                    e.matmul(psum[:, bq, 0:128], lhsT=bbR[:, q, :], rhs=usb[:, cc, :], start=True, stop=True)
                    return e.matmul(psum[:, bq, 128:256], lhsT=bbI[:, q, :], rhs=usb[:, cc, :], start=True, stop=True)
                T(f, ['usb'], pk(bq, 0) + pk(bq, 1))
                buR = psum[:, bq, 0:128]
                buI = psum[:, bq, 128:256]
                kb = pk(bq, 0) + pk(bq, 1)
                cq = cosT[:, q, :]
                sq_ = sinT[:, q, :]
                rho = bcast(cst, MAGS + q, [[0, 128]]) if RHO_BC else rhoT[:, q, :]
                V(lambda e, cq=cq, buR=buR: e.tensor_tensor(out=t1, in0=buR, in1=cq, op=ALU.mult), kb, ['t1'])
                V(lambda e, sq_=sq_, buI=buI: e.tensor_tensor(out=t2, in0=buI, in1=sq_, op=ALU.mult), kb, ['t2'])
                V(lambda e: e.tensor_tensor(out=wre, in0=t1, in1=t2, op=ALU.add), ['t1', 't2'], ['wre'])
                V(lambda e, cq=cq, buI=buI: e.tensor_tensor(out=t1, in0=buI, in1=cq, op=ALU.mult), kb + ['t1'], ['t1'])
                V(lambda e, sq_=sq_, buR=buR: e.tensor_tensor(out=t2, in0=buR, in1=sq_, op=ALU.mult), kb + ['t2'], ['t2'])
                V(lambda e: e.tensor_tensor(out=wim, in0=t1, in1=t2, op=ALU.subtract), ['t1', 't2'], ['wim'])
                i_r = 0.0 if first else ire[:, q:q + 1]
                i_i = 0.0 if first else iim[:, q:q + 1]
                V(lambda e, rho=rho, i_r=i_r: e.tensor_tensor_scan(out=gre, data0=rho, data1=wre, initial=i_r, op0=ALU.mult, op1=ALU.add),
                  ['wre', f'ic{q}'], ['gre'])
                V(lambda e, rho=rho, i_i=i_i: e.tensor_tensor_scan(out=gim, data0=rho, data1=wim, initial=i_i, op0=ALU.mult, op1=ALU.add),
                  ['wim', f'ic{q}'], ['gim'])
                c128 = cst[:, C128 + q:C128 + q + 1]
                s128 = cst[:, S128 + q:S128 + q + 1]
                V(lambda e, s128=s128: e.tensor_scalar(out=sm[:, 8:9], in0=gim[:, 127:128], scalar1=s128, scalar2=None, op0=ALU.mult), ['gim'], ['sm8'])
                V(lambda e, q=q, c128=c128: e.scalar_tensor_tensor(out=ire[:, q:q + 1], in0=gre[:, 127:128], scalar=c128, in1=sm[:, 8:9],
                                                                   op0=ALU.mult, op1=ALU.subtract), ['gre', 'sm8', f'ic{q}'], [f'ic{q}'])
                V(lambda e, s128=s128: e.tensor_scalar(out=sm[:, 9:10], in0=gre[:, 127:128], scalar1=s128, scalar2=None, op0=ALU.mult), ['gre'], ['sm9'])
                V(lambda e, q=q, c128=c128: e.scalar_tensor_tensor(out=iim[:, q:q + 1], in0=gim[:, 127:128], scalar=c128, in1=sm[:, 9:10],
                                                                   op0=ALU.mult, op1=ALU.add), ['gim', 'sm9', f'ic{q}'], [f'ic{q}'])
                V(lambda e, cq=cq: e.tensor_tensor(out=t1, in0=gre, in1=cq, op=ALU.mult), ['gre', 't1'], ['t1'])
                V(lambda e, sq_=sq_: e.tensor_tensor(out=t2, in0=gim, in1=sq_, op=ALU.mult), ['gim', 't2'], ['t2'])
                V(lambda e: e.tensor_tensor(out=hreb, in0=t1, in1=t2, op=ALU.subtract), ['t1', 't2'], ['hreb'])
                V(lambda e, cq=cq: e.tensor_tensor(out=t1, in0=gim, in1=cq, op=ALU.mult), ['gim', 't1'], ['t1'])
                V(lambda e, sq_=sq_: e.tensor_tensor(out=t2, in0=gre, in1=sq_, op=ALU.mult), ['gre', 't2'], ['t2'])
                V(lambda e: e.tensor_tensor(out=himb, in0=t1, in1=t2, op=ALU.add), ['t1', 't2'], ['himb'])

                def f2(e, q=q, pq=pq):
                    e.matmul(psum[:, 1, 0:128], lhsT=cR[:, q, :], rhs=hreb, start=(pq == 0), stop=False)
                    return e.matmul(psum[:, 1, 0:128], lhsT=cIn[:, q, :], rhs=himb, start=False, stop=(pq == 3))
                T(f2, ['hreb', 'himb'], pk(1, 0))
            V(lambda e, cc=cc: e.scalar_tensor_tensor(out=t1, in0=pT[:, 4 + cc, :], scalar=vecs[:, SD + cc:SD + cc + 1], in1=psum[:, 1, 0:128],
                                                      op0=ALU.mult, op1=ALU.add), ['usf', 'vecs', 't1'] + pk(1, 0), ['t1'])
            A(lambda e, cc=cc: e.activation(out=gsf[:, cc, :], in_=t1, func=AF.Gelu_apprx_tanh), ['t1'], ['gsf'])
            A(lambda e, cc=cc: e.copy(out=gsb[:, cc, :], in_=gsf[:, cc, :]), ['gsf'], ['gsb'])
        for m in range(4):
            def f(e, m=m):
                ins = None
                for k in range(4):
                    ins = e.matmul(psum[:, 2, 0:128], lhsT=glu[:, k, m * 128:(m + 1) * 128], rhs=gsb[:, k, :], start=(k == 0), stop=(k == 3))
                return ins
            T(f, ['gsb'], pk(2, 0))
            A(lambda e, m=m: e.activation(out=rr, in_=psum[:, 2, 0:128], func=AF.Sigmoid, bias=vecs[:, GB + m:GB + m + 1]), pk(2, 0) + ['vecs'], ['rr'])
            V(lambda e, m=m: e.tensor_tensor(out=yar, in0=gsf[:, m, :], in1=rr, op=ALU.mult), ['gsf', 'rr'], ['yar'])
            A(lambda e: e.activation(out=sq, in_=yar, func=AF.Square), ['yar'], ['sq'])
            T(lambda e, m=m: e.matmul(psum[:, 3, 257:258], lhsT=sq, rhs=ones[:, 0:1], start=(m == 0), stop=(m == 3)), ['sq', 'ones'], pk(3, 2))
            A(lambda e, m=m: e.activation(out=yb[:, m, :], in_=yar, func=AF.Copy, scale=vecs[:, GNS + m:GNS + m + 1]), ['yar', 'vecs'], ['yb'])
        rstd_from(psum[:, 3, 256:258], 512.0, sm[:, 1:3], pk(3, 2), ['sm12'])
        for half in range(2):
            def f(e, half=half):
                ins = None
                for c in range(4):
                    ins = e.matmul(psum[:, 4 + half, :], lhsT=ya[:, c, :], rhs=w_out[:, c, half * 512:(half + 1) * 512], start=(c == 0), stop=(c == 3))
                return ins
            T(f, ['ya', 'w_out'], pk(4 + half))

            def f(e, half=half):
                ins = None
                for c in range(4):
                    ins = e.matmul(psum[:, 1 + half, :], lhsT=yb[:, c, :], rhs=w_out[:, 4 + c, half * 512:(half + 1) * 512], start=(c == 0), stop=(c == 3))
                return ins
            T(f, ['yb', 'w_out'], pk(1 + half))
            xs = xb[:, half * 512:(half + 1) * 512]
            V(lambda e, half=half, xs=xs: e.scalar_tensor_tensor(out=xs, in0=psum[:, 4 + half, :], scalar=sm[:, 1:2], in1=xs, op0=ALU.mult, op1=ALU.add),
              pk(4 + half) + ['sm12', kx], [kx])
            V(lambda e, half=half, xs=xs: e.scalar_tensor_tensor(out=xs, in0=psum[:, 1 + half, :], scalar=sm[:, 2:3], in1=xs, op0=ALU.mult, op1=ALU.add),
              pk(1 + half) + ['sm12', kx], [kx])
        A(lambda e: e.activation(out=jbf, in_=xb, func=AF.Square, accum_out=sm[:, 3:4]), [kx], ['jbf', 'sm3'])
        rstd_from(sm[:, 3:4], 1024.0, sm[:, 3:4], ['sm3'], ['sm3'])
        V(lambda e: e.scalar_tensor_tensor(out=xn2, in0=xb, scalar=sm[:, 3:4], in1=gbc[:, 1, :], op0=ALU.mult, op1=ALU.mult),
          [kx, 'sm3', 'gbc'], [kxn2])
        transpose8(xn2, xn2T, kxn2, 'xn2T')
        for grp in range(4):
            bqT = 4 + (grp % 2)
            bsc = 1 + (grp % 2)
            for jj in range(4):
                j = grp * 4 + jj
                wslot = j % 2
                wb = wqb[wslot]
                dma('sync', wb, wqs_d[j], [], [f'wq{wslot}'], f'wq{wslot}')

                def f(e, wb=wb, jj=jj, bqT=bqT):
                    ins = None
                    for kc in range(8):
                        ins = e.matmul(psum[:, bqT, jj * 128:(jj + 1) * 128], lhsT=wb[:, kc, :], rhs=xn2T[:, kc, :], start=(kc == 0), stop=(kc == 7))
                    return ins
                T(f, [f'wq{wslot}', 'xn2T'], pk(bqT, jj))
            qs = qTs[grp % 2]
            kq = 'qTs0'
            A(lambda e, qs=qs, bqT=bqT: e.copy(out=qs.rearrange("p a b -> p (a b)"), in_=psum[:, bqT, :]), pk(bqT), [kq])

            def f(e, qs=qs, bsc=bsc):
                ins = None
                for jj in range(4):
                    ins = e.matmul(psum[:, bsc, jj * 128:(jj + 1) * 128], lhsT=qs[:, jj, :], rhs=kT[:, jj % 2, :], start=True, stop=True)
                return ins
            T(f, [kq, 'kT'], pk(bsc))
            sb = ssb[grp % 2]
            ks = 'ssb0'
            A(lambda e, sb=sb, bsc=bsc: e.copy(out=sb.rearrange("p a b -> p (a b)"), in_=psum[:, bsc, :]), pk(bsc), [ks])
            for jj in range(4):
                j = grp * 4 + jj
                s_j = sb[:, jj, :]
                V(lambda e, j=j, s_j=s_j: e.max(out=v_all[:, j, 0:8], in_=s_j), [ks], ['v_all'])
                V(lambda e, j=j, s_j=s_j: e.match_replace(out=mr[:, 0:128], in_to_replace=v_all[:, j, 0:8], in_values=s_j, imm_value=-1e30), [ks, 'v_all'], ['mr'])
                V(lambda e, j=j: e.max(out=v_all[:, j, 8:16], in_=mr[:, 0:128]), ['mr'], ['v_all'])
                V(lambda e, j=j, s_j=s_j: e.max_index(out=ix_all[:, j, 0:8], in_max=v_all[:, j, 0:8], in_values=s_j), [ks, 'v_all'], ['ix_all'])
                V(lambda e, j=j, s_j=s_j: e.max_index(out=ix_all[:, j, 8:16], in_max=v_all[:, j, 8:16], in_values=s_j), [ks, 'v_all'], ['ix_all'])
        V(lambda e: e.tensor_copy(out=ixf, in_=ix_all), ['ix_all'], ['ixf'])
        vps = v_all.ap[0][0]
        for h in range(8):
            cd = cand[h % 2]
            kc_ = 'cand0'
            in0 = bass.AP(v_all.tensor, v_all.offset + (2 * h) * 16, [[vps, 128], [1, 16], [0, 16]])
            in1 = bass.AP(v_all.tensor, v_all.offset + (2 * h + 1) * 16, [[vps, 128], [0, 16], [1, 16]])
            V(lambda e, cd=cd, in0=in0, in1=in1: e.tensor_tensor(out=cd.rearrange("p (a b) -> p a b", a=16), in0=in0, in1=in1, op=ALU.add), ['v_all'], [kc_])
            V(lambda e, cd=cd, h=h: e.max(out=sc_all[:, h, 0:8], in_=cd), [kc_], ['sc_all'])
            V(lambda e, cd=cd, h=h: e.match_replace(out=mr, in_to_replace=sc_all[:, h, 0:8], in_values=cd, imm_value=-1e30), [kc_, 'sc_all'], ['mr'])
            V(lambda e, h=h: e.max(out=sc_all[:, h, 8:16], in_=mr), ['mr'], ['sc_all'])
            V(lambda e, cd=cd, h=h: e.max_index(out=pos_all[:, h, 0:8], in_max=sc_all[:, h, 0:8], in_values=cd), [kc_, 'sc_all'], ['pos_all'])
            V(lambda e, cd=cd, h=h: e.max_index(out=pos_all[:, h, 8:16], in_max=sc_all[:, h, 8:16], in_values=cd), [kc_, 'sc_all'], ['pos_all'])
        pos2 = pos_all.rearrange("p a b -> p (a b)").bitcast(I32)
        V(lambda e: e.tensor_single_scalar(out=pa_i, in_=pos2, scalar=4, op=ALU.arith_shift_right), ['pos_all', 'ixf'], ['ix_all'])
        V(lambda e: e.tensor_single_scalar(out=pb_i, in_=pos2, scalar=15, op=ALU.bitwise_and), ['pos_all', 'ixf'], ['ix_all'])
        V(lambda e: e.tensor_copy(out=paf, in_=pa_i), ['ix_all'], ['paf'])
        V(lambda e: e.tensor_copy(out=pbf, in_=pb_i), ['ix_all'], ['pbf'])
        iot = bcast(vecs, IOTA, [[0, 16], [1, 16]])
        ixf2 = ixf.rearrange("p a b -> p (a b)")
        for h in range(8):
            for (pf, kpf, off, dst, kd) in ((paf, 'paf', 0, i1s, 'i1s'), (pbf, 'pbf', 16, i2s, 'i2s')):
                pfb = bcast(pf, h * 16, [[1, 16], [0, 16]])
                ixb = bcast(ixf2, h * 32 + off, [[0, 16], [1, 16]])
                V(lambda e, pfb=pfb: e.tensor_tensor(out=oh, in0=pfb, in1=iot, op=ALU.is_equal), [kpf, 'vecs', 'mr'], ['mr'])
                V(lambda e, ixb=ixb: e.tensor_tensor(out=oh, in0=oh, in1=ixb, op=ALU.mult), ['mr', 'ixf'], ['mr'])
                V(lambda e, dst=dst, h=h: e.tensor_reduce(out=dst[:, h * 16:(h + 1) * 16], in_=oh, axis=AX.X, op=ALU.add), ['mr', kd], [kd])
        V(lambda e: e.scalar_tensor_tensor(out=ef, in0=i1s, scalar=128.0, in1=i2s, op0=ALU.mult, op1=ALU.add), ['i1s', 'i2s'], ['ef'])
        V(lambda e: e.tensor_copy(out=ei, in_=ef), ['ef'], [kei])
        sps = sc_all.ap[0][0]
        mxb = bass.AP(sc_all.tensor, sc_all.offset, [[sps, 128], [16, 8], [0, 16]])
        V(lambda e: e.tensor_tensor(out=ex, in0=sc_all, in1=mxb, op=ALU.subtract), ['sc_all'], ['ex'])
        A(lambda e: e.activation(out=ex, in_=ex, func=AF.Exp), ['ex'], ['ex'])
        V(lambda e: e.tensor_reduce(out=zz[:, 0:8], in_=ex, axis=AX.X, op=ALU.add), ['ex'], ['zz'])
        V(lambda e: e.reciprocal(out=zz[:, 0:8], in_=zz[:, 0:8]), ['zz'], ['zz'])
        zb = bcast(zz, 0, [[1, 8], [0, 16]])
        V(lambda e: e.tensor_tensor(out=gw.rearrange("p (a b) -> p a b", a=8), in0=ex, in1=zb, op=ALU.mult), ['ex', 'zz'], [kgw])

    def back(t):
        xb = xh[t % NXH]
        kx = f'xh{t % NXH}'
        p2 = t % 2
        xn2 = xn2bs[p2]
        kxn2 = f'xn2_{p2}'
        ei = eis[p2]
        kei = f'ei{p2}'
        gw = gws[p2]
        kgw = f'gw{p2}'
        ddk = [f'dd{n}' for n in range(128)]
        V(lambda e: e.memset(dd, 0.0), [], ddk)
        NG = 128 // GS
        slots = {}

        def gathers(g):
            for n in range(g * GS, (g + 1) * GS):
                rs = rcount[0] % NR
                rcount[0] += 1
                slots[n] = rs
                rb = ring[rs]
                GD(lambda e, rb=rb, n=n: e.indirect_dma_start(out=rb, out_offset=None, in_=uvb_d,
                                                              in_offset=bass.IndirectOffsetOnAxis(ap=ei[:, n:n + 1], axis=0), bounds_check=bchk(e), oob_is_err=False),
                   [kei], [f'ring{rs}'], f'rg{rs}')

        LOOK = NR // GS - 1
        for g0 in range(min(LOOK + 1, NG)):
            gathers(g0)

        def tail2_and_mm(g):
            cs = slice(g * GS, (g + 1) * GS)
            dk = [f'dd{n}' for n in range(g * GS, (g + 1) * GS)]
            kt = f'gt{g % 2}'
            tg = gtmp[g % 2]
            d_ = dd[:, cs]
            V(lambda e, tg=tg, d_=d_: e.scalar_tensor_tensor(out=tg, in0=tg, scalar=1.0, in1=d_, op0=ALU.add, op1=ALU.mult), [kt] + dk, [kt])
            V(lambda e, tg=tg, cs=cs: e.scalar_tensor_tensor(out=coef[:, cs], in0=tg, scalar=0.5, in1=gw[:, cs], op0=ALU.mult, op1=ALU.mult),
              [kt, kgw], [f'coef{g % 2}'])
            for n in range(g * GS, (g + 1) * GS):
                rs = slots[n]
                rb = ring[rs]
                dg = diag[n % ND]
                kdg = f'diag{n % ND}'
                A(lambda e, dg=dg, n=n: e.activation(out=dg, in_=identf, func=AF.Copy, scale=coef[:, n:n + 1]), [f'coef{g % 2}'], [kdg])

                def f(e, dg=dg, rb=rb, n=n):
                    e.matmul(psum[:, 6, :], lhsT=dg, rhs=rb[:, 1024:1536], start=(n == 0), stop=(n == 127))
                    return e.matmul(psum[:, 7, :], lhsT=dg, rhs=rb[:, 1536:2048], start=(n == 0), stop=(n == 127))
                T(f, [kdg, f'ring{rs}'], pk(6) + pk(7))

        for g in range(NG):
            cs = slice(g * GS, (g + 1) * GS)
            dk = [f'dd{n}' for n in range(g * GS, (g + 1) * GS)]
            for n in range(g * GS, (g + 1) * GS):
                rs = slots[n]
                rb = ring[rs]
                V(lambda e, rb=rb, n=n: e.scalar_tensor_tensor(out=rb[:, 0:1024], in0=rb[:, 0:1024], scalar=1.0, in1=xn2, op0=ALU.mult, op1=ALU.mult,
                                                               accum_out=dd[:, n:n + 1]),
                  [f'ring{rs}', kxn2, f'dd{n}'], [f'dd{n}', f'ring{rs}'])
            kt = f'gt{g % 2}'
            tg = gtmp[g % 2]
            d_ = dd[:, cs]
            V(lambda e, tg=tg, d_=d_: e.scalar_tensor_tensor(out=tg, in0=d_, scalar=0.044715, in1=d_, op0=ALU.mult, op1=ALU.mult), dk, [kt])
            V(lambda e, tg=tg, d_=d_: e.scalar_tensor_tensor(out=tg, in0=tg, scalar=1.0, in1=d_, op0=ALU.add, op1=ALU.mult), [kt] + dk, [kt])
            A(lambda e, tg=tg: e.activation(out=tg, in_=tg, func=AF.Tanh, scale=0.7978845608028654), [kt], [kt])
            if g >= 1:
                tail2_and_mm(g - 1)
                if g - 1 + LOOK + 1 < NG:
                    gathers(g - 1 + LOOK + 1)
        tail2_and_mm(NG - 1)
        V(lambda e: e.tensor_tensor(out=xb[:, 0:512], in0=psum[:, 6, :], in1=xb[:, 0:512], op=ALU.add), pk(6) + [kx], [kx])
        V(lambda e: e.tensor_tensor(out=xb[:, 512:1024], in0=psum[:, 7, :], in1=xb[:, 512:1024], op=ALU.add), pk(7) + [kx], [kx])
        A(lambda e: e.activation(out=jbf, in_=xb, func=AF.Square, accum_out=sm[:, 4:5]), [kx], ['jbf', 'sm4'])
        rstd_from(sm[:, 4:5], 1024.0, sm[:, 4:5], ['sm4'], ['sm4'])
        V(lambda e: e.scalar_tensor_tensor(out=xb, in0=xb, scalar=sm[:, 4:5], in1=gbc[:, 2, :], op0=ALU.mult, op1=ALU.mult),
          [kx, 'sm4', 'gbc'], [kx])
        dma('sync', out_d[t * 128:(t + 1) * 128, :], xb, [kx], [], 'ost')

    def record(fn, t):
        lst = []
        cur[0] = lst
        fn(t)
        cur[0] = None
        return lst

    def run_merged(F, B):
        nf, nb = len(F), len(B)
        i = j = 0
        while i < nf or j < nb:
            if j >= nb or (i < nf and i * nb <= j * nf):
                F[i]()
                i += 1
            else:
                B[j]()
                j += 1

    if PIPE:
        run_merged(record(front, 0), [])
        for t in range(1, NT):
            Fl = record(front, t)
            Bl = record(back, t - 1)
            run_merged(Fl, Bl)
        run_merged([], record(back, NT - 1))
    else:
        for t in range(NT):
            run_merged(record(front, t), [])
            run_merged(record(back, t), [])
    sc.add('sync', lambda e: e.nop(), [], [], extra=list(sc.last_dma.values()))
    sc.emit(es)
    es.close()
    return nc


def prep_shared(inp):
    f = lambda a: np.ascontiguousarray(np.asarray(a, dtype=np.float32))
    col = lambda v, n: f(v).reshape(n, 128).T
    vecs = np.zeros((128, NV), np.float32)
    cw = f(inp['conv_w'][0])
    for c in range(4):
        for k in range(4):
            vecs[:, c * 4 + k] = cw[k, c * 128:(c + 1) * 128]
    vecs[:, 16:20] = col(inp['conv_b'][0], 4)
    vecs[:, 20:24] = col(inp['lru_ba'][0], 4)
    vecs[:, 24:28] = col(inp['lru_bx'][0], 4)
    vecs[:, 28:32] = col(inp['lru_lambda'][0], 4)
    vecs[:, 32:36] = col(inp['s5_d'][0], 4)
    vecs[:, 36:40] = col(inp['s5_glu_b'][0], 4)
    vecs[:, 40:44] = col(inp['gnorm_lru'][0], 4)
    vecs[:, 44:48] = col(inp['gnorm_s5'][0], 4)
    lr = f(inp['s5_lambda_re'][0])
    li = f(inp['s5_lambda_im'][0])
    ld = f(inp['s5_log_dt'][0])
    lrs = lr.reshape(16, 128).T
    lis = li.reshape(16, 128).T
    lds = np.repeat(ld.reshape(16, 2, 1), 64, axis=2).reshape(16, 128).T
    vecs[:, 48:64] = lrs
    vecs[:, 64:80] = lis
    vecs[:, 80:96] = lds
    vecs[:, 96:112] = np.arange(16, dtype=np.float32)[None, :]
    wax = np.zeros((128, 8, 128), np.float32)
    wa = f(inp['lru_wa'][0])
    wx = f(inp['lru_wx'][0])
    for h in range(8):
        c, o = h // 2, (h % 2) * 64
        wax[o:o + 64, c, o:o + 64] = wa[h]
        wax[o:o + 64, 4 + c, o:o + 64] = wx[h]
    kT = np.stack([f(inp['peer_k1'][0]).T, f(inp['peer_k2'][0]).T], axis=1)
    gbc = np.stack([np.broadcast_to(f(inp['norm1_g'][0]), (128, 1024)),
                    np.broadcast_to(f(inp['norm2_g'][0]), (128, 1024)),
                    np.broadcast_to(f(inp['normf_g']), (128, 1024))], axis=1)
    bR = f(inp['s5_b_re'][0])
    bI = f(inp['s5_b_im'][0])
    cRe = f(inp['s5_c_re'][0])
    cIm = f(inp['s5_c_im'][0])
    s5l = np.zeros((128, 5, 16, 128), np.float32)
    s5c = np.zeros((128, 2, 16, 128), np.float32)
    for g in range(32):
        q, r0, c0 = g // 2, (g % 8) * 16, (g % 2) * 64
        s5l[r0:r0 + 16, 0, q, c0:c0 + 64] = bR[g].T
        s5l[r0:r0 + 16, 1, q, c0:c0 + 64] = bI[g].T
        s5c[c0:c0 + 64, 0, q, r0:r0 + 16] = cRe[g].T
        s5c[c0:c0 + 64, 1, q, r0:r0 + 16] = cIm[g].T
    s5l[:, 2] = lrs.T[None, :, :]
    s5l[:, 3] = lis.T[None, :, :]
    s5l[:, 4] = lds.T[None, :, :]
    return dict(
        w_in=f(inp['w_in'][0]), w_out=f(inp['w_out'][0]), glu_w=f(inp['s5_glu_w'][0]), wq=f(inp['peer_wq'][0]),
        peer_u=f(inp['peer_u'][0]), peer_v=f(inp['peer_v'][0]), kT=f(kT), vecs=vecs, wax=wax, gbc=f(gbc),
        s5l=f(s5l.reshape(128, 5, 2048)), s5c=f(s5c.reshape(128, 2, 2048)))


_NC_CACHE = {}


def run(inputs, NT, ncores=NCORES, stage=2):
    x = np.asarray(inputs['x'], dtype=np.float32)
    shared = prep_shared(inputs)
    if (NT, stage) not in _NC_CACHE:
        _NC_CACHE[(NT, stage)] = build(NT, stage)
    nc = _NC_CACHE[(NT, stage)]
    in_maps = []
    for c in range(ncores):
        xc = x[4 * c:4 * c + 4].reshape(-1, 1024)[:NT * 128]
        m = dict(shared)
        m['x'] = np.ascontiguousarray(xc)
        in_maps.append(m)
    res = run_bass_kernel_spmd(nc, in_maps, core_ids=list(range(ncores)))
    return [r['out'] for r in res.results]


def kernel(**inputs):
    outs = run(inputs, 4 * TPS)
    out = np.stack([o.reshape(4, 2048, 1024) for o in outs], axis=0).reshape(32, 2048, 1024)
    return out.astype(np.float32)
```

```python
import math
from contextlib import ExitStack

import numpy as np
import concourse.bass as bass
import concourse.mybir as mybir
from concourse.bass_utils import run_bass_kernel_spmd

F32 = mybir.dt.float32
BF16 = mybir.dt.bfloat16
I32 = mybir.dt.int32
U32 = mybir.dt.uint32
ALU = mybir.AluOpType
AF = mybir.ActivationFunctionType
AX = mybir.AxisListType

ENG = ('tensor', 'vector', 'scalar', 'gpsimd', 'sync')
CAP = 30000
NCORES = 8
TPS = 16
EPS = 1e-6
TWO_PI = 6.283185
NV = 112
RHO_BCAST = True


class Op:
    __slots__ = ('eng', 'fn', 'deps', 'sig', 'is_dma', 'chan', 'chan_val', 'signo')


class Sched:
    def __init__(self, nc):
        self.nc = nc
        self.ops = {e: [] for e in ENG}
        self.lastw = {}
        self.readers = {}
        self.chan_cnt = {}
        self.last_op = {}
        self.last_dma = {}

    def add(self, eng, fn, reads=(), writes=(), chan=None, extra=()):
        op = Op()
        op.eng = eng
        op.fn = fn
        op.is_dma = chan is not None
        op.chan = chan
        op.sig = False
        op.signo = 0
        op.chan_val = 0
        deps = {}
        for r in reads:
            w = self.lastw.get(r)
            if w is not None:
                deps[id(w)] = w
        for k in writes:
            w = self.lastw.get(k)
            if w is not None:
                deps[id(w)] = w
            for d in self.readers.get(k, ()):
                deps[id(d)] = d
        for d in extra:
            deps[id(d)] = d
        op.deps = []
        for d in deps.values():
            if (not d.is_dma) and d.eng == 'tensor' and eng == 'tensor':
                continue
            op.deps.append(d)
            if not d.is_dma:
                d.sig = True
        if op.is_dma:
            c = self.chan_cnt.get(chan, 0) + 1
            self.chan_cnt[chan] = c
            op.chan_val = 16 * c
            self.last_dma[chan] = op
        else:
            self.last_op[eng] = op
        for r in reads:
            self.readers.setdefault(r, []).append(op)
        for k in writes:
            self.lastw[k] = op
            self.readers[k] = []
        self.ops[eng].append(op)
        return op

    def barrier(self):
        prev = list(self.last_op.values()) + list(self.last_dma.values())
        for e in ENG:
            self.add(e, lambda eo: eo.nop(), extra=prev)
        self.lastw = {}
        self.readers = {}

    def emit(self, es):
        nc = self.nc
        esem = {}
        for e in ENG:
            n = 0
            for op in self.ops[e]:
                if op.sig and not op.is_dma:
                    n += 1
                    op.signo = n
            nep = max(1, (n + CAP - 1) // CAP)
            esem[e] = [es.enter_context(nc.semaphore(f"s_{e}_{i}")) for i in range(nep)]
        csem = {c: es.enter_context(nc.semaphore(f"c_{c}")) for c in self.chan_cnt}
        block = es.enter_context(nc.Block())
        for e in ENG:
            ops = self.ops[e]

            def body(eo, e=e, ops=ops):
                waited = {}
                maxep = {}
                for op in ops:
                    need = {}
                    for d in op.deps:
                        if d.is_dma:
                            key = ('c', d.chan)
                            val = d.chan_val
                        else:
                            ep = (d.signo - 1) // CAP
                            if maxep.get(d.eng, -1) > ep:
                                continue
                            key = ('e', d.eng, ep)
                            val = (d.signo - 1) % CAP + 1
                        if need.get(key, 0) < val:
                            need[key] = val
                    for key, val in need.items():
                        if waited.get(key, 0) >= val:
                            continue
                        if key[0] == 'c':
                            eo.wait_ge(csem[key[1]], val)
                        else:
                            eo.wait_ge(esem[key[1]][key[2]], val)
                            if maxep.get(key[1], -1) < key[2]:
                                maxep[key[1]] = key[2]
                        waited[key] = val
                    ins = op.fn(eo)
                    if op.is_dma:
                        ins.then_inc(csem[op.chan], 16)
                    elif op.sig:
                        ins.then_inc(esem[e][(op.signo - 1) // CAP], 1)

            getattr(block, e)(body)


class Arena:
    def __init__(self, nc, es, nbytes):
        self.t = es.enter_context(nc.sbuf_tensor("arena", [128, nbytes // 4], F32))
        self.off = 0
        self.cap = nbytes

    def alloc(self, shape, dtype=F32):
        esz = {F32: 4, I32: 4, U32: 4, BF16: 2}[dtype]
        n = int(np.prod(shape))
        nb = (n * esz + 31) // 32 * 32
        o = self.off
        self.off += nb
        assert self.off <= self.cap, (self.off, self.cap)
        ap = self.t[:, o // 4:(o + nb) // 4]
        if dtype != F32:
            ap = ap.bitcast(dtype)
        ap = ap[:, 0:n]
        if len(shape) == 2:
            ap = ap.rearrange("p (a b) -> p a b", a=shape[0])
        elif len(shape) == 3:
            ap = ap.rearrange("p (a b c) -> p a b c", a=shape[0], b=shape[1])
        return ap


def bcast(ap2d_src, off, dims):
    base = ap2d_src
    pstep = base.ap[0][0]
    return bass.AP(base.tensor, base.offset + off, [[pstep, 128]] + [list(d) for d in dims])


def build(NT, stage=2, PIPE=True):
    nc = bass.Bass("TRN2", target_bir_lowering=False)
    es = ExitStack()

    def D(n, s, dt=F32, kind="ExternalInput"):
        return nc.dram_tensor(n, s, dt, kind=kind).ap()

    x_d = D("x", [NT * 128, 1024])
    out_d = D("out", [NT * 128, 1024], kind="ExternalOutput")
    w_in_d = D("w_in", [1024, 1536])
    w_out_d = D("w_out", [1024, 1024])
    glu_d = D("glu_w", [512, 512])
    wq_d = D("wq", [1024, 2048])
    pu_d = D("peer_u", [16384, 1024])
    pv_d = D("peer_v", [16384, 1024])
    kT_d = D("kT", [128, 2, 128])
    vecs_d = D("vecs", [128, NV])
    wax_d = D("wax", [128, 8, 128])
    gbc_d = D("gbc", [128, 3, 1024])
    s5l_d = D("s5l", [128, 5, 2048])
    s5c_d = D("s5c", [128, 2, 2048])
    ub_d = nc.dram_tensor("ub_scr", [16384, 1024], BF16).ap()
    wqs_d = nc.dram_tensor("wq_scr", [16, 128, 8, 128], BF16).ap()
    vb_d = nc.dram_tensor("vb_scr", [16384, 1024], BF16).ap()

    ar = Arena(nc, es, 207 * 1024)
    psum = es.enter_context(nc.psum_tensor("ps", [128, 8, 512], F32))
    sc = Sched(nc)

    cur = [None]

    def _emit(eng, fn, r, w, chan=None):
        if cur[0] is not None:
            cur[0].append(lambda: sc.add(eng, fn, r, w, chan=chan))
            return None
        return sc.add(eng, fn, r, w, chan=chan)

    def T(fn, r=(), w=()):
        return _emit('tensor', fn, r, w)

    def V(fn, r=(), w=()):
        return _emit('vector', fn, r, w)

    def A(fn, r=(), w=()):
        return _emit('scalar', fn, r, w)

    def G(fn, r=(), w=()):
        return _emit('gpsimd', fn, r, w)

    def GD(fn, r, w, chan):
        return _emit('gpsimd', fn, r, w, chan)

    clc = [0]

    def dma(eng, out, in_, r, w, chan):
        if chan == 'cl':
            chan = f'cl{clc[0]}'
            clc[0] += 1
        return _emit(eng, lambda e: e.dma_start(out=out, in_=in_), r, w, chan)

    def pk(b, q=None):
        return [f"ps{b}_{i}" for i in range(4)] if q is None else [f"ps{b}_{q}"]

    w_in = ar.alloc([8, 1536], BF16)
    w_out = ar.alloc([8, 1024], BF16)
    glu = ar.alloc([4, 512], BF16)
    bbR = ar.alloc([16, 128], BF16)
    bbI = ar.alloc([16, 128], BF16)
    cR = ar.alloc([16, 128], BF16)
    cIn = ar.alloc([16, 128], BF16)
    cosT = ar.alloc([16, 128])
    sinT = ar.alloc([16, 128])
    rhoT = None if RHO_BCAST else ar.alloc([16, 128])
    wax = ar.alloc([8, 128])
    kT = ar.alloc([2, 128])
    gbc = ar.alloc([3, 1024])
    vecs = ar.alloc([NV])
    cst = ar.alloc([128])
    ident = ar.alloc([128], BF16)
    identf = ar.alloc([128])
    ones = ar.alloc([8])
    hl = ar.alloc([4])
    ire = ar.alloc([16])
    iim = ar.alloc([16])
    xah = ar.alloc([4, 131])
    base = ar.off

    CW, CB, BA, BX, LAM, SD, GB, GNL, GNS, LRS, LIS, LDS, IOTA = 0, 16, 20, 24, 28, 32, 36, 40, 44, 48, 64, 80, 96
    SC8, SC16, MAGS, COSS, SINS, C128, S128 = 0, 4, 8, 24, 40, 56, 72

    wl_prev = [None]

    def wload(dst, src_):
        i = len(sc.ops['gpsimd'])
        op = sc.add('gpsimd', lambda e: e.dma_start(out=dst, in_=src_), [], [], chan=f'wl{i % 2}',
                    extra=[wl_prev[0]] if wl_prev[0] is not None else [])
        wl_prev[0] = op

    for kc in range(8):
        wload(w_in[:, kc, :], w_in_d[kc * 128:(kc + 1) * 128, :])
        wload(w_out[:, kc, :], w_out_d[kc * 128:(kc + 1) * 128, :])
    for kc in range(4):
        wload(glu[:, kc, :], glu_d[kc * 128:(kc + 1) * 128, :])
    for i in range(16):
        rs_ = slice(i * 1024, (i + 1) * 1024)
        sc.add('gpsimd', lambda e, rs_=rs_: e.dma_start(out=ub_d[rs_, :], in_=pu_d[rs_, :]), [], [], chan='sbu')
        sc.add('gpsimd', lambda e, rs_=rs_: e.dma_start(out=vb_d[rs_, :], in_=pv_d[rs_, :]), [], [], chan='sbv')
    for kc in range(8):
        sc.add('gpsimd', lambda e, kc=kc: e.dma_start(out=wqs_d[:, :, kc, :], in_=wq_d[kc * 128:(kc + 1) * 128, :].rearrange("p (j c) -> j p c", c=128)),
               [], [], chan='sbw')
    dma('sync', wax, wax_d, [], ['wax'], 'cl')
    dma('sync', kT, kT_d, [], ['kT'], 'cl')
    dma('sync', gbc, gbc_d, [], ['gbc'], 'cl')
    dma('sync', vecs, vecs_d, [], ['vecs'], 'cl')
    G(lambda e: e.memset(identf, 0.0), [], ['identf'])
    G(lambda e: e.affine_select(out=identf, in_=identf, pattern=[[-1, 128]], compare_op=ALU.not_equal,
                                fill=1.0, base=0, channel_multiplier=1), ['identf'], ['identf'])
    V(lambda e: e.tensor_copy(out=ident, in_=identf), ['identf'], ['ident'])
    G(lambda e: e.memset(ones, 1.0), [], ['ones'])
    G(lambda e: e.memset(cst, 0.0), [], ['cst'])

    A(lambda e: e.activation(out=cst[:, 100:104], in_=vecs[:, LAM:LAM + 4], func=AF.Exp, scale=-1.0), ['vecs', 'cst'], ['cst'])
    A(lambda e: e.activation(out=cst[:, 104:108], in_=cst[:, 100:104], func=AF.Ln, bias=1.0), ['cst'], ['cst'])
    V(lambda e: e.tensor_scalar(out=cst[:, SC8:SC8 + 4], in0=cst[:, 104:108], scalar1=-8.0, scalar2=None, op0=ALU.mult), ['cst'], ['cst'])
    V(lambda e: e.tensor_scalar(out=cst[:, SC16:SC16 + 4], in0=cst[:, 104:108], scalar1=-16.0, scalar2=None, op0=ALU.mult), ['cst'], ['cst'])

    def abar(lamR, lamI, logdt, mag, cosv, sinv, tmp, tmpi, kp):
        t0, t1, t2 = tmp
        k = [kp + str(i) for i in range(8)]
        A(lambda e: e.activation(out=t0, in_=logdt, func=AF.Exp), [k[0]], [k[1]])
        V(lambda e: e.tensor_tensor(out=t1, in0=lamI, in1=t0, op=ALU.mult), [k[0], k[1]], [k[2]])
        V(lambda e: e.tensor_tensor(out=t2, in0=lamR, in1=t0, op=ALU.mult), [k[0], k[1]], [k[3]])
        A(lambda e: e.activation(out=mag, in_=t2, func=AF.Exp), [k[3]], [k[4]])
        V(lambda e: e.tensor_scalar(out=t1, in0=t1, scalar1=1.0 / (2.0 * math.pi), scalar2=None, op0=ALU.mult), [k[2]], [k[2]])
        V(lambda e: e.tensor_copy(out=tmpi, in_=t1), [k[2]], [k[5]])
        V(lambda e: e.tensor_copy(out=t0, in_=tmpi), [k[5], k[1]], [k[1]])
        V(lambda e: e.tensor_tensor(out=t1, in0=t1, in1=t0, op=ALU.subtract), [k[2], k[1]], [k[2]])
        A(lambda e: e.activation(out=sinv, in_=t1, func=AF.Sin, scale=TWO_PI), [k[2]], [k[6]])
        V(lambda e: e.tensor_scalar(out=t0, in0=t1, scalar1=0.25, scalar2=None, op0=ALU.is_gt), [k[2], k[1]], [k[1]])
        V(lambda e: e.scalar_tensor_tensor(out=t2, in0=t1, scalar=0.25, in1=t0, op0=ALU.add, op1=ALU.subtract), [k[2], k[1], k[3], k[4]], [k[3]])
        A(lambda e: e.activation(out=cosv, in_=t2, func=AF.Sin, scale=TWO_PI), [k[3]], [k[7]])
        return k

    LT = [ar.alloc([1024]) for _ in range(15)]
    lti = ar.alloc([1024], I32)
    bbR2 = bbR.rearrange("p a b -> p (a b)")
    bbI2 = bbI.rearrange("p a b -> p (a b)")
    cR2 = cR.rearrange("p a b -> p (a b)")
    cI2 = cIn.rearrange("p a b -> p (a b)")
    for hf in range(2):
        cs = slice(hf * 1024, (hf + 1) * 1024)
        KL = ['L' + str(i) for i in range(8)]
        allL = KL + ['LT']
        for i in range(5):
            dma('sync', LT[i], s5l_d[:, i, cs], [], ['LT'], 'cl')
        dma('sync', LT[5], s5c_d[:, 0, cs], [], ['LT'], 'cl')
        dma('sync', LT[6], s5c_d[:, 1, cs], [], ['LT'], 'cl')
        bRp, bIp, lRl, lIl, ldl, cRp, cIp = LT[0:7]
        magL, cosL, sinL, tA, tB = LT[7:12]
        V(lambda e: e.tensor_copy(out=LT[12][:, 0:1], in_=LT[2][:, 0:1]), allL, allL)
        abar(lRl, lIl, ldl, magL, cosL, sinL, (LT[12], LT[13], LT[14]), lti, 'L')
        V(lambda e, cosL=cosL, magL=magL: e.tensor_tensor(out=cosL, in0=cosL, in1=magL, op=ALU.mult), allL, allL)
        V(lambda e, sinL=sinL, magL=magL: e.tensor_tensor(out=sinL, in0=sinL, in1=magL, op=ALU.mult), allL, allL)
        V(lambda e, cosL=cosL: e.tensor_scalar(out=cosL, in0=cosL, scalar1=-1.0, scalar2=None, op0=ALU.add), allL, allL)
        V(lambda e, magL=magL, lRl=lRl: e.tensor_tensor(out=magL, in0=lRl, in1=lRl, op=ALU.mult), allL, allL)
        V(lambda e, tA=tA, lIl=lIl: e.tensor_tensor(out=tA, in0=lIl, in1=lIl, op=ALU.mult), allL, allL)
        V(lambda e, magL=magL, tA=tA: e.tensor_tensor(out=magL, in0=magL, in1=tA, op=ALU.add), allL, allL)
        V(lambda e, magL=magL: e.reciprocal(out=magL, in_=magL), allL, allL)
        V(lambda e, tA=tA, cosL=cosL, lRl=lRl: e.tensor_tensor(out=tA, in0=cosL, in1=lRl, op=ALU.mult), allL, allL)
        V(lambda e, tB=tB, sinL=sinL, lIl=lIl: e.tensor_tensor(out=tB, in0=sinL, in1=lIl, op=ALU.mult), allL, allL)
        V(lambda e, tA=tA, tB=tB: e.tensor_tensor(out=tA, in0=tA, in1=tB, op=ALU.add), allL, allL)
        V(lambda e, tA=tA, magL=magL: e.tensor_tensor(out=tA, in0=tA, in1=magL, op=ALU.mult), allL, allL)
        V(lambda e, tB=tB, sinL=sinL, lRl=lRl: e.tensor_tensor(out=tB, in0=sinL, in1=lRl, op=ALU.mult), allL, allL)
        V(lambda e, cosL=cosL, lIl=lIl: e.tensor_tensor(out=cosL, in0=cosL, in1=lIl, op=ALU.mult), allL, allL)
        V(lambda e, tB=tB, cosL=cosL: e.tensor_tensor(out=tB, in0=tB, in1=cosL, op=ALU.subtract), allL, allL)
        V(lambda e, tB=tB, magL=magL: e.tensor_tensor(out=tB, in0=tB, in1=magL, op=ALU.mult), allL, allL)
        V(lambda e, cosL=cosL, tA=tA, bRp=bRp: e.tensor_tensor(out=cosL, in0=tA, in1=bRp, op=ALU.mult), allL, allL)
        V(lambda e, sinL=sinL, tB=tB, bIp=bIp: e.tensor_tensor(out=sinL, in0=tB, in1=bIp, op=ALU.mult), allL, allL)
        V(lambda e, cs=cs, cosL=cosL, sinL=sinL: e.tensor_tensor(out=bbR2[:, cs], in0=cosL, in1=sinL, op=ALU.subtract), allL, allL)
        V(lambda e, cosL=cosL, tA=tA, bIp=bIp: e.tensor_tensor(out=cosL, in0=tA, in1=bIp, op=ALU.mult), allL, allL)
        V(lambda e, sinL=sinL, tB=tB, bRp=bRp: e.tensor_tensor(out=sinL, in0=tB, in1=bRp, op=ALU.mult), allL, allL)
        V(lambda e, cs=cs, cosL=cosL, sinL=sinL: e.tensor_tensor(out=bbI2[:, cs], in0=cosL, in1=sinL, op=ALU.add), allL, allL)
        V(lambda e, cs=cs, cRp=cRp: e.tensor_copy(out=cR2[:, cs], in_=cRp), allL, allL)
        V(lambda e, cs=cs, cIp=cIp: e.tensor_scalar(out=cI2[:, cs], in0=cIp, scalar1=-1.0, scalar2=None, op0=ALU.mult), allL, allL)

    st = [ar.alloc([16]) for _ in range(3)]
    sti = ar.alloc([16], I32)
    V(lambda e: e.tensor_copy(out=st[0][:, 0:1], in_=vecs[:, 0:1]), ['vecs'], ['S0'])
    abar(vecs[:, LRS:LRS + 16], vecs[:, LIS:LIS + 16], vecs[:, LDS:LDS + 16],
         cst[:, MAGS:MAGS + 16], cst[:, COSS:COSS + 16], cst[:, SINS:SINS + 16], st, sti, 'S')
    KS = ['S' + str(i) for i in range(8)] + ['cst', 'tab']
    if not RHO_BCAST:
        V(lambda e: e.tensor_copy(out=rhoT, in_=bcast(cst, MAGS, [[1, 16], [0, 128]])), KS, KS)
    V(lambda e: e.memset(cosT[:, :, 0:1], 1.0), KS, KS)
    V(lambda e: e.memset(sinT[:, :, 0:1], 0.0), KS, KS)
    V(lambda e: e.tensor_copy(out=cosT[:, :, 1:2], in_=cst[:, COSS:COSS + 16].unsqueeze(2)), KS, KS)
    V(lambda e: e.tensor_copy(out=sinT[:, :, 1:2], in_=cst[:, SINS:SINS + 16].unsqueeze(2)), KS, KS)
    t4 = ar.alloc([16, 64])
    t5 = ar.alloc([16, 64])
    cps = cosT.ap[0][0]

    def cmul(oc, osn, ic, isn, mc, ms, a4, a5):
        V(lambda e: e.tensor_tensor(out=a4, in0=ic, in1=mc, op=ALU.mult), KS, KS)
        V(lambda e: e.tensor_tensor(out=a5, in0=isn, in1=ms, op=ALU.mult), KS, KS)
        V(lambda e: e.tensor_tensor(out=oc, in0=a4, in1=a5, op=ALU.subtract), KS, KS)
        V(lambda e: e.tensor_tensor(out=a4, in0=ic, in1=ms, op=ALU.mult), KS, KS)
        V(lambda e: e.tensor_tensor(out=a5, in0=isn, in1=mc, op=ALU.mult), KS, KS)
        V(lambda e: e.tensor_tensor(out=osn, in0=a4, in1=a5, op=ALU.add), KS, KS)

    for n in (2, 4, 8, 16, 32, 64, 128):
        if n < 128:
            oc, osn = cosT[:, :, n:n + 1], sinT[:, :, n:n + 1]
        else:
            oc, osn = cst[:, C128:C128 + 16].unsqueeze(2), cst[:, S128:S128 + 16].unsqueeze(2)
        cmul(oc, osn, cosT[:, :, n - 1:n], sinT[:, :, n - 1:n], cosT[:, :, 1:2], sinT[:, :, 1:2], t4[:, :, 0:1], t5[:, :, 0:1])
        if n < 128:
            m = n - 1
            mc = bass.AP(cosT.tensor, cosT.offset + n, [[cps, 128], [128, 16], [0, m]])
            ms = bass.AP(sinT.tensor, sinT.offset + n, [[cps, 128], [128, 16], [0, m]])
            cmul(cosT[:, :, n + 1:2 * n], sinT[:, :, n + 1:2 * n], cosT[:, :, 1:n], sinT[:, :, 1:n], mc, ms, t4[:, :, 0:m], t5[:, :, 0:m])

    sc.barrier()

    ar.off = base
    NXH = 3
    xh = [ar.alloc([1024]) for _ in range(NXH)]
    xn = ar.alloc([1024], BF16)
    jbf = ar.alloc([1024], BF16)
    xnT = ar.alloc([8, 128], BF16)
    pT = ar.alloc([8, 128])
    usb = ar.alloc([4, 128], BF16)
    lt = [ar.alloc([128]) for _ in range(10)]
    ya = ar.alloc([4, 128], BF16)
    yb = ar.alloc([4, 128], BF16)
    s5t = [ar.alloc([128]) for _ in range(6)]
    hreb = ar.alloc([128], BF16)
    himb = ar.alloc([128], BF16)
    gsf = ar.alloc([4, 128])
    gsb = ar.alloc([4, 128], BF16)
    sm = ar.alloc([16])
    obuf = ar.alloc([1024])
    xn2bs = [ar.alloc([1024], BF16) for _ in range(2)]
    xn2T = ar.alloc([8, 128], BF16)
    wqb = [ar.alloc([8, 128], BF16) for _ in range(3)]
    qTs = [ar.alloc([4, 128]) for _ in range(2)]
    ssb = [ar.alloc([4, 128]) for _ in range(2)]
    mr = ar.alloc([256])
    v_all = ar.alloc([16, 16])
    ix_all = ar.alloc([16, 16], U32)
    ixf = ar.alloc([16, 16])
    cand = [ar.alloc([256]) for _ in range(2)]
    sc_all = ar.alloc([8, 16])
    pos_all = ar.alloc([8, 16], U32)
    pa_i = ar.alloc([128], I32)
    pb_i = ar.alloc([128], I32)
    paf = ar.alloc([128])
    pbf = ar.alloc([128])
    oh = ar.alloc([16, 16])
    i1s = ar.alloc([128])
    i2s = ar.alloc([128])
    ef = ar.alloc([128])
    eis = [ar.alloc([128], I32) for _ in range(2)]
    ex = ar.alloc([8, 16])
    zz = ar.alloc([16])
    gws = [ar.alloc([128]) for _ in range(2)]
    dd = ar.alloc([128])
    coef = ar.alloc([128])
    ND = 4
    diag = [ar.alloc([128], BF16) for _ in range(ND)]
    NR = min(16, (ar.cap - ar.off) // 2048)
    assert NR >= 4, NR
    ring = [ar.alloc([1024], BF16) for _ in range(NR)]

    tpb = psum[:, 0, :].bitcast(BF16)

    def rstd_from(ss_ap, n, out_ap, kr, kw):
        A(lambda e: e.activation(out=out_ap, in_=ss_ap, func=AF.Sqrt, scale=1.0 / n, bias=EPS), kr, kw)
        V(lambda e: e.reciprocal(out=out_ap, in_=out_ap), kw, kw)

    def transpose8(src_bf, dst, ksrc, kdst):
        def f(e):
            ins = None
            for kc in range(8):
                ins = e.transpose(out=tpb[:, kc * 128:(kc + 1) * 128], in_=src_bf[:, kc * 128:(kc + 1) * 128], identity=ident)
            return ins
        T(f, [ksrc, 'ident'], pk(0))
        A(lambda e: e.copy(out=dst.rearrange("p a b -> p (a b)"), in_=tpb), pk(0), [kdst])

    rcount = [0]
    _bc = {}

    def bchk(e):
        if 'r' not in _bc:
            _bc['r'] = e.to_reg(16383)
        return _bc['r']

    RHO_BC = RHO_BCAST

    def front(t):
        first = (t % TPS == 0)
        xb = xh[t % NXH]
        kx = f'xh{t % NXH}'
        p2 = t % 2
        xn2 = xn2bs[p2]
        kxn2 = f'xn2_{p2}'
        ei = eis[p2]
        kei = f'ei{p2}'
        gw = gws[p2]
        kgw = f'gw{p2}'
        if t == 0:
            dma('sync', xb, x_d[0:128, :], [], [kx], 'xl0')
        if t + 1 < NT:
            n1 = (t + 1) % NXH
            dma('sync', xh[n1], x_d[(t + 1) * 128:(t + 2) * 128, :], [], [f'xh{n1}'], f'xl{n1}')
        A(lambda e: e.activation(out=jbf, in_=xb, func=AF.Square, accum_out=sm[:, 0:1]), [kx], ['jbf', 'sm0'])
        rstd_from(sm[:, 0:1], 1024.0, sm[:, 0:1], ['sm0'], ['sm0'])
        V(lambda e: e.scalar_tensor_tensor(out=xn, in0=xb, scalar=sm[:, 0:1], in1=gbc[:, 0, :], op0=ALU.mult, op1=ALU.mult),
          [kx, 'sm0', 'gbc'], ['xn'])
        transpose8(xn, xnT, 'xn', 'xnT')
        if first:
            V(lambda e: e.memset(xah[:, :, 0:3], 0.0), [], ['xah'])
        else:
            A(lambda e: e.copy(out=xah[:, :, 0:3], in_=xah[:, :, 128:131]), ['xah'], ['xah'])
        for gi in range(3):
            b = 1 + (gi % 2)

            def f(e, gi=gi, b=b):
                ins = None
                for j in range(4):
                    m = gi * 4 + j
                    for kc in range(8):
                        ins = e.matmul(psum[:, b, j * 128:(j + 1) * 128], lhsT=w_in[:, kc, m * 128:(m + 1) * 128],
                                       rhs=xnT[:, kc, :], start=(kc == 0), stop=(kc == 7))
                return ins
            T(f, ['w_in', 'xnT'], pk(b))
            if gi == 0:
                A(lambda e, b=b: e.copy(out=xah[:, :, 3:131], in_=psum[:, b, :].rearrange("p (a b) -> p a b", a=4)), pk(b), ['xah'])
            elif gi == 1:
                A(lambda e, b=b: e.copy(out=pT[:, 0:4, :], in_=psum[:, b, :].rearrange("p (a b) -> p a b", a=4)), pk(b), ['gate'])
            else:
                A(lambda e, b=b: e.copy(out=pT[:, 4:8, :], in_=psum[:, b, :].rearrange("p (a b) -> p a b", a=4)), pk(b), ['usf'])
                A(lambda e: e.copy(out=usb, in_=pT[:, 4:8, :]), ['usf'], ['usb'])
        xc, rr, ig, aa, a2, uu, hh_, gg, yar, sq = lt
        for c in range(4):
            cw = lambda k, c=c: vecs[:, CW + c * 4 + k:CW + c * 4 + k + 1]
            V(lambda e, c=c, cw=cw: e.tensor_scalar(out=xc, in0=xah[:, c, 0:128], scalar1=cw(0), scalar2=vecs[:, CB + c:CB + c + 1],
                                                    op0=ALU.mult, op1=ALU.add), ['xah', 'vecs'], ['xc'])
            for k in range(1, 4):
                V(lambda e, c=c, k=k, cw=cw: e.scalar_tensor_tensor(out=xc, in0=xah[:, c, k:k + 128], scalar=cw(k), in1=xc,
                                                                    op0=ALU.mult, op1=ALU.add), ['xah', 'vecs', 'xc'], ['xc'])

            def f(e, c=c):
                e.matmul(psum[:, 2, 0:128], lhsT=wax[:, c, :], rhs=xc, start=True, stop=True)
                return e.matmul(psum[:, 2, 128:256], lhsT=wax[:, 4 + c, :], rhs=xc, start=True, stop=True)
            T(f, ['wax', 'xc'], pk(2, 0) + pk(2, 1))
            A(lambda e, c=c: e.activation(out=rr, in_=psum[:, 2, 0:128], func=AF.Sigmoid, bias=vecs[:, BA + c:BA + c + 1]), pk(2, 0) + ['vecs'], ['rr'])
            A(lambda e, c=c: e.activation(out=ig, in_=psum[:, 2, 128:256], func=AF.Sigmoid, bias=vecs[:, BX + c:BX + c + 1]), pk(2, 1) + ['vecs'], ['ig'])
            A(lambda e, c=c: e.activation(out=aa, in_=rr, func=AF.Exp, scale=cst[:, SC8 + c:SC8 + c + 1]), ['rr'], ['aa'])
            A(lambda e, c=c: e.activation(out=a2, in_=rr, func=AF.Exp, scale=cst[:, SC16 + c:SC16 + c + 1]), ['rr'], ['a2'])
            A(lambda e: e.activation(out=a2, in_=a2, func=AF.Sqrt, scale=-1.0, bias=1.0), ['a2'], ['a2'])
            V(lambda e: e.tensor_tensor(out=uu, in0=ig, in1=xc, op=ALU.mult), ['ig', 'xc'], ['uu'])
            V(lambda e: e.tensor_tensor(out=uu, in0=uu, in1=a2, op=ALU.mult), ['uu', 'a2'], ['uu'])
            init = 0.0 if first else hl[:, c:c + 1]
            V(lambda e, init=init: e.tensor_tensor_scan(out=hh_, data0=aa, data1=uu, initial=init, op0=ALU.mult, op1=ALU.add),
              ['aa', 'uu', f'hl{c}'], ['hh'])
            A(lambda e, c=c: e.copy(out=hl[:, c:c + 1], in_=hh_[:, 127:128]), ['hh'], [f'hl{c}'])
            A(lambda e, c=c: e.activation(out=gg, in_=pT[:, c, :], func=AF.Gelu_apprx_tanh), ['gate'], ['gg'])
            V(lambda e: e.tensor_tensor(out=yar, in0=hh_, in1=gg, op=ALU.mult), ['hh', 'gg'], ['yar'])
            A(lambda e: e.activation(out=sq, in_=yar, func=AF.Square), ['yar'], ['sq'])
            T(lambda e, c=c: e.matmul(psum[:, 3, 256:257], lhsT=sq, rhs=ones[:, 0:1], start=(c == 0), stop=(c == 3)), ['sq', 'ones'], pk(3, 2))
            A(lambda e, c=c: e.activation(out=ya[:, c, :], in_=yar, func=AF.Copy, scale=vecs[:, GNL + c:GNL + c + 1]), ['yar', 'vecs'], ['ya'])
        t1, t2, wre, wim, gre, gim = s5t
        for cc in range(4):
            for pq in range(4):
                q = cc * 4 + pq
                bq = 4 + (q % 2)

                def f(e, q=q, cc=cc, bq=bq):
                    e.matmul(psum[:, bq, 0:128], lhsT=bbR[:, q, :], rhs=usb[:, cc, :], start=True, stop=True)
                    return e.matmul(psum[:, bq, 128:256], lhsT=bbI[:, q, :], rhs=usb[:, cc, :], start=True, stop=True)
                T(f, ['usb'], pk(bq, 0) + pk(bq, 1))
                buR = psum[:, bq, 0:128]
                buI = psum[:, bq, 128:256]
                kb = pk(bq, 0) + pk(bq, 1)
                cq = cosT[:, q, :]
                sq_ = sinT[:, q, :]
                rho = bcast(cst, MAGS + q, [[0, 128]]) if RHO_BC else rhoT[:, q, :]
                V(lambda e, cq=cq, buR=buR: e.tensor_tensor(out=t1, in0=buR, in1=cq, op=ALU.mult), kb, ['t1'])
                V(lambda e, sq_=sq_, buI=buI: e.tensor_tensor(out=t2, in0=buI, in1=sq_, op=ALU.mult), kb, ['t2'])
                V(lambda e: e.tensor_tensor(out=wre, in0=t1, in1=t2, op=ALU.add), ['t1', 't2'], ['wre'])
                V(lambda e, cq=cq, buI=buI: e.tensor_tensor(out=t1, in0=buI, in1=cq, op=ALU.mult), kb + ['t1'], ['t1'])
                V(lambda e, sq_=sq_, buR=buR: e.tensor_tensor(out=t2, in0=buR, in1=sq_, op=ALU.mult), kb + ['t2'], ['t2'])
                V(lambda e: e.tensor_tensor(out=wim, in0=t1, in1=t2, op=ALU.subtract), ['t1', 't2'], ['wim'])
                i_r = 0.0 if first else ire[:, q:q + 1]
                i_i = 0.0 if first else iim[:, q:q + 1]
                V(lambda e, rho=rho, i_r=i_r: e.tensor_tensor_scan(out=gre, data0=rho, data1=wre, initial=i_r, op0=ALU.mult, op1=ALU.add),
                  ['wre', f'ic{q}'], ['gre'])
                V(lambda e, rho=rho, i_i=i_i: e.tensor_tensor_scan(out=gim, data0=rho, data1=wim, initial=i_i, op0=ALU.mult, op1=ALU.add),
                  ['wim', f'ic{q}'], ['gim'])
                c128 = cst[:, C128 + q:C128 + q + 1]
                s128 = cst[:, S128 + q:S128 + q + 1]
                V(lambda e, s128=s128: e.tensor_scalar(out=sm[:, 8:9], in0=gim[:, 127:128], scalar1=s128, scalar2=None, op0=ALU.mult), ['gim'], ['sm8'])
                V(lambda e, q=q, c128=c128: e.scalar_tensor_tensor(out=ire[:, q:q + 1], in0=gre[:, 127:128], scalar=c128, in1=sm[:, 8:9],
                                                                   op0=ALU.mult, op1=ALU.subtract), ['gre', 'sm8', f'ic{q}'], [f'ic{q}'])
                V(lambda e, s128=s128: e.tensor_scalar(out=sm[:, 9:10], in0=gre[:, 127:128], scalar1=s128, scalar2=None, op0=ALU.mult), ['gre'], ['sm9'])
                V(lambda e, q=q, c128=c128: e.scalar_tensor_tensor(out=iim[:, q:q + 1], in0=gim[:, 127:128], scalar=c128, in1=sm[:, 9:10],
                                                                   op0=ALU.mult, op1=ALU.add), ['gim', 'sm9', f'ic{q}'], [f'ic{q}'])
                V(lambda e, cq=cq: e.tensor_tensor(out=t1, in0=gre, in1=cq, op=ALU.mult), ['gre', 't1'], ['t1'])
                V(lambda e, sq_=sq_: e.tensor_tensor(out=t2, in0=gim, in1=sq_, op=ALU.mult), ['gim', 't2'], ['t2'])
                V(lambda e: e.tensor_tensor(out=hreb, in0=t1, in1=t2, op=ALU.subtract), ['t1', 't2'], ['hreb'])
                V(lambda e, cq=cq: e.tensor_tensor(out=t1, in0=gim, in1=cq, op=ALU.mult), ['gim', 't1'], ['t1'])
                V(lambda e, sq_=sq_: e.tensor_tensor(out=t2, in0=gre, in1=sq_, op=ALU.mult), ['gre', 't2'], ['t2'])
                V(lambda e: e.tensor_tensor(out=himb, in0=t1, in1=t2, op=ALU.add), ['t1', 't2'], ['himb'])

                def f2(e, q=q, pq=pq):
                    e.matmul(psum[:, 1, 0:128], lhsT=cR[:, q, :], rhs=hreb, start=(pq == 0), stop=False)
                    return e.matmul(psum[:, 1, 0:128], lhsT=cIn[:, q, :], rhs=himb, start=False, stop=(pq == 3))
                T(f2, ['hreb', 'himb'], pk(1, 0))
            V(lambda e, cc=cc: e.scalar_tensor_tensor(out=t1, in0=pT[:, 4 + cc, :], scalar=vecs[:, SD + cc:SD + cc + 1], in1=psum[:, 1, 0:128],
                                                      op0=ALU.mult, op1=ALU.add), ['usf', 'vecs', 't1'] + pk(1, 0), ['t1'])
            A(lambda e, cc=cc: e.activation(out=gsf[:, cc, :], in_=t1, func=AF.Gelu_apprx_tanh), ['t1'], ['gsf'])
            A(lambda e, cc=cc: e.copy(out=gsb[:, cc, :], in_=gsf[:, cc, :]), ['gsf'], ['gsb'])
        for m in range(4):
            def f(e, m=m):
                ins = None
                for k in range(4):
                    ins = e.matmul(psum[:, 2, 0:128], lhsT=glu[:, k, m * 128:(m + 1) * 128], rhs=gsb[:, k, :], start=(k == 0), stop=(k == 3))
                return ins
            T(f, ['gsb'], pk(2, 0))
            A(lambda e, m=m: e.activation(out=rr, in_=psum[:, 2, 0:128], func=AF.Sigmoid, bias=vecs[:, GB + m:GB + m + 1]), pk(2, 0) + ['vecs'], ['rr'])
            V(lambda e, m=m: e.tensor_tensor(out=yar, in0=gsf[:, m, :], in1=rr, op=ALU.mult), ['gsf', 'rr'], ['yar'])
            A(lambda e: e.activation(out=sq, in_=yar, func=AF.Square), ['yar'], ['sq'])
            T(lambda e, m=m: e.matmul(psum[:, 3, 257:258], lhsT=sq, rhs=ones[:, 0:1], start=(m == 0), stop=(m == 3)), ['sq', 'ones'], pk(3, 2))
            A(lambda e, m=m: e.activation(out=yb[:, m, :], in_=yar, func=AF.Copy, scale=vecs[:, GNS + m:GNS + m + 1]), ['yar', 'vecs'], ['yb'])
        rstd_from(psum[:, 3, 256:258], 512.0, sm[:, 1:3], pk(3, 2), ['sm12'])
        for half in range(2):
            def f(e, half=half):
                ins = None
                for c in range(4):
                    ins = e.matmul(psum[:, 4 + half, :], lhsT=ya[:, c, :], rhs=w_out[:, c, half * 512:(half + 1) * 512], start=(c == 0), stop=(c == 3))
                return ins
            T(f, ['ya', 'w_out'], pk(4 + half))

            def f(e, half=half):
                ins = None
                for c in range(4):
                    ins = e.matmul(psum[:, 1 + half, :], lhsT=yb[:, c, :], rhs=w_out[:, 4 + c, half * 512:(half + 1) * 512], start=(c == 0), stop=(c == 3))
                return ins
            T(f, ['yb', 'w_out'], pk(1 + half))
            xs = xb[:, half * 512:(half + 1) * 512]
            V(lambda e, half=half, xs=xs: e.scalar_tensor_tensor(out=xs, in0=psum[:, 4 + half, :], scalar=sm[:, 1:2], in1=xs, op0=ALU.mult, op1=ALU.add),
              pk(4 + half) + ['sm12', kx], [kx])
            V(lambda e, half=half, xs=xs: e.scalar_tensor_tensor(out=xs, in0=psum[:, 1 + half, :], scalar=sm[:, 2:3], in1=xs, op0=ALU.mult, op1=ALU.add),
              pk(1 + half) + ['sm12', kx], [kx])
        A(lambda e: e.activation(out=jbf, in_=xb, func=AF.Square, accum_out=sm[:, 3:4]), [kx], ['jbf', 'sm3'])
        rstd_from(sm[:, 3:4], 1024.0, sm[:, 3:4], ['sm3'], ['sm3'])
        V(lambda e: e.scalar_tensor_tensor(out=xn2, in0=xb, scalar=sm[:, 3:4], in1=gbc[:, 1, :], op0=ALU.mult, op1=ALU.mult),
          [kx, 'sm3', 'gbc'], [kxn2])
        transpose8(xn2, xn2T, kxn2, 'xn2T')
        for grp in range(4):
            bqT = 4 + (grp % 2)
            bsc = 1 + (grp % 2)
            for jj in range(4):
                j = grp * 4 + jj
                wslot = j % 3
                wb = wqb[wslot]
                dma('sync', wb, wqs_d[j], [], [f'wq{wslot}'], f'wq{wslot}')

                def f(e, wb=wb, jj=jj, bqT=bqT):
                    ins = None
                    for kc in range(8):
                        ins = e.matmul(psum[:, bqT, jj * 128:(jj + 1) * 128], lhsT=wb[:, kc, :], rhs=xn2T[:, kc, :], start=(kc == 0), stop=(kc == 7))
                    return ins
                T(f, [f'wq{wslot}', 'xn2T'], pk(bqT, jj))
            qs = qTs[grp % 2]
            kq = f'qTs{grp % 2}'
            A(lambda e, qs=qs, bqT=bqT: e.copy(out=qs.rearrange("p a b -> p (a b)"), in_=psum[:, bqT, :]), pk(bqT), [kq])

            def f(e, qs=qs, bsc=bsc):
                ins = None
                for jj in range(4):
                    ins = e.matmul(psum[:, bsc, jj * 128:(jj + 1) * 128], lhsT=qs[:, jj, :], rhs=kT[:, jj % 2, :], start=True, stop=True)
                return ins
            T(f, [kq, 'kT'], pk(bsc))
            sb = ssb[grp % 2]
            ks = f'ssb{grp % 2}'
            A(lambda e, sb=sb, bsc=bsc: e.copy(out=sb.rearrange("p a b -> p (a b)"), in_=psum[:, bsc, :]), pk(bsc), [ks])
            for jj in range(4):
                j = grp * 4 + jj
                s_j = sb[:, jj, :]
                V(lambda e, j=j, s_j=s_j: e.max(out=v_all[:, j, 0:8], in_=s_j), [ks], ['v_all'])
                V(lambda e, j=j, s_j=s_j: e.match_replace(out=mr[:, 0:128], in_to_replace=v_all[:, j, 0:8], in_values=s_j, imm_value=-1e30), [ks, 'v_all'], ['mr'])
                V(lambda e, j=j: e.max(out=v_all[:, j, 8:16], in_=mr[:, 0:128]), ['mr'], ['v_all'])
                V(lambda e, j=j, s_j=s_j: e.max_index(out=ix_all[:, j, 0:8], in_max=v_all[:, j, 0:8], in_values=s_j), [ks, 'v_all'], ['ix_all'])
                V(lambda e, j=j, s_j=s_j: e.max_index(out=ix_all[:, j, 8:16], in_max=v_all[:, j, 8:16], in_values=s_j), [ks, 'v_all'], ['ix_all'])
        V(lambda e: e.tensor_copy(out=ixf, in_=ix_all), ['ix_all'], ['ixf'])
        vps = v_all.ap[0][0]
        for h in range(8):
            cd = cand[h % 2]
            kc_ = f'cand{h % 2}'
            in0 = bass.AP(v_all.tensor, v_all.offset + (2 * h) * 16, [[vps, 128], [1, 16], [0, 16]])
            in1 = bass.AP(v_all.tensor, v_all.offset + (2 * h + 1) * 16, [[vps, 128], [0, 16], [1, 16]])
            V(lambda e, cd=cd, in0=in0, in1=in1: e.tensor_tensor(out=cd.rearrange("p (a b) -> p a b", a=16), in0=in0, in1=in1, op=ALU.add), ['v_all'], [kc_])
            V(lambda e, cd=cd, h=h: e.max(out=sc_all[:, h, 0:8], in_=cd), [kc_], ['sc_all'])
            V(lambda e, cd=cd, h=h: e.match_replace(out=mr, in_to_replace=sc_all[:, h, 0:8], in_values=cd, imm_value=-1e30), [kc_, 'sc_all'], ['mr'])
            V(lambda e, h=h: e.max(out=sc_all[:, h, 8:16], in_=mr), ['mr'], ['sc_all'])
            V(lambda e, cd=cd, h=h: e.max_index(out=pos_all[:, h, 0:8], in_max=sc_all[:, h, 0:8], in_values=cd), [kc_, 'sc_all'], ['pos_all'])
            V(lambda e, cd=cd, h=h: e.max_index(out=pos_all[:, h, 8:16], in_max=sc_all[:, h, 8:16], in_values=cd), [kc_, 'sc_all'], ['pos_all'])
        pos2 = pos_all.rearrange("p a b -> p (a b)").bitcast(I32)
        V(lambda e: e.tensor_single_scalar(out=pa_i, in_=pos2, scalar=4, op=ALU.arith_shift_right), ['pos_all'], ['pa_i'])
        V(lambda e: e.tensor_single_scalar(out=pb_i, in_=pos2, scalar=15, op=ALU.bitwise_and), ['pos_all'], ['pb_i'])
        V(lambda e: e.tensor_copy(out=paf, in_=pa_i), ['pa_i'], ['paf'])
        V(lambda e: e.tensor_copy(out=pbf, in_=pb_i), ['pb_i'], ['pbf'])
        iot = bcast(vecs, IOTA, [[0, 16], [1, 16]])
        ixf2 = ixf.rearrange("p a b -> p (a b)")
        for h in range(8):
            for (pf, kpf, off, dst, kd) in ((paf, 'paf', 0, i1s, 'i1s'), (pbf, 'pbf', 16, i2s, 'i2s')):
                pfb = bcast(pf, h * 16, [[1, 16], [0, 16]])
                ixb = bcast(ixf2, h * 32 + off, [[0, 16], [1, 16]])
                V(lambda e, pfb=pfb: e.tensor_tensor(out=oh, in0=pfb, in1=iot, op=ALU.is_equal), [kpf, 'vecs', 'oh'], ['oh'])
                V(lambda e, ixb=ixb: e.tensor_tensor(out=oh, in0=oh, in1=ixb, op=ALU.mult), ['oh', 'ixf'], ['oh'])
                V(lambda e, dst=dst, h=h: e.tensor_reduce(out=dst[:, h * 16:(h + 1) * 16], in_=oh, axis=AX.X, op=ALU.add), ['oh', kd], [kd])
        V(lambda e: e.scalar_tensor_tensor(out=ef, in0=i1s, scalar=128.0, in1=i2s, op0=ALU.mult, op1=ALU.add), ['i1s', 'i2s'], ['ef'])
        V(lambda e: e.tensor_copy(out=ei, in_=ef), ['ef'], [kei])
        sps = sc_all.ap[0][0]
        mxb = bass.AP(sc_all.tensor, sc_all.offset, [[sps, 128], [16, 8], [0, 16]])
        V(lambda e: e.tensor_tensor(out=ex, in0=sc_all, in1=mxb, op=ALU.subtract), ['sc_all'], ['ex'])
        A(lambda e: e.activation(out=ex, in_=ex, func=AF.Exp), ['ex'], ['ex'])
        V(lambda e: e.tensor_reduce(out=zz[:, 0:8], in_=ex, axis=AX.X, op=ALU.add), ['ex'], ['zz'])
        V(lambda e: e.reciprocal(out=zz[:, 0:8], in_=zz[:, 0:8]), ['zz'], ['zz'])
        zb = bcast(zz, 0, [[1, 8], [0, 16]])
        V(lambda e: e.tensor_tensor(out=gw.rearrange("p (a b) -> p a b", a=8), in0=ex, in1=zb, op=ALU.mult), ['ex', 'zz'], [kgw])

    def back(t):
        xb = xh[t % NXH]
        kx = f'xh{t % NXH}'
        p2 = t % 2
        xn2 = xn2bs[p2]
        kxn2 = f'xn2_{p2}'
        ei = eis[p2]
        kei = f'ei{p2}'
        gw = gws[p2]
        kgw = f'gw{p2}'
        ddk = [f'dd{n}' for n in range(128)]
        V(lambda e: e.memset(dd, 0.0), [], ddk)
        for n in range(128):
            rs = rcount[0] % NR
            rcount[0] += 1
            rb = ring[rs]
            GD(lambda e, rb=rb, n=n: e.indirect_dma_start(out=rb, out_offset=None, in_=ub_d,
                                                          in_offset=bass.IndirectOffsetOnAxis(ap=ei[:, n:n + 1], axis=0), bounds_check=bchk(e), oob_is_err=False),
               [kei], [f'ring{rs}'], f'rg{rs}')
            V(lambda e, rb=rb, n=n: e.scalar_tensor_tensor(out=rb, in0=rb, scalar=1.0, in1=xn2, op0=ALU.mult, op1=ALU.mult, accum_out=dd[:, n:n + 1]),
              [f'ring{rs}', kxn2, f'dd{n}'], [f'dd{n}', f'ring{rs}'])
        A(lambda e: e.activation(out=coef, in_=dd, func=AF.Gelu_apprx_tanh), ddk, ['coef'])
        V(lambda e: e.tensor_tensor(out=coef, in0=coef, in1=gw, op=ALU.mult), ['coef', kgw], ['coef'])
        for n in range(128):
            rs = rcount[0] % NR
            rcount[0] += 1
            rb = ring[rs]
            GD(lambda e, rb=rb, n=n: e.indirect_dma_start(out=rb, out_offset=None, in_=vb_d,
                                                          in_offset=bass.IndirectOffsetOnAxis(ap=ei[:, n:n + 1], axis=0), bounds_check=bchk(e), oob_is_err=False),
               [kei], [f'ring{rs}'], f'rg{rs}')
            dg = diag[n % ND]
            kdg = f'diag{n % ND}'
            A(lambda e, dg=dg, n=n: e.activation(out=dg, in_=identf, func=AF.Copy, scale=coef[:, n:n + 1]), ['coef'], [kdg])

            def f(e, dg=dg, rb=rb, n=n):
                e.matmul(psum[:, 6, :], lhsT=dg, rhs=rb[:, 0:512], start=(n == 0), stop=(n == 127))
                return e.matmul(psum[:, 7, :], lhsT=dg, rhs=rb[:, 512:1024], start=(n == 0), stop=(n == 127))
            T(f, [kdg, f'ring{rs}'], pk(6) + pk(7))
        V(lambda e: e.tensor_tensor(out=xb[:, 0:512], in0=psum[:, 6, :], in1=xb[:, 0:512], op=ALU.add), pk(6) + [kx], [kx])
        V(lambda e: e.tensor_tensor(out=xb[:, 512:1024], in0=psum[:, 7, :], in1=xb[:, 512:1024], op=ALU.add), pk(7) + [kx], [kx])
        A(lambda e: e.activation(out=jbf, in_=xb, func=AF.Square, accum_out=sm[:, 4:5]), [kx], ['jbf', 'sm4'])
        rstd_from(sm[:, 4:5], 1024.0, sm[:, 4:5], ['sm4'], ['sm4'])
        V(lambda e: e.scalar_tensor_tensor(out=obuf, in0=xb, scalar=sm[:, 4:5], in1=gbc[:, 2, :], op0=ALU.mult, op1=ALU.mult),
          [kx, 'sm4', 'gbc'], ['obuf'])
        dma('sync', out_d[t * 128:(t + 1) * 128, :], obuf, ['obuf'], [], 'ost')

    def record(fn, t):
        lst = []
        cur[0] = lst
        fn(t)
        cur[0] = None
        return lst

    def run_merged(F, B):
        nf, nb = len(F), len(B)
        i = j = 0
        while i < nf or j < nb:
            if j >= nb or (i < nf and i * nb <= j * nf):
                F[i]()
                i += 1
            else:
                B[j]()
                j += 1

    if PIPE:
        run_merged(record(front, 0), [])
        for t in range(1, NT):
            Fl = record(front, t)
            Bl = record(back, t - 1)
            run_merged(Fl, Bl)
        run_merged([], record(back, NT - 1))
    else:
        for t in range(NT):
            run_merged(record(front, t), [])
            run_merged(record(back, t), [])
    sc.add('sync', lambda e: e.nop(), [], [], extra=list(sc.last_dma.values()))
    sc.emit(es)
    es.close()
    return nc


def prep_shared(inp):
    f = lambda a: np.ascontiguousarray(np.asarray(a, dtype=np.float32))
    col = lambda v, n: f(v).reshape(n, 128).T
    vecs = np.zeros((128, NV), np.float32)
    cw = f(inp['conv_w'][0])
    for c in range(4):
        for k in range(4):
            vecs[:, c * 4 + k] = cw[k, c * 128:(c + 1) * 128]
    vecs[:, 16:20] = col(inp['conv_b'][0], 4)
    vecs[:, 20:24] = col(inp['lru_ba'][0], 4)
    vecs[:, 24:28] = col(inp['lru_bx'][0], 4)
    vecs[:, 28:32] = col(inp['lru_lambda'][0], 4)
    vecs[:, 32:36] = col(inp['s5_d'][0], 4)
    vecs[:, 36:40] = col(inp['s5_glu_b'][0], 4)
    vecs[:, 40:44] = col(inp['gnorm_lru'][0], 4)
    vecs[:, 44:48] = col(inp['gnorm_s5'][0], 4)
    lr = f(inp['s5_lambda_re'][0])
    li = f(inp['s5_lambda_im'][0])
    ld = f(inp['s5_log_dt'][0])
    lrs = lr.reshape(16, 128).T
    lis = li.reshape(16, 128).T
    lds = np.repeat(ld.reshape(16, 2, 1), 64, axis=2).reshape(16, 128).T
    vecs[:, 48:64] = lrs
    vecs[:, 64:80] = lis
    vecs[:, 80:96] = lds
    vecs[:, 96:112] = np.arange(16, dtype=np.float32)[None, :]
    wax = np.zeros((128, 8, 128), np.float32)
    wa = f(inp['lru_wa'][0])
    wx = f(inp['lru_wx'][0])
    for h in range(8):
        c, o = h // 2, (h % 2) * 64
        wax[o:o + 64, c, o:o + 64] = wa[h]
        wax[o:o + 64, 4 + c, o:o + 64] = wx[h]
    kT = np.stack([f(inp['peer_k1'][0]).T, f(inp['peer_k2'][0]).T], axis=1)
    gbc = np.stack([np.broadcast_to(f(inp['norm1_g'][0]), (128, 1024)),
                    np.broadcast_to(f(inp['norm2_g'][0]), (128, 1024)),
                    np.broadcast_to(f(inp['normf_g']), (128, 1024))], axis=1)
    bR = f(inp['s5_b_re'][0])
    bI = f(inp['s5_b_im'][0])
    cRe = f(inp['s5_c_re'][0])
    cIm = f(inp['s5_c_im'][0])
    s5l = np.zeros((128, 5, 16, 128), np.float32)
    s5c = np.zeros((128, 2, 16, 128), np.float32)
    for g in range(32):
        q, r0, c0 = g // 2, (g % 8) * 16, (g % 2) * 64
        s5l[r0:r0 + 16, 0, q, c0:c0 + 64] = bR[g].T
        s5l[r0:r0 + 16, 1, q, c0:c0 + 64] = bI[g].T
        s5c[c0:c0 + 64, 0, q, r0:r0 + 16] = cRe[g].T
        s5c[c0:c0 + 64, 1, q, r0:r0 + 16] = cIm[g].T
    s5l[:, 2] = lrs.T[None, :, :]
    s5l[:, 3] = lis.T[None, :, :]
    s5l[:, 4] = lds.T[None, :, :]
    return dict(
        w_in=f(inp['w_in'][0]), w_out=f(inp['w_out'][0]), glu_w=f(inp['s5_glu_w'][0]), wq=f(inp['peer_wq'][0]),
        peer_u=f(inp['peer_u'][0]), peer_v=f(inp['peer_v'][0]), kT=f(kT), vecs=vecs, wax=wax, gbc=f(gbc),
        s5l=f(s5l.reshape(128, 5, 2048)), s5c=f(s5c.reshape(128, 2, 2048)))


_NC_CACHE = {}


def run(inputs, NT, ncores=NCORES, stage=2):
    x = np.asarray(inputs['x'], dtype=np.float32)
    shared = prep_shared(inputs)
    if (NT, stage) not in _NC_CACHE:
        _NC_CACHE[(NT, stage)] = build(NT, stage)
    nc = _NC_CACHE[(NT, stage)]
    in_maps = []
    for c in range(ncores):
        xc = x[4 * c:4 * c + 4].reshape(-1, 1024)[:NT * 128]
        m = dict(shared)
        m['x'] = np.ascontiguousarray(xc)
        in_maps.append(m)
    res = run_bass_kernel_spmd(nc, in_maps, core_ids=list(range(ncores)))
    return [r['out'] for r in res.results]


def kernel(**inputs):
    outs = run(inputs, 4 * TPS)
    out = np.stack([o.reshape(4, 2048, 1024) for o in outs], axis=0).reshape(32, 2048, 1024)
    return out.astype(np.float32)
```
